# Optimizing a Trainium2 kernel written in Bass

```python
import jax, jax.numpy as jnp
from jax import lax
import numpy as np

D_MODEL = 2048
BATCH = 16
SEQ = 256
DEPTH = 2
DEC_BATCH = 4
DEC_SEQ = 1024
PAST_LEN = 512

f32 = jnp.float32
GRID_W = 64
MIX_W = D_MODEL
MLA_W = MIX_W // 2
LRU_W = MIX_W // 4
CONV_W = MIX_W // 4
V_HEAD = 128
MLA_HEADS = MLA_W // V_HEAD
QK_NOPE = 128
QK_ROPE = 64
QK_HEAD = QK_NOPE + QK_ROPE
Q_LORA = D_MODEL // 4
KV_LORA = D_MODEL // 8
ROPE_THETA = 10000.0
Q_BLOCK = 128
LRU_HEADS = 8
LRU_HD = LRU_W // LRU_HEADS
LRU_CONV = 4
RG_C = 8.0
CM_K = 31
PEER_HEADS = 8
N_KEYS = 128
N_EXPERTS = N_KEYS * N_KEYS
PEER_QDIM = 256
PEER_TOPK = 16
TOK_BLOCK = 128
IN_SPLITS = (Q_LORA, KV_LORA, QK_ROPE, LRU_W, LRU_W, 2 * CONV_W)
IN_W = Q_LORA + KV_LORA + QK_ROPE + 2 * LRU_W + 2 * CONV_W
EPS = 1e-6

kernel_name = 'hymba_mla_rglru_conformer_peer_dit_step'


def rms_norm(x, g):
    xf = x.astype(f32)
    y = xf * lax.rsqrt(jnp.mean(xf * xf, -1, keepdims=True) + EPS)
    return (y * g.astype(f32)).astype(x.dtype)


def layer_norm(x, g, b):
    xf = x.astype(f32)
    mu = jnp.mean(xf, -1, keepdims=True)
    var = jnp.mean(jnp.square(xf - mu), -1, keepdims=True)
    return ((xf - mu) * lax.rsqrt(var + 1e-5) * g.astype(f32) + b.astype(f32)).astype(x.dtype)


def depthwise_conv(x, w, b, pad_l, pad_r):
    y = lax.conv_general_dilated(x, w[:, None, :].astype(x.dtype), window_strides=(1,),
                                 padding=[(pad_l, pad_r)], dimension_numbers=('NWC', 'WIO', 'NWC'),
                                 feature_group_count=x.shape[-1])
    return y + b


def axial_rope(n_tokens):
    n_rows = n_tokens // GRID_W
    row = jnp.repeat(jnp.arange(n_rows), GRID_W).astype(f32)
    col = jnp.tile(jnp.arange(GRID_W), n_rows).astype(f32)
    n_freq = QK_ROPE // 4
    inv = 1.0 / (ROPE_THETA ** (jnp.arange(n_freq, dtype=f32) / n_freq))
    ang = jnp.concatenate([row[:, None] * inv, col[:, None] * inv], -1)
    return jnp.cos(ang), jnp.sin(ang)


def rope_tail(x, cos, sin):
    r = x[..., QK_NOPE:].astype(f32)
    r1, r2 = jnp.split(r, 2, -1)
    rot = jnp.concatenate([r1 * cos - r2 * sin, r1 * sin + r2 * cos], -1).astype(x.dtype)
    return jnp.concatenate([x[..., :QK_NOPE], rot], -1)


def mla_kv(ckv, krope, w_kvb, k_g):
    B, T, _ = ckv.shape
    kv = (ckv @ w_kvb).reshape(B, T, MLA_HEADS, QK_NOPE + V_HEAD)
    k_nope, v = kv[..., :QK_NOPE], kv[..., QK_NOPE:]
    k_r = jnp.broadcast_to(krope[:, :, None, :], (B, T, MLA_HEADS, QK_ROPE)).astype(k_nope.dtype)
    k = rms_norm(jnp.concatenate([k_nope, k_r], -1), k_g)
    return k, v


def block_attention(q, k, v):
    B, Tq, H, dk = q.shape
    nb = Tq // Q_BLOCK
    qb = jnp.moveaxis(q.reshape(B, nb, Q_BLOCK, H, dk), 1, 0)
    scale = dk ** -0.5

    def one(q_blk):
        s = jnp.einsum('bqhd,bkhd->bhqk', q_blk, k).astype(f32) * scale
        p = jax.nn.softmax(s, -1).astype(v.dtype)
        return jnp.einsum('bhqk,bkhd->bqhd', p, v)

    o = lax.map(one, qb)
    return jnp.moveaxis(o, 0, 1).reshape(B, Tq, H, v.shape[-1])


def bidir_rglru(xc, w_a, b_a, w_i, b_i, lam, h0):
    B, T, W = xc.shape
    xh = xc.reshape(B, T, LRU_HEADS, LRU_HD)
    r = jax.nn.sigmoid((jnp.einsum('bthi,dhij->dbthj', xh, w_a).reshape(2, B, T, W)
                        + b_a[:, None, None, :]).astype(f32))
    i = jax.nn.sigmoid((jnp.einsum('bthi,dhij->dbthj', xh, w_i).reshape(2, B, T, W)
                        + b_i[:, None, None, :]).astype(f32))
    log_a = -RG_C * r * jax.nn.softplus(-lam.astype(f32))[:, None, None, :]
    a = jnp.exp(log_a)
    u = jnp.sqrt(-jnp.expm1(2.0 * log_a)) * i * xc.astype(f32)[None]
    a = jnp.stack([a[0], a[1][:, ::-1]])
    u = jnp.stack([u[0], u[1][:, ::-1]])

    def step(h, au):
        a_t, u_t = au
        h = a_t * h + u_t
        return h, h

    h_fin, hs = lax.scan(step, h0, (jnp.moveaxis(a, 2, 0), jnp.moveaxis(u, 2, 0)))
    hs = jnp.moveaxis(hs, 0, 2)
    return hs[0] + hs[1][:, ::-1], h_fin


def conformer_conv(zc, dw_w, dw_b, ln_g, ln_b):
    a, b = jnp.split(zc, 2, -1)
    h = a * jax.nn.sigmoid(b)
    h = depthwise_conv(h, dw_w, dw_b, CM_K // 2, CM_K // 2)
    h = layer_norm(h, ln_g, ln_b)
    return jax.nn.silu(h)


def peer(h, w_q, sub_keys, u, v):
    B, T, D = h.shape
    n = B * T
    hf = h.reshape(n, D)
    q = (hf @ w_q).reshape(n, PEER_HEADS, 2, PEER_QDIM // 2)
    s = jnp.einsum('nhpd,hpkd->nhpk', q, sub_keys).astype(f32)
    s1, i1 = lax.top_k(s[:, :, 0], PEER_TOPK)
    s2, i2 = lax.top_k(s[:, :, 1], PEER_TOPK)
    cand_s = (s1[..., :, None] + s2[..., None, :]).reshape(n, PEER_HEADS, PEER_TOPK * PEER_TOPK)
    cand_i = (i1[..., :, None] * N_KEYS + i2[..., None, :]).reshape(n, PEER_HEADS, PEER_TOPK * PEER_TOPK)
    top_s, pos = lax.top_k(cand_s, PEER_TOPK)
    idx = jnp.take_along_axis(cand_i, pos, -1)
    gate = jax.nn.softmax(top_s, -1)
    nblk = n // TOK_BLOCK
    idx = idx.reshape(nblk, TOK_BLOCK, PEER_HEADS * PEER_TOPK)
    gate = gate.reshape(nblk, TOK_BLOCK, PEER_HEADS * PEER_TOPK)
    xb = hf.reshape(nblk, TOK_BLOCK, D)

    def apply(args):
        x_b, i_b, g_b = args
        act = jax.nn.gelu(jnp.einsum('tkd,td->tk', u[i_b], x_b).astype(f32))
        wgt = (g_b * act).astype(x_b.dtype)
        return jnp.einsum('tk,tkd->td', wgt, v[i_b])

    out = lax.map(apply, (xb, idx, gate))
    return out.reshape(B, T, D)


def trunk_layer(x, cvec, p, ctx_ckv=None, ctx_krope=None, h0=None):
    latent = ctx_ckv is not None
    B, T, _ = x.shape
    mod = jax.nn.silu(cvec) @ p['ada_w'] + p['ada_b']
    sh1, sc1, g1, sh2, sc2, g2 = jnp.split(mod[:, None, :], 6, -1)
    hn = rms_norm(x, p['norm_mix_g']) * (1 + sc1) + sh1
    offs = [int(o) for o in np.cumsum(IN_SPLITS)[:-1]]
    zq, zkv, zkr, zx, zg, zc = jnp.split(hn @ p['w_in'], offs, -1)
    q = (rms_norm(zq, p['mla_qa_g']) @ p['mla_w_qb']).reshape(B, T, MLA_HEADS, QK_HEAD)
    q = rms_norm(q, p['mla_q_g'])
    ckv = rms_norm(zkv, p['mla_kva_g'])
    k, v = mla_kv(ckv, zkr, p['mla_w_kvb'], p['mla_k_g'])
    if latent:
        cos, sin = axial_rope(T)
        cos, sin = cos[:, None, :], sin[:, None, :]
        q = rope_tail(q, cos, sin)
        k = rope_tail(k, cos, sin)
        kc, vc = mla_kv(ctx_ckv, ctx_krope, p['mla_w_kvb'], p['mla_k_g'])
        k = jnp.concatenate([kc, k], 1)
        v = jnp.concatenate([vc, v], 1)
    else:
        h0 = jnp.zeros((2, B, LRU_W), f32)
    attn = block_attention(q, k, v).reshape(B, T, MLA_W)
    xc = depthwise_conv(zx, p['lru_conv_w'], p['lru_conv_b'], 2, 1)
    y_lru, h_fin = bidir_rglru(xc, p['lru_w_a'], p['lru_b_a'], p['lru_w_i'], p['lru_b_i'], p['lru_lam'], h0)
    lru = y_lru.astype(x.dtype) * jax.nn.gelu(zg)
    conv = conformer_conv(zc, p['cm_dw_w'], p['cm_dw_b'], p['cm_ln_g'], p['cm_ln_b'])
    g_att, g_lru, g_conv = jnp.split(p['grp_g'], [MLA_W, MLA_W + LRU_W])
    mixed = jnp.concatenate([rms_norm(attn, g_att), rms_norm(lru, g_lru), rms_norm(conv, g_conv)], -1)
    x = x + g1 * (mixed @ p['w_out'])
    hf = rms_norm(x, p['norm_ffn_g']) * (1 + sc2) + sh2
    x = x + g2 * peer(hf, p['peer_w_q'], p['peer_keys'], p['peer_u'], p['peer_v'])
    return x, ckv, zkr, jnp.swapaxes(h_fin, 0, 1).astype(x.dtype)


def setup_inputs(seed: int = 0) -> dict:
    key = jax.random.key(seed)
    ks = iter(jax.random.split(key, 48))

    def nrm(shape, scale=1.0):
        return scale * jax.random.normal(next(ks), shape, f32)

    def gain(shape):
        return 1.0 + nrm(shape, 0.01)

    L = DEPTH
    s_lam = jax.random.uniform(next(ks), (L, 2, LRU_W), f32, 0.9, 0.999) ** (1.0 / RG_C)
    return {
        'x_prompt': nrm((BATCH, SEQ, D_MODEL)),
        'x_sample': nrm((DEC_BATCH, DEC_SEQ, D_MODEL)),
        'cache_ckv': nrm((DEC_BATCH, L, PAST_LEN, KV_LORA)),
        'cache_krope': nrm((DEC_BATCH, L, PAST_LEN, QK_ROPE)),
        'state_lru': nrm((DEC_BATCH, L, 2, LRU_W), 0.5),
        'c': nrm((DEC_BATCH, D_MODEL)),
        'c_ctx': nrm((D_MODEL,)),
        'ada_w': nrm((L, D_MODEL, 6 * D_MODEL), 0.5 * D_MODEL ** -0.5),
        'ada_b': nrm((L, 6 * D_MODEL), 0.01),
        'norm_mix_g': gain((L, D_MODEL)),
        'w_in': nrm((L, D_MODEL, IN_W), D_MODEL ** -0.5),
        'mla_qa_g': gain((L, Q_LORA)),
        'mla_w_qb': nrm((L, Q_LORA, MLA_HEADS * QK_HEAD), Q_LORA ** -0.5),
        'mla_q_g': gain((L, QK_HEAD)),
        'mla_kva_g': gain((L, KV_LORA)),
        'mla_w_kvb': nrm((L, KV_LORA, MLA_HEADS * (QK_NOPE + V_HEAD)), KV_LORA ** -0.5),
        'mla_k_g': gain((L, QK_HEAD)),
        'lru_conv_w': nrm((L, LRU_CONV, LRU_W), LRU_CONV ** -0.5),
        'lru_conv_b': nrm((L, LRU_W), 0.01),
        'lru_w_a': nrm((L, 2, LRU_HEADS, LRU_HD, LRU_HD), LRU_HD ** -0.5),
        'lru_b_a': nrm((L, 2, LRU_W), 0.01),
        'lru_w_i': nrm((L, 2, LRU_HEADS, LRU_HD, LRU_HD), LRU_HD ** -0.5),
        'lru_b_i': nrm((L, 2, LRU_W), 0.01),
        'lru_lam': jnp.log(s_lam) - jnp.log1p(-s_lam),
        'cm_dw_w': nrm((L, CM_K, CONV_W), CM_K ** -0.5),
        'cm_dw_b': nrm((L, CONV_W), 0.01),
        'cm_ln_g': gain((L, CONV_W)),
        'cm_ln_b': nrm((L, CONV_W), 0.01),
        'grp_g': gain((L, MIX_W)),
        'w_out': nrm((L, MIX_W, D_MODEL), MIX_W ** -0.5),
        'norm_ffn_g': gain((L, D_MODEL)),
        'peer_w_q': nrm((L, D_MODEL, PEER_HEADS * PEER_QDIM), D_MODEL ** -0.5),
        'peer_keys': nrm((L, PEER_HEADS, 2, N_KEYS, PEER_QDIM // 2), (PEER_QDIM // 2) ** -0.5),
        'peer_u': nrm((L, N_EXPERTS, D_MODEL), D_MODEL ** -0.5),
        'peer_v': nrm((L, N_EXPERTS, D_MODEL), 0.25),
    }


def reference(x_prompt, x_sample, cache_ckv, cache_krope, state_lru, c, c_ctx,
              ada_w, ada_b, norm_mix_g, w_in, mla_qa_g, mla_w_qb, mla_q_g, mla_kva_g, mla_w_kvb, mla_k_g,
              lru_conv_w, lru_conv_b, lru_w_a, lru_b_a, lru_w_i, lru_b_i, lru_lam,
              cm_dw_w, cm_dw_b, cm_ln_g, cm_ln_b, grp_g, w_out, norm_ffn_g,
              peer_w_q, peer_keys, peer_u, peer_v):
    stacked = {
        'ada_w': ada_w, 'ada_b': ada_b, 'norm_mix_g': norm_mix_g, 'w_in': w_in,
        'mla_qa_g': mla_qa_g, 'mla_w_qb': mla_w_qb, 'mla_q_g': mla_q_g, 'mla_kva_g': mla_kva_g,
        'mla_w_kvb': mla_w_kvb, 'mla_k_g': mla_k_g,
        'lru_conv_w': lru_conv_w, 'lru_conv_b': lru_conv_b, 'lru_w_a': lru_w_a, 'lru_b_a': lru_b_a,
        'lru_w_i': lru_w_i, 'lru_b_i': lru_b_i, 'lru_lam': lru_lam,
        'cm_dw_w': cm_dw_w, 'cm_dw_b': cm_dw_b, 'cm_ln_g': cm_ln_g, 'cm_ln_b': cm_ln_b,
        'grp_g': grp_g, 'w_out': w_out, 'norm_ffn_g': norm_ffn_g,
        'peer_w_q': peer_w_q, 'peer_keys': peer_keys, 'peer_u': peer_u, 'peer_v': peer_v,
    }
    yp, ys = x_prompt, x_sample
    ckv_list, kr_list, h_list = [], [], []
    for l in range(DEPTH):
        p = {name: arr[l] for name, arr in stacked.items()}
        yp, ckv_l, kr_l, h_l = trunk_layer(yp, c_ctx[None, :], p)
        ckv_list.append(ckv_l)
        kr_list.append(kr_l)
        h_list.append(h_l)
        h0 = jnp.swapaxes(state_lru[:, l], 0, 1).astype(f32)
        ys, _, _, _ = trunk_layer(ys, c, p, cache_ckv[:, l], cache_krope[:, l], h0)
    new_ckv = jnp.stack(ckv_list, 1)
    new_krope = jnp.stack(kr_list, 1)
    new_lru = jnp.stack(h_list, 1)
    return (yp, ys, new_ckv, new_krope, new_lru)
```

```python
import numpy as np
from contextlib import ExitStack
import concourse.bass as bass
import concourse.mybir as mybir
from concourse.bass_utils import run_bass_kernel_spmd

F32 = mybir.dt.float32
BF16 = mybir.dt.bfloat16
I32 = mybir.dt.int32
U32 = mybir.dt.uint32
ALU = mybir.AluOpType
AF = mybir.ActivationFunctionType
AX = mybir.AxisListType

D = 2048
L = 2
NT = 8
T = 1024
NK = 1536
NKT = 12
IN_W = 2880
EPS = 1e-6
NEG = -30000.0

ENGS = ("pe", "act", "dve", "pool", "sp")
N_DMA_SEM = {"sp": 12, "pool": 6, "act": 4}


class Op:
    __slots__ = ("eng", "fn", "reads", "writes", "dma", "deps", "sig", "idx", "dsem", "dval", "prev_same", "barrier")

    def __init__(self, eng, fn, reads, writes, dma):
        self.eng, self.fn, self.reads, self.writes, self.dma = eng, fn, reads, writes, dma
        self.deps = set()
        self.sig = None
        self.dsem = None
        self.dval = None
        self.prev_same = None
        self.barrier = False


class Prog:
    def __init__(self, nc):
        self.nc = nc
        self.ops = []
        self.overlaps = {}

    def alias(self, a, others):
        for b in others:
            self.overlaps.setdefault(a, set()).add(b)
            self.overlaps.setdefault(b, set()).add(a)

    def add(self, eng, fn, reads=(), writes=(), dma=False):
        o = Op(eng, fn, tuple(reads), tuple(writes), dma)
        o.idx = len(self.ops)
        self.ops.append(o)
        return o

    def pe(self, fn, r=(), w=()):
        return self.add("pe", fn, r, w)

    def act(self, fn, r=(), w=()):
        return self.add("act", fn, r, w)

    def dve(self, fn, r=(), w=()):
        return self.add("dve", fn, r, w)

    def pool(self, fn, r=(), w=()):
        return self.add("pool", fn, r, w)

    def dma(self, fn, r=(), w=(), eng="sp"):
        return self.add(eng, fn, r, w, dma=True)

    def barrier(self, fns):
        for eng, fn in fns.items():
            o = self.add(eng, fn, (), ())
            o.barrier = True

    def build(self, stack):
        nc = self.nc
        ops = self.ops
        last_w = {}
        readers = {}
        ov = self.overlaps
        for o in ops:
            deps = set()
            for k0 in o.reads:
                for k in (k0, *ov.get(k0, ())):
                    if k in last_w:
                        deps.add(last_w[k])
            for k0 in o.writes:
                for k in (k0, *ov.get(k0, ())):
                    if k in last_w:
                        deps.add(last_w[k])
                    for rd in readers.get(k, ()):
                        deps.add(rd)
            if o.barrier:
                seen_e = {}
                seen_q = {}
                for p in ops[:o.idx]:
                    if p.dma:
                        seen_q.setdefault(p.eng, []).append(p.idx)
                    elif p.eng != "sp":
                        seen_e[p.eng] = p.idx
                deps |= set(seen_e.values())
                for q, lst in seen_q.items():
                    deps |= set(lst[-N_DMA_SEM[q]:])
            deps.discard(o.idx)
            o.deps = deps
            for k in o.writes:
                last_w[k] = o.idx
                readers[k] = []
            for k in o.reads:
                readers.setdefault(k, []).append(o.idx)
        need = [False] * len(ops)
        for o in ops:
            latest = {}
            for d in o.deps:
                p = ops[d]
                if p.dma:
                    continue
                if p.eng == o.eng and p.eng == "pe" and not o.dma:
                    continue
                if d > latest.get(p.eng, -1):
                    latest[p.eng] = d
            for d in latest.values():
                need[d] = True
        esem = {e: stack.enter_context(nc.semaphore("s_" + e)) for e in ENGS}
        dsems = {e: [stack.enter_context(nc.semaphore("d_%s%d" % (e, i))) for i in range(n)]
                 for e, n in N_DMA_SEM.items()}
        cnt = {e: 0 for e in ENGS}
        dcnt = {e: 0 for e in N_DMA_SEM}
        prev_on_sem = {}
        for o in ops:
            if o.dma:
                k = dcnt[o.eng]
                n = N_DMA_SEM[o.eng]
                o.dsem = dsems[o.eng][k % n]
                o.dval = 16 * (k // n + 1)
                o.prev_same = prev_on_sem.get((o.eng, k % n))
                prev_on_sem[(o.eng, k % n)] = o.idx
                dcnt[o.eng] += 1
            elif need[o.idx]:
                cnt[o.eng] += 1
                o.sig = cnt[o.eng]
        global SEM_COUNTS
        SEM_COUNTS = (dict(cnt), dict(dcnt), len(ops))
        print('SEM_COUNTS', SEM_COUNTS, flush=True)
        by_eng = {e: [o for o in ops if o.eng == e] for e in ENGS}
        block = stack.enter_context(nc.Block())

        def emit(ename):
            def body(eng):
                waited = {}

                def wait(sem, val):
                    key = id(sem)
                    if waited.get(key, 0) >= val:
                        return
                    waited[key] = val
                    eng.wait_ge(sem, val)

                issued = []
                for o in by_eng[ename]:
                    if o.dma and o.prev_same is not None:
                        p = ops[o.prev_same]
                        wait(p.dsem, p.dval)
                    for d in sorted(o.deps):
                        p = ops[d]
                        if p.dma:
                            wait(p.dsem, p.dval)
                        else:
                            if p.sig is None:
                                continue
                            if p.eng == ename and ename == "pe" and not o.dma:
                                continue
                            wait(esem[p.eng], p.sig)
                    ins = o.fn(eng)
                    if o.dma:
                        ins.then_inc(o.dsem, 16)
                        issued.append(o)
                    elif o.sig is not None:
                        ins.then_inc(esem[ename], 1)
                for o in issued[-N_DMA_SEM.get(ename, 0):]:
                    wait(o.dsem, o.dval)
            return body

        block.tensor(emit("pe"))
        block.scalar(emit("act"))
        block.vector(emit("dve"))
        block.gpsimd(emit("pool"))
        block.sync(emit("sp"))


W_NAMES = ['ada_w', 'ada_b', 'norm_mix_g', 'w_in', 'mla_qa_g', 'mla_w_qb', 'mla_q_g', 'mla_kva_g', 'mla_w_kvb',
           'mla_k_g', 'lru_conv_w', 'lru_conv_b', 'lru_w_a', 'lru_b_a', 'lru_w_i', 'lru_b_i', 'lru_lam',
           'cm_dw_w', 'cm_dw_b', 'cm_ln_g', 'cm_ln_b', 'grp_g', 'w_out', 'norm_ffn_g', 'peer_w_q', 'peer_keys',
           'peer_u', 'peer_v']
GELU_C = 1.5957691216057308
POOL_HEADS = 0
POOL_HEAD_SET = (1, 3, 5, 7)


def build_program(shapes, n_layers=L, do_mixer=True, do_peer=True, dbg=None, peer_mode='gather', peer_stop=0):
    nc = bass.Bass("TRN2", target_bir_lowering=False)
    st = ExitStack()
    P = Prog(nc)
    dr = {}
    for name, (shape, dt) in shapes.items():
        dr[name] = nc.dram_tensor(name, list(shape), dt, kind="ExternalInput").ap()
    o_y = nc.dram_tensor("o_y", [T, D], F32, kind="ExternalOutput").ap()
    o_ckv = nc.dram_tensor("o_ckv", [L, T, 256], F32, kind="ExternalOutput").ap()
    o_kr = nc.dram_tensor("o_kr", [L, T, 64], F32, kind="ExternalOutput").ap()
    o_lru = nc.dram_tensor("o_lru", [L, 4, 2, 512], F32, kind="ExternalOutput").ap()
    o_dbg = nc.dram_tensor("o_dbg", list(dbg), F32, kind="ExternalOutput").ap() if dbg else None
    xd = o_y.rearrange("(t p) d -> t p d", p=128)

    def sb(name, shape, dt=F32):
        return st.enter_context(nc.sbuf_tensor("sb_" + name, shape, dt))

    def ps(name, shape, dt=F32):
        return st.enter_context(nc.psum_tensor("ps_" + name, shape, dt))

    big = sb("big", [128, 8192])
    hT = big[:, :].bitcast(BF16).rearrange("p (c n) -> p c n", c=16)
    ug = [big[:, i * D:(i + 1) * D] for i in range(3)]
    accp = big[:, 3 * D:4 * D]
    bigb = big[:, :].bitcast(BF16)
    ugb = [bigb[:, i * D:(i + 1) * D] for i in range(6)]
    P.alias("hT", ["ug0", "ug1", "ug2", "accp", "ugb0", "ugb1", "ugb2", "ugb3", "ugb4", "ugb5", "junkb"])
    mixS = sb("mixS", [128, 8, T], BF16)
    vgb = ugb
    junkb = bigb[:, 7 * D:8 * D]
    P.alias("accp", ["junkb"])
    modAB = sb("modAB", [128, 2 * D])
    modA, modB = modAB[:, 0:D], modAB[:, D:2 * D]
    kst = modAB[:, 0:8 * 320].rearrange("p (t d) -> p t d", t=8)
    P.alias("kst", ["modA", "modB"])
    modG = sb("modG", [128, D])
    wb = [sb("wb%d" % i, [128, 16, 256], BF16) for i in range(2)]
    tmA = sb("tmA", [128, D])
    tmB = sb("tmB", [128, D], BF16)
    ar = [sb("ar%d" % i, [128, NK]) for i in range(9)]
    ark = ["ar%d" % i for i in range(9)]
    CC = sb("CC", [64, NK])
    SS = sb("SS", [64, NK])
    qaT = sb("qaT", [128, 4, T], BF16)
    ckvT = sb("ckvT", [128, 2, NK], BF16)
    knT = sb("knT", [128, NK], BF16)
    krT = sb("krT", [64, NK], BF16)
    qnT = sb("qnT", [128, T], BF16)
    qrT = sb("qrT", [64, T], BF16)
    vh = sb("vh", [128, NKT, 128], BF16)
    pT = [sb("pT%d" % i, [128, 512], BF16) for i in range(2)]
    keysT = qaT[:, 0:2, :].rearrange("p a (b n) -> p (a b) n", b=8)
    P.alias("qaT", ["keysT"])
    qTj = sb("qTj", [128, 2, 128], BF16)
    ident = sb("ident", [128, 128], BF16)
    identf = sb("identf", [128, 128])
    ones = sb("ones", [128, 128], BF16)
    onesf = sb("onesf", [128, 128])
    cT = sb("cT", [128, 16])
    cTb = sb("cTb", [128, 16, 128], BF16)
    sm = sb("sm", [128, 16])
    vec = sb("vec", [128, 96])
    maskb = sb("maskb", [128, 48])
    flag = sb("flag", [128, 1])
    h0t = sb("h0t", [128, L * 2 * 4])
    taps = sb("taps", [128, 4, 31])
    bwt = [sb("bwt%d" % i, [128, 128], BF16) for i in range(2)]
    bwf = sb("bwf", [128, 128])
    iot = sb("iot", [128, 16])
    pk = ckvT[:, 0, :].bitcast(F32).rearrange("p (s n) -> p s n", s=6)
    tk = ckvT[:, 1, 0:1024].bitcast(F32).rearrange("p (h n) -> p h n", h=8)
    tki = knT[:, 0:768].bitcast(U32).rearrange("p (h n) -> p h n", h=8)
    pki = knT[:, 768:1024].bitcast(I32)
    P.alias("ckvT", ["pk0", "pk1", "pk2", "pk3", "pk4", "pk5", "tk"])
    P.alias("knT", ["tki", "pki"])
    thrT = sb("thrT", [128, 8, 8])
    biasT = sb("biasT", [128, 8, 8])
    s1sb = sb("s1sb", [128, 2, 8, 4])
    bsc = sb("bsc", [128, 4])
    dgr = [sb("dgr%d" % i, [128, 128], BF16) for i in range(4)]
    zer = sb("zer", [128, 128], BF16)
    bank = [ps("bank%d" % i, [128, 512]) for i in range(6)]
    bankb = [ps("bankb%d" % i, [128, 1024], BF16) for i in range(2)]

    wb_i = [0]

    def next_wb():
        i = wb_i[0] % 2
        wb_i[0] += 1
        return wb[i], "wb%d" % i

    P.pool(lambda e: e.memset(identf[:], 0.0), w=["identf"])
    P.pool(lambda e: e.affine_select(out=identf[:], in_=identf[:], pattern=[[-1, 128]], compare_op=ALU.not_equal,
                                     fill=1.0, base=0, channel_multiplier=1), r=["identf"], w=["identf"])
    P.dve(lambda e: e.tensor_copy(ident[:], identf[:]), r=["identf"], w=["ident"])
    P.dve(lambda e: e.memset(onesf[:], 1.0), w=["onesf"])
    P.dve(lambda e: e.memset(bsc[:], 0.0), w=["bsc"])
    P.dve(lambda e: e.memset(zer[:], 0.0), w=["zer"])
    P.dve(lambda e: e.memset(ones[:], 1.0), w=["ones"])
    for t in range(NT):
        P.dma(lambda e, t=t: e.dma_start(out=xd[t], in_=dr["x"][t * 128:(t + 1) * 128, :]), w=["xd%d" % t])
    P.dma(lambda e: e.dma_start(out=maskb[:], in_=dr["maskb"]), w=["maskb"])
    P.dma(lambda e: e.dma_start(out=flag[:], in_=dr["flag"]), w=["flag"])
    P.dma(lambda e: e.dma_start(out=CC[:], in_=dr["CC"]), w=["CC"])
    P.dma(lambda e: e.dma_start(out=SS[:], in_=dr["SS"]), w=["SS"])
    P.dma(lambda e: e.dma_start(out=h0t[:], in_=dr["h0"]), w=["h0t"])
    P.dma(lambda e: e.dma_start(out=iot[:], in_=dr["iota16"]), w=["iot"])
    P.dma(lambda e: e.dma_start(out=cT[:], in_=dr["cvec"]), w=["cT"])
    P.act(lambda e: e.activation(out=cT[:], in_=cT[:], func=AF.Silu), r=["cT"], w=["cT"])
    P.dve(lambda e: e.tensor_copy(cTb[:], cT[:, :].unsqueeze(2).to_broadcast([128, 16, 128])), r=["cT"], w=["cTb"])

    dbg_n = [0]

    def dump(src_ap, key, ncols):
        if o_dbg is None:
            return
        o = dbg_n[0]
        dbg_n[0] += 1
        P.dma(lambda e: e.dma_start(out=o_dbg[o, :, 0:ncols], in_=src_ap), r=[key])

    def vcol(i, n=128):
        return vec[0:n, i:i + 1]

    def load_vec(i, ap_1d, n=128):
        P.dma(lambda e: e.dma_start(out=vec[0:n, i:i + 1], in_=ap_1d.rearrange("(p o) -> p o", o=1)), w=["vec%d" % i])

    def load_cols(wname, l, c0, ncols, kdim):
        w_t, wk = next_wb()
        kc = kdim // 128
        P.dma(lambda e: e.dma_start(out=w_t[:, 0:kc, 0:ncols],
                                    in_=dr[wname][l, :, c0:c0 + ncols].rearrange("(k p) n -> p k n", p=128)),
              w=[wk], eng="pool")
        return w_t, wk

    def proj_fm(w_t, wk, col, m, src, srck, kc, dst_fn, nblk=2, bi0=2):
        for blk in range(nblk):
            bi = bi0 + blk % 2
            bk = bank[bi]
            bkk = "bank%d" % bi
            for k in range(kc):
                P.pe(lambda e, k=k, bk=bk, blk=blk: e.matmul(bk[0:m, :], w_t[:, k, col:col + m], src[:, k, blk * 512:(blk + 1) * 512],
                                                            start=(k == 0), stop=(k == kc - 1)), r=[wk, srck], w=[bkk])
            dst_fn(blk, bk, bkk)

    def rstd_rows(srcs, dst, dstk, nfeat, eps, nblk=2):
        n = len(srcs)
        for blk in range(nblk):
            bi = 4 + blk % 2
            bk = bank[bi]
            bkk = "bank%d" % bi
            for i, (fn, key, m) in enumerate(srcs):
                sq = tmA[0:m, (i % 2) * 512:(i % 2) * 512 + 512]
                sqk = "tmA%d" % (i % 2)
                P.act(lambda e, fn=fn, sq=sq, blk=blk: e.activation(out=sq, in_=fn(blk), func=AF.Square), r=[key], w=[sqk])
                P.pe(lambda e, sq=sq, m=m, i=i, bk=bk: e.matmul(bk[:], onesf[0:m, :], sq, start=(i == 0), stop=(i == n - 1)),
                     r=[sqk, "onesf"], w=[bkk])
            P.act(lambda e, bk=bk, blk=blk: e.activation(out=dst[:, blk * 512:(blk + 1) * 512], in_=bk[:], func=AF.Sqrt,
                                                        bias=eps, scale=1.0 / nfeat), r=[bkk], w=[dstk])
        P.dve(lambda e: e.reciprocal(dst[:, 0:nblk * 512], dst[:, 0:nblk * 512]), r=[dstk], w=[dstk])
    P.alias("tmA", ["tmA0", "tmA1"])
    P.alias("smg", ["sm0", "sm1", "sm2", "sm3"])

    def ada_mod(l, part):
        gname = "norm_mix_g" if part == 0 else "norm_ffn_g"
        P.dve(lambda e: e.tensor_copy(cTb[:], cT[:, :].unsqueeze(2).to_broadcast([128, 16, 128])), r=["cT"], w=["cTb"])
        for j, (dst, dname) in enumerate(((modB, "modB"), (modA, "modA"), (modG[:, :], "modG"))):
            c0 = (part * 3 + j) * D
            for cb in range(8):
                w_t, wk = load_cols("ada_w", l, c0 + cb * 256, 256, D)
                bi = cb % 2
                bk = bank[bi]
                bkk = "bank%d" % bi
                for k in range(16):
                    P.pe(lambda e, w_t=w_t, k=k, bk=bk: e.matmul(bk[:, 0:256], cTb[:, k, :], w_t[:, k, :], start=(k == 0), stop=(k == 15)),
                         r=["cTb", wk], w=[bkk])
                P.act(lambda e, dst=dst, cb=cb, bk=bk: e.copy(dst[:, cb * 256:(cb + 1) * 256], bk[:, 0:256]), r=[bkk], w=[dname])
            P.dma(lambda e, c0=c0: e.dma_start(out=tmA[:], in_=dr["ada_b"][l:l + 1, c0:c0 + D].partition_broadcast(128)), w=["tmA"])
            P.dve(lambda e, dst=dst: e.tensor_tensor(dst, dst, tmA[:], ALU.add), r=[dname, "tmA"], w=[dname])
        P.dma(lambda e: e.dma_start(out=tmA[:], in_=dr[gname][l:l + 1, :].partition_broadcast(128)), w=["tmA"])
        P.dve(lambda e: e.scalar_tensor_tensor(modA, modA, 1.0, tmA[:], ALU.add, ALU.mult), r=["modA", "tmA"], w=["modA"])

    def norm_tile(t):
        for hf in range(2):
            P.dma(lambda e, hf=hf: e.dma_start(out=ar[hf][:, 0:1024], in_=xd[t][:, hf * 1024:(hf + 1) * 1024]), r=["xd%d" % t], w=[ark[hf]])
            P.act(lambda e, hf=hf: e.activation(out=tmB[:, hf * 1024:(hf + 1) * 1024], in_=ar[hf][:, 0:1024], func=AF.Square, accum_out=sm[:, hf:hf + 1]),
                  r=[ark[hf]], w=["tmB", "sm%d" % hf])
        P.dve(lambda e: e.tensor_tensor(sm[:, 2:3], sm[:, 0:1], sm[:, 1:2], ALU.add), r=["sm0", "sm1"], w=["sm2"])
        P.act(lambda e: e.activation(out=sm[:, 3:4], in_=sm[:, 2:3], func=AF.Sqrt, bias=EPS, scale=1.0 / D), r=["sm2"], w=["sm3"])
        P.dve(lambda e: e.reciprocal(sm[:, 3:4], sm[:, 3:4]), r=["sm3"], w=["sm3"])
        for hf in range(2):
            hs = slice(hf * 1024, (hf + 1) * 1024)
            P.dve(lambda e, hf=hf, hs=hs: e.scalar_tensor_tensor(tmA[:, hs], ar[hf][:, 0:1024], sm[:, 3:4], modA[:, hs], ALU.mult, ALU.mult),
                  r=[ark[hf], "sm3", "modA"], w=["tmA"])
        P.dve(lambda e: e.tensor_tensor(tmA[:], tmA[:], modB, ALU.add), r=["tmA", "modB"], w=["tmA"])
        P.act(lambda e: e.copy(tmB[:], tmA[:]), r=["tmA"], w=["tmB"])

    def transpose_tile_to(dst_fn, dstk):
        for half in range(2):
            bb = bankb[half]
            bbk = "bankb%d" % half
            for c in range(8):
                cc = half * 8 + c
                P.pe(lambda e, bb=bb, c=c, cc=cc: e.transpose(bb[:, c * 128:(c + 1) * 128], tmB[:, cc * 128:(cc + 1) * 128], ident[:]),
                     r=["tmB", "ident"], w=[bbk])
            if half == 0:
                P.act(lambda e, bb=bb, half=half: e.copy(dst_fn(half), bb[:, :].rearrange("p (c n) -> p c n", c=8)), r=[bbk], w=[dstk])
            else:
                P.dve(lambda e, bb=bb, half=half: e.tensor_copy(dst_fn(half), bb[:, :].rearrange("p (c n) -> p c n", c=8)), r=[bbk], w=[dstk])

    def norm_to_hT():
        for t in range(NT):
            norm_tile(t)
            transpose_tile_to(lambda half, t=t: hT[:, half * 8:(half + 1) * 8, t * 128:(t + 1) * 128], "hT")

    def gelu_fm(src, srck, tmp, tmpk, dst, dstk, n=T):
        P.dve(lambda e: e.tensor_tensor(tmp[:, 0:n], src[:, 0:n], src[:, 0:n], ALU.mult), r=[srck], w=[tmpk])
        P.dve(lambda e: e.tensor_scalar(tmp[:, 0:n], tmp[:, 0:n], 0.044715, 1.0, op0=ALU.mult, op1=ALU.add), r=[tmpk], w=[tmpk])
        P.dve(lambda e: e.tensor_tensor(tmp[:, 0:n], tmp[:, 0:n], src[:, 0:n], ALU.mult), r=[tmpk, srck], w=[tmpk])
        P.act(lambda e: e.activation(out=tmp[:, 0:n], in_=tmp[:, 0:n], func=AF.Sigmoid, scale=GELU_C), r=[tmpk], w=[tmpk])
        P.dve(lambda e: e.tensor_tensor(dst[:, 0:n], tmp[:, 0:n], src[:, 0:n], ALU.mult), r=[tmpk, srck], w=[dstk])

    def group_norm_inplace(mix, mixk, c0, nch, gcol0, nfeat, rs, rsk):
        rstd_rows([(lambda blk, i=i: mix[:, c0 + i, blk * 512:(blk + 1) * 512], mixk, 128) for i in range(nch)], rs, rsk, nfeat, EPS)
        for i in range(nch):
            P.dve(lambda e, i=i: e.scalar_tensor_tensor(mix[:, c0 + i, :], mix[:, c0 + i, :], vcol(gcol0 + i), rs[:, 0:T], ALU.mult, ALU.mult),
                  r=[mixk, rsk, "vec%d" % (gcol0 + i)], w=[mixk])

    def lru_branch(l):
        for c in range(4):
            for j in range(4):
                load_vec(j * 4 + c, dr["lru_conv_w"][l, j, c * 128:(c + 1) * 128])
            load_vec(16 + c, dr["lru_conv_b"][l, c * 128:(c + 1) * 128])
            for d in range(2):
                load_vec(20 + d * 4 + c, dr["lru_b_a"][l, d, c * 128:(c + 1) * 128])
                load_vec(28 + d * 4 + c, dr["lru_b_i"][l, d, c * 128:(c + 1) * 128])
                load_vec(36 + d * 4 + c, dr["lru_lam"][l, d, c * 128:(c + 1) * 128])
            load_vec(44 + c, dr["grp_g"][l, 1024 + c * 128:1024 + (c + 1) * 128])
        vk = ["vec%d" % i for i in range(48)]
        P.act(lambda e: e.activation(out=vec[:, 36:44], in_=vec[:, 36:44], func=AF.Exp, scale=-1.0), r=vk[36:44], w=vk[36:44])
        P.act(lambda e: e.activation(out=vec[:, 36:44], in_=vec[:, 36:44], func=AF.Ln, bias=1.0), r=vk[36:44], w=vk[36:44])
        P.dve(lambda e: e.tensor_scalar(vec[:, 36:44], vec[:, 36:44], -8.0, None, op0=ALU.mult), r=vk[36:44], w=vk[36:44])
        XP = 259
        xpad, xc, hb, r_t, i_t, a_t, u_t, hfw = ar[0], ar[1], ar[2], ar[3], ar[4], ar[5], ar[6], ar[7]
        xp = xpad[:, 0:4 * XP].rearrange("p (s n) -> p s n", s=4)
        xc3 = xc[:, 0:T].rearrange("p (s n) -> p s n", s=4)
        for c in range(4):
            for cbh in range(1):
                w_x, wxk = load_cols("w_in", l, 832 + c * 128, 128, D)

            def to_pad(blk, bk, bkk):
                P.act(lambda e: e.copy(xp[:, 2 * blk:2 * blk + 2, 2:258], bk[:, :].rearrange("p (s n) -> p s n", s=2)), r=[bkk], w=["ar0"])
            proj_fm(w_x, wxk, 0, 128, hT, "hT", 16, to_pad)
            P.dve(lambda e: e.memset(xp[:, 0:1, 0:2], 0.0), r=["ar0"], w=["ar0"])
            P.dve(lambda e: e.memset(xp[:, 3:4, 258:259], 0.0), r=["ar0"], w=["ar0"])
            P.dve(lambda e: e.tensor_scalar(xp[:, 1:4, 0:2], xp[:, 0:3, 256:258], flag[:, 0:1], None, op0=ALU.mult), r=["ar0", "flag"], w=["ar0"])
            P.dve(lambda e: e.tensor_scalar(xp[:, 0:3, 258:259], xp[:, 1:4, 2:3], flag[:, 0:1], None, op0=ALU.mult), r=["ar0", "flag"], w=["ar0"])
            P.dve(lambda e, c=c: e.tensor_scalar(xc3, xp[:, :, 0:256], vcol(c), vcol(16 + c), op0=ALU.mult, op1=ALU.add),
                  r=["ar0", "vec%d" % c, "vec%d" % (16 + c)], w=["ar1"])
            for j in range(1, 4):
                P.dve(lambda e, c=c, j=j: e.scalar_tensor_tensor(xc3, xp[:, :, j:j + 256], vcol(j * 4 + c), xc3, ALU.mult, ALU.add),
                      r=["ar0", "ar1", "vec%d" % (j * 4 + c)], w=["ar1"])
            xcb = tmB[:, 0:T]
            P.act(lambda e: e.copy(xcb, xc[:, 0:T]), r=["ar1"], w=["tmB"])
            for d in range(2):
                for gi, (gname, bcol, gdst, gk) in enumerate((("lru_w_a", 20, r_t, "ar3"), ("lru_w_i", 28, i_t, "ar4"))):
                    P.pool(lambda e: e.memset(bwf[:], 0.0), w=["bwf"])
                    for hh in range(2):
                        P.dma(lambda e, gname=gname, hh=hh, d=d, c=c: e.dma_start(out=bwf[hh * 64:(hh + 1) * 64, hh * 64:(hh + 1) * 64],
                                                                                 in_=dr[gname][l, d, 2 * c + hh, :, :]), r=["bwf"], w=["bwf"])
                    P.dve(lambda e, gi=gi: e.tensor_copy(bwt[gi][:], bwf[:]), r=["bwf"], w=["bwt%d" % gi])
                    for blk in range(2):
                        bi = 2 + blk
                        bk = bank[bi]
                        bkk = "bank%d" % bi
                        P.pe(lambda e, gi=gi, bk=bk, blk=blk: e.matmul(bk[:], bwt[gi][:], tmB[:, blk * 512:(blk + 1) * 512], start=True, stop=True),
                             r=["bwt%d" % gi, "tmB"], w=[bkk])
                        P.act(lambda e, gdst=gdst, bk=bk, blk=blk, bcol=bcol, d=d, c=c: e.activation(
                            out=gdst[:, blk * 512:(blk + 1) * 512], in_=bk[:], func=AF.Sigmoid, bias=vcol(bcol + d * 4 + c)),
                            r=[bkk, "vec%d" % (bcol + d * 4 + c)], w=[gk])
                P.act(lambda e, d=d, c=c: e.activation(out=a_t[:, 0:T], in_=r_t[:, 0:T], func=AF.Exp, scale=vcol(36 + d * 4 + c)),
                      r=["ar3", "vec%d" % (36 + d * 4 + c)], w=["ar5"])
                P.dve(lambda e: e.tensor_tensor(u_t[:, 0:T], a_t[:, 0:T], a_t[:, 0:T], ALU.mult), r=["ar5"], w=["ar6"])
                P.act(lambda e: e.activation(out=u_t[:, 0:T], in_=u_t[:, 0:T], func=AF.Sqrt, scale=-1.0, bias=1.0), r=["ar6"], w=["ar6"])
                P.dve(lambda e: e.tensor_tensor(u_t[:, 0:T], u_t[:, 0:T], i_t[:, 0:T], ALU.mult), r=["ar6", "ar4"], w=["ar6"])
                P.dve(lambda e: e.tensor_tensor(u_t[:, 0:T], u_t[:, 0:T], xc[:, 0:T], ALU.mult), r=["ar6", "ar1"], w=["ar6"])
                hcol = (l * 2 + d) * 4 + c
                hdst, hk = (hfw, "ar7") if d == 0 else (hb, "ar2")
                order = range(4) if d == 0 else range(3, -1, -1)
                for si, s in enumerate(order):
                    ss_ = slice(s * 256, (s + 1) * 256)
                    if si == 0:
                        init = h0t[:, hcol:hcol + 1]
                        ik = "h0t"
                    else:
                        prev = (s * 256 - 1) if d == 0 else ((s + 1) * 256)
                        P.dve(lambda e, prev=prev, hdst=hdst: e.tensor_scalar(sm[:, 8:9], hdst[:, prev:prev + 1], flag[:, 0:1], None, op0=ALU.mult),
                              r=[hk, "flag"], w=["sm8"])
                        init = sm[:, 8:9]
                        ik = "sm8"
                    if d == 0:
                        P.dve(lambda e, ss_=ss_, init=init, hdst=hdst: e.tensor_tensor_scan(hdst[:, ss_], a_t[:, ss_], u_t[:, ss_], init, ALU.mult, ALU.add),
                              r=["ar5", "ar6", ik, hk], w=[hk])
                    else:
                        rs_ = slice((s + 1) * 256 - 1, s * 256 - 1 if s > 0 else None, -1)
                        P.dve(lambda e, rs_=rs_, init=init, hdst=hdst: e.tensor_tensor_scan(hdst[:, rs_], a_t[:, rs_], u_t[:, rs_], init, ALU.mult, ALU.add),
                              r=["ar5", "ar6", ik, hk], w=[hk])
                for s in range(4):
                    col = s * 256 + 255 if d == 0 else s * 256
                    P.dma(lambda e, c=c, s=s, d=d, col=col, hdst=hdst: e.dma_start(
                        out=o_lru[l, s, d, c * 128:(c + 1) * 128].rearrange("(p o) -> p o", o=1), in_=hdst[:, col:col + 1]), r=[hk])
            P.dve(lambda e: e.tensor_tensor(hfw[:, 0:T], hfw[:, 0:T], hb[:, 0:T], ALU.add), r=["ar7", "ar2"], w=["ar7"])
            w_g, wgk = load_cols("w_in", l, 1344 + c * 128, 128, D)
            zg = r_t

            def to_zg(blk, bk, bkk):
                P.act(lambda e: e.copy(zg[:, blk * 512:(blk + 1) * 512], bk[:]), r=[bkk], w=["ar3"])
            proj_fm(w_g, wgk, 0, 128, hT, "hT", 16, to_zg)
            gelu_fm(zg, "ar3", i_t, "ar4", a_t, "ar5")
            P.dve(lambda e, c=c: e.tensor_tensor(mixS[:, c, :], hfw[:, 0:T], a_t[:, 0:T], ALU.mult), r=["ar7", "ar5"], w=["mixS"])
        group_norm_inplace(mixS, "mixS", 0, 4, 44, 512, ar[8], "ar8")

    def conv_branch(l):
        for c in range(4):
            load_vec(48 + c, dr["cm_dw_b"][l, c * 128:(c + 1) * 128])
            load_vec(52 + c, dr["cm_ln_g"][l, c * 128:(c + 1) * 128])
            load_vec(56 + c, dr["cm_ln_b"][l, c * 128:(c + 1) * 128])
            load_vec(60 + c, dr["grp_g"][l, 1536 + c * 128:1536 + (c + 1) * 128])
        for c in range(4):
            P.dma(lambda e, c=c: e.dma_start(out=taps[:, c, :], in_=dr["cm_dw_w"][l, :, c * 128:(c + 1) * 128].rearrange("j p -> p j"),
                                             allow_slow_non_contiguous=True), r=["taps"], w=["taps"])
        HP = 286
        hp = ar[0]
        hp3 = hp[:, 0:4 * HP].rearrange("p (s n) -> p s n", s=4)
        sgm = ar[1]
        keep = [ar[4 + c] for c in range(4)]
        for c in range(4):
            w_b, wbk = load_cols("w_in", l, 2368 + c * 128, 128, D)

            def to_sig(blk, bk, bkk):
                P.act(lambda e: e.activation(out=sgm[:, blk * 512:(blk + 1) * 512], in_=bk[:], func=AF.Sigmoid), r=[bkk], w=["ar1"])
            proj_fm(w_b, wbk, 0, 128, hT, "hT", 16, to_sig)
            w_a, wak = load_cols("w_in", l, 1856 + c * 128, 128, D)

            def to_glu(blk, bk, bkk):
                P.dve(lambda e: e.tensor_tensor(hp3[:, 2 * blk:2 * blk + 2, 15:271], bk[:, :].rearrange("p (s n) -> p s n", s=2),
                                                sgm[:, blk * 512:(blk + 1) * 512].rearrange("p (s n) -> p s n", s=2), ALU.mult),
                      r=[bkk, "ar1"], w=["ar0"])
            proj_fm(w_a, wak, 0, 128, hT, "hT", 16, to_glu)
            P.dve(lambda e: e.memset(hp3[:, 0:1, 0:15], 0.0), r=["ar0"], w=["ar0"])
            P.dve(lambda e: e.memset(hp3[:, 3:4, 271:286], 0.0), r=["ar0"], w=["ar0"])
            P.dve(lambda e: e.tensor_scalar(hp3[:, 1:4, 0:15], hp3[:, 0:3, 256:271], flag[:, 0:1], None, op0=ALU.mult), r=["ar0", "flag"], w=["ar0"])
            P.dve(lambda e: e.tensor_scalar(hp3[:, 0:3, 271:286], hp3[:, 1:4, 15:30], flag[:, 0:1], None, op0=ALU.mult), r=["ar0", "flag"], w=["ar0"])
            acc3 = keep[c][:, 0:T].rearrange("p (s n) -> p s n", s=4)
            ck = ark[4 + c]
            P.dve(lambda e, c=c, acc3=acc3: e.tensor_scalar(acc3, hp3[:, :, 0:256], taps[:, c, 0:1], vcol(48 + c), op0=ALU.mult, op1=ALU.add),
                  r=["ar0", "taps", "vec%d" % (48 + c)], w=[ck])
            for j in range(1, 31):
                P.dve(lambda e, c=c, j=j, acc3=acc3: e.scalar_tensor_tensor(acc3, hp3[:, :, j:j + 256], taps[:, c, j:j + 1], acc3, ALU.mult, ALU.add),
                      r=["ar0", "taps", ck], w=[ck])
        mu, rs = ar[2], ar[3]
        for blk in range(2):
            bs = slice(blk * 512, (blk + 1) * 512)
            b1, b2 = bank[4], bank[5]
            for c in range(4):
                P.pe(lambda e, c=c, bs=bs: e.matmul(b1[:], onesf[:], keep[c][:, bs], start=(c == 0), stop=(c == 3)), r=[ark[4 + c], "onesf"], w=["bank4"])
            for c in range(4):
                sq = tmA[:, (c % 2) * 512:(c % 2) * 512 + 512]
                sqk = "tmA%d" % (c % 2)
                P.act(lambda e, c=c, bs=bs, sq=sq: e.activation(out=sq, in_=keep[c][:, bs], func=AF.Square), r=[ark[4 + c]], w=[sqk])
                P.pe(lambda e, c=c, sq=sq: e.matmul(b2[:], onesf[:], sq, start=(c == 0), stop=(c == 3)), r=[sqk, "onesf"], w=["bank5"])
            P.act(lambda e, bs=bs: e.activation(out=mu[:, bs], in_=b1[:], func=AF.Copy, scale=1.0 / 512), r=["bank4"], w=["ar2"])
            P.dve(lambda e, bs=bs: e.tensor_tensor(rs[:, bs], mu[:, bs], mu[:, bs], ALU.mult), r=["ar2"], w=["ar3"])
            P.dve(lambda e, bs=bs: e.scalar_tensor_tensor(rs[:, bs], b2[:], 1.0 / 512, rs[:, bs], ALU.mult, ALU.subtract), r=["bank5", "ar3"], w=["ar3"])
        P.act(lambda e: e.activation(out=rs[:, 0:T], in_=rs[:, 0:T], func=AF.Sqrt, bias=1e-5, scale=1.0), r=["ar3"], w=["ar3"])
        P.dve(lambda e: e.reciprocal(rs[:, 0:T], rs[:, 0:T]), r=["ar3"], w=["ar3"])
        for c in range(4):
            ck = ark[4 + c]
            acc = keep[c][:, 0:T]
            P.dve(lambda e, acc=acc: e.tensor_tensor(acc, acc, mu[:, 0:T], ALU.subtract), r=[ck, "ar2"], w=[ck])
            P.dve(lambda e, acc=acc: e.tensor_tensor(acc, acc, rs[:, 0:T], ALU.mult), r=[ck, "ar3"], w=[ck])
            P.act(lambda e, acc=acc, c=c: e.activation(out=mixS[:, 4 + c, :], in_=acc, func=AF.Silu, scale=vcol(52 + c), bias=vcol(56 + c)),
                  r=[ck, "vec%d" % (52 + c), "vec%d" % (56 + c)], w=["mixS"])
        group_norm_inplace(mixS, "mixS", 4, 4, 60, 512, ar[8], "ar8")

    def attention(l):
        for c in range(4):
            load_vec(64 + c, dr["mla_qa_g"][l, c * 128:(c + 1) * 128])
        for c in range(2):
            load_vec(68 + c, dr["mla_kva_g"][l, c * 128:(c + 1) * 128])
        load_vec(70, dr["mla_q_g"][l, 0:128])
        load_vec(71, dr["mla_q_g"][l, 128:192], 64)
        load_vec(73, dr["mla_k_g"][l, 0:128])
        load_vec(74, dr["mla_k_g"][l, 128:192], 64)
        for c in range(8):
            load_vec(76 + c, dr["grp_g"][l, c * 128:(c + 1) * 128])
        zq = [ar[i] for i in range(4)]
        for c in range(4):
            w_q, wqk = load_cols("w_in", l, c * 128, 128, D)

            def to_zq(blk, bk, bkk, c=c):
                P.act(lambda e: e.copy(zq[c][:, blk * 512:(blk + 1) * 512], bk[:]), r=[bkk], w=[ark[c]])
            proj_fm(w_q, wqk, 0, 128, hT, "hT", 16, to_zq)
        rs = ar[8]
        rstd_rows([(lambda blk, c=c: zq[c][:, blk * 512:(blk + 1) * 512], ark[c], 128) for c in range(4)], rs, "ar8", 512, EPS)
        for c in range(4):
            P.dve(lambda e, c=c: e.scalar_tensor_tensor(qaT[:, c, :], zq[c][:, 0:T], vcol(64 + c), rs[:, 0:T], ALU.mult, ALU.mult),
                  r=[ark[c], "ar8", "vec%d" % (64 + c)], w=["qaT"])
        zkv = [ar[4], ar[5]]
        for c in range(2):
            w_kv, wkvk = load_cols("w_in", l, 512 + c * 128, 128, D)

            def to_zkv(blk, bk, bkk, c=c):
                P.act(lambda e: e.copy(zkv[c][:, blk * 512:(blk + 1) * 512], bk[:]), r=[bkk], w=[ark[4 + c]])
            proj_fm(w_kv, wkvk, 0, 128, hT, "hT", 16, to_zkv)
        rstd_rows([(lambda blk, c=c: zkv[c][:, blk * 512:(blk + 1) * 512], ark[4 + c], 128) for c in range(2)], rs, "ar8", 256, EPS)
        for c in range(2):
            P.dve(lambda e, c=c: e.scalar_tensor_tensor(zkv[c][:, 0:T], zkv[c][:, 0:T], vcol(68 + c), rs[:, 0:T], ALU.mult, ALU.mult),
                  r=[ark[4 + c], "ar8", "vec%d" % (68 + c)], w=[ark[4 + c]])
            P.act(lambda e, c=c: e.copy(ckvT[:, c, 512:NK], zkv[c][:, 0:T]), r=[ark[4 + c]], w=["ckvT"])
        for t in range(NT):
            for c in range(2):
                bk = bank[2 + c]
                P.pe(lambda e, bk=bk, c=c, t=t: e.transpose(bk[:, 0:128], zkv[c][:, t * 128:(t + 1) * 128], identf[:]), r=[ark[4 + c], "identf"], w=["bank%d" % (2 + c)])
                P.act(lambda e, bk=bk, c=c, t=t: e.copy(kst[:, t, c * 128:(c + 1) * 128], bk[:, 0:128]), r=["bank%d" % (2 + c)], w=["kst"])
        kr = ar[7]
        w_kr, wkrk = load_cols("w_in", l, 768, 64, D)

        def to_kr(blk, bk, bkk):
            P.act(lambda e: e.copy(kr[0:64, 512 + blk * 512:512 + (blk + 1) * 512], bk[0:64, :]), r=[bkk], w=["ar7"])
        proj_fm(w_kr, wkrk, 0, 64, hT, "hT", 16, to_kr)
        for t in range(NT):
            bk = bank[2 + t % 2]
            bkk = "bank%d" % (2 + t % 2)
            P.pe(lambda e, bk=bk, t=t: e.transpose(bk[:, 0:64], kr[0:64, 512 + t * 128:512 + (t + 1) * 128], identf[0:64, 0:64]),
                 r=["ar7", "identf"], w=[bkk])
            P.act(lambda e, bk=bk, t=t: e.copy(kst[:, t, 256:320], bk[:, 0:64]), r=[bkk], w=["kst"])
        P.dma(lambda e: e.dma_start(out=o_ckv[l].rearrange("(t p) d -> p t d", p=128), in_=kst[:, :, 0:256]), r=["kst"])
        P.dma(lambda e: e.dma_start(out=o_kr[l].rearrange("(t p) d -> p t d", p=128), in_=kst[:, :, 256:320]), r=["kst"])
        P.dma(lambda e: e.dma_start(out=kst[:, 0:4, 0:256], in_=dr["cache_ckv"][l].rearrange("(t p) d -> p t d", p=128)), r=["kst"], w=["kst"])
        P.dma(lambda e: e.dma_start(out=kst[:, 0:4, 256:320], in_=dr["cache_krope"][l].rearrange("(t p) d -> p t d", p=128)), r=["kst"], w=["kst"])
        for t in range(4):
            for c in range(2):
                bk = bank[2 + c]
                P.pe(lambda e, bk=bk, c=c, t=t: e.transpose(bk[:, 0:128], kst[:, t, c * 128:(c + 1) * 128], identf[:]), r=["kst", "identf"], w=["bank%d" % (2 + c)])
                P.act(lambda e, bk=bk, c=c, t=t: e.copy(ckvT[:, c, t * 128:(t + 1) * 128], bk[:, 0:128]), r=["bank%d" % (2 + c)], w=["ckvT"])
            bk = bank[4]
            P.pe(lambda e, t=t: e.transpose(bk[0:64, 0:128], kst[:, t, 256:320], identf[:]), r=["kst", "identf"], w=["bank4"])
            P.act(lambda e, t=t: e.copy(kr[0:64, t * 128:(t + 1) * 128], bk[0:64, 0:128]), r=["bank4"], w=["ar7"])
        sskr = ar[6]
        for blk in range(3):
            bs = slice(blk * 512, (blk + 1) * 512)
            P.act(lambda e, bs=bs: e.activation(out=tmA[0:64, 0:512], in_=kr[0:64, bs], func=AF.Square), r=["ar7"], w=["tmA0"])
            P.pe(lambda e: e.matmul(bank[4][:], onesf[0:64, :], tmA[0:64, 0:512], start=True, stop=True), r=["tmA0", "onesf"], w=["bank4"])
            P.act(lambda e, bs=bs: e.copy(sskr[:, bs], bank[4][:]), r=["bank4"], w=["ar6"])
        kk, kk2, rk, qr, Rs, t2 = ar[0], ar[1], ar[2], ar[3], ar[4], ar[5]
        for h in range(8):
            w_qb, wqbk = load_cols("mla_w_qb", l, h * 192, 192, 512)
            w_kvb, wkvbk = load_cols("mla_w_kvb", l, h * 256, 256, 256)
            for blk in range(3):
                bs = slice(blk * 512, (blk + 1) * 512)
                for k in range(2):
                    P.pe(lambda e, k=k, bs=bs, w_kvb=w_kvb: e.matmul(bank[2][:], w_kvb[:, k, 0:128], ckvT[:, k, bs], start=(k == 0), stop=(k == 1)), r=[wkvbk, "ckvT"], w=["bank2"])
                P.act(lambda e, bs=bs: e.copy(kk[:, bs], bank[2][:]), r=["bank2"], w=["ar0"])
                P.act(lambda e, bs=bs: e.activation(out=kk2[:, bs], in_=bank[2][:], func=AF.Square), r=["bank2"], w=["ar1"])
                P.pe(lambda e, bs=bs: e.matmul(bank[3][:], onesf[:], kk2[:, bs], start=True, stop=True), r=["ar1", "onesf"], w=["bank3"])
                P.dve(lambda e, bs=bs: e.tensor_tensor(rk[:, bs], bank[3][:], sskr[:, bs], ALU.add), r=["bank3", "ar6"], w=["ar2"])
            P.act(lambda e: e.activation(out=rk[:], in_=rk[:], func=AF.Sqrt, bias=EPS, scale=1.0 / 192), r=["ar2"], w=["ar2"])
            P.dve(lambda e: e.reciprocal(rk[:], rk[:]), r=["ar2"], w=["ar2"])
            P.dve(lambda e: e.scalar_tensor_tensor(knT[:], kk[:], vcol(73), rk[:], ALU.mult, ALU.mult), r=["ar0", "ar2", "vec73"], w=["knT"])
            P.dve(lambda e: e.scalar_tensor_tensor(kk2[0:64, :], kr[0:64, :], vcol(74, 64), rk[0:64, :], ALU.mult, ALU.mult), r=["ar7", "ar2", "vec74"], w=["ar1"])
            rope64(kk2, "ar1", Rs, "ar4", t2, "ar5", CC, SS, 0, NK, krT, "krT")
            for kt in range(NKT):
                bi = kt % 2 + 2
                bk = bank[bi]
                for k in range(2):
                    P.pe(lambda e, k=k, kt=kt, bk=bk, w_kvb=w_kvb: e.matmul(bk[:, 0:128], ckvT[:, k, kt * 128:(kt + 1) * 128], w_kvb[:, k, 128:256],
                                                              start=(k == 0), stop=(k == 1)), r=["ckvT", wkvbk], w=["bank%d" % bi])
                P.act(lambda e, kt=kt, bk=bk: e.copy(vh[:, kt, :], bk[:, 0:128]), r=["bank%d" % bi], w=["vh"])
            qq, qq2, rq = kk, kk2, rk
            for blk in range(2):
                bs = slice(blk * 512, (blk + 1) * 512)
                for k in range(4):
                    P.pe(lambda e, k=k, bs=bs, w_qb=w_qb: e.matmul(bank[2][:], w_qb[:, k, 0:128], qaT[:, k, bs], start=(k == 0), stop=(k == 3)), r=[wqbk, "qaT"], w=["bank2"])
                P.act(lambda e, bs=bs: e.copy(qq[:, bs], bank[2][:]), r=["bank2"], w=["ar0"])
                P.act(lambda e, bs=bs: e.activation(out=qq2[:, bs], in_=bank[2][:], func=AF.Square), r=["bank2"], w=["ar1"])
                for k in range(4):
                    P.pe(lambda e, k=k, bs=bs, w_qb=w_qb: e.matmul(bank[3][0:64, :], w_qb[:, k, 128:192], qaT[:, k, bs], start=(k == 0), stop=(k == 3)), r=[wqbk, "qaT"], w=["bank3"])
                P.act(lambda e, bs=bs: e.copy(qr[0:64, bs], bank[3][0:64, :]), r=["bank3"], w=["ar3"])
                P.act(lambda e, bs=bs: e.activation(out=t2[0:64, bs], in_=bank[3][0:64, :], func=AF.Square), r=["bank3"], w=["ar5"])
                P.pe(lambda e, bs=bs: e.matmul(bank[4][:], onesf[:], qq2[:, bs], start=True, stop=False), r=["ar1", "onesf"], w=["bank4"])
                P.pe(lambda e, bs=bs: e.matmul(bank[4][:], onesf[0:64, :], t2[0:64, bs], start=False, stop=True), r=["ar5", "onesf"], w=["bank4"])
                P.act(lambda e, bs=bs: e.activation(out=rq[:, bs], in_=bank[4][:], func=AF.Sqrt, bias=EPS, scale=1.0 / 192), r=["bank4"], w=["ar2"])
            P.dve(lambda e: e.reciprocal(rq[:, 0:T], rq[:, 0:T]), r=["ar2"], w=["ar2"])
            P.dve(lambda e: e.scalar_tensor_tensor(qnT[:], qq[:, 0:T], vcol(70), rq[:, 0:T], ALU.mult, ALU.mult), r=["ar0", "ar2", "vec70"], w=["qnT"])
            P.dve(lambda e: e.scalar_tensor_tensor(qr[0:64, 0:T], qr[0:64, 0:T], vcol(71, 64), rq[0:64, 0:T], ALU.mult, ALU.mult), r=["ar3", "ar2", "vec71"], w=["ar3"])
            rope64(qr, "ar3", Rs, "ar4", t2, "ar5", CC, SS, 512, T, qrT, "qrT")
            for blk in range(2):
                bs = slice(blk * 512, (blk + 1) * 512)
                for kt in range(NKT):
                    ks = slice(kt * 128, (kt + 1) * 128)
                    bi = kt % 2
                    bk = bank[bi]
                    bkk = "bank%d" % bi
                    P.pe(lambda e, ks=ks, bs=bs, bk=bk: e.matmul(bk[:], knT[:, ks], qnT[:, bs], start=True, stop=False), r=["knT", "qnT"], w=[bkk])
                    P.pe(lambda e, ks=ks, bs=bs, bk=bk: e.matmul(bk[:], krT[:, ks], qrT[:, bs], start=False, stop=True), r=["krT", "qrT"], w=[bkk])
                    pt = pT[bi]
                    ptk = "pT%d" % bi
                    for qb in range(2):
                        mcol = kt * 4 + blk * 2 + qb
                        P.act(lambda e, bk=bk, pt=pt, qb=qb, mcol=mcol: e.activation(out=pt[:, qb * 256:(qb + 1) * 256], in_=bk[:, qb * 256:(qb + 1) * 256],
                                                                                     func=AF.Exp, scale=192.0 ** -0.5, bias=maskb[:, mcol:mcol + 1]),
                              r=[bkk, "maskb"], w=[ptk])
                    P.pe(lambda e, kt=kt, pt=pt: e.matmul(bank[2][:], vh[:, kt, :], pt[:], start=(kt == 0), stop=(kt == NKT - 1)), r=["vh", ptk], w=["bank2"])
                    P.pe(lambda e, kt=kt, pt=pt: e.matmul(bank[3][:], ones[:], pt[:], start=(kt == 0), stop=(kt == NKT - 1)), r=["ones", ptk], w=["bank3"])
                P.dve(lambda e: e.reciprocal(tmA[:, 0:512], bank[3][:]), r=["bank3"], w=["tmA0"])
                P.dve(lambda e, bs=bs, h=h: e.tensor_tensor(hT[:, h, bs], bank[2][:], tmA[:, 0:512], ALU.mult), r=["bank2", "tmA0"], w=["hT"])
        group_norm_inplace(hT, "hT", 0, 8, 76, 1024, ar[8], "ar8")

    def rope64(R, Rk, Rs, Rsk, t2, t2k, CCt, SSt, c0, n, out, outk):
        P.act(lambda e: e.copy(Rs[0:32, 0:n], R[32:64, 0:n]), r=[Rk], w=[Rsk])
        P.act(lambda e: e.copy(Rs[32:64, 0:n], R[0:32, 0:n]), r=[Rk, Rsk], w=[Rsk])
        P.dve(lambda e: e.tensor_tensor(Rs[0:64, 0:n], Rs[0:64, 0:n], SSt[:, c0:c0 + n], ALU.mult), r=[Rsk, "SS"], w=[Rsk])
        P.dve(lambda e: e.tensor_tensor(t2[0:64, 0:n], R[0:64, 0:n], CCt[:, c0:c0 + n], ALU.mult), r=[Rk, "CC"], w=[t2k])
        P.dve(lambda e: e.tensor_tensor(out[:, 0:n], t2[0:64, 0:n], Rs[0:64, 0:n], ALU.add), r=[t2k, Rsk], w=[outk])

    def out_proj(l):
        if o_dbg is not None and l == 0:
            for k in range(16):
                src = hT[:, k, :] if k < 8 else mixS[:, k - 8, :]
                P.dma(lambda e, k=k, src=src: e.dma_start(out=o_dbg[k, :, 0:T], in_=src), r=["hT", "mixS"], eng="pool")
        def mixchunk(k, t):
            if k < 8:
                return hT[:, k, t * 128:(t + 1) * 128]
            return mixS[:, k - 8, t * 128:(t + 1) * 128]
        for cb in range(8):
            w_t, wk = load_cols("w_out", l, cb * 256, 256, D)
            cs = slice(cb * 256, (cb + 1) * 256)
            for t in range(NT):
                bi = t % 2
                bk = bank[bi]
                bkk = "bank%d" % bi
                xs = ar[t % 2]
                xsk = ark[t % 2]
                P.dma(lambda e, t=t, cs=cs, xs=xs: e.dma_start(out=xs[:, 0:256], in_=xd[t][:, cs]), r=["xd%d" % t], w=[xsk])
                for k in range(16):
                    P.pe(lambda e, k=k, t=t, bk=bk, w_t=w_t: e.matmul(bk[:, 0:256], mixchunk(k, t), w_t[:, k, :], start=(k == 0), stop=(k == 15)),
                         r=["hT", "mixS", wk], w=[bkk])
                P.dve(lambda e, bk=bk, cs=cs, xs=xs: e.tensor_tensor(xs[:, 256:512], bk[:, 0:256], modG[:, cs], ALU.mult), r=[bkk, "modG", xsk], w=[xsk])
                P.dve(lambda e, xs=xs: e.tensor_tensor(xs[:, 0:256], xs[:, 0:256], xs[:, 256:512], ALU.add), r=[xsk], w=[xsk])
                P.dma(lambda e, t=t, cs=cs, xs=xs: e.dma_start(out=xd[t][:, cs], in_=xs[:, 0:256]), r=[xsk], w=["xd%d" % t])

    def peer(l):
        hfT = mixS[:, 0:2, :].rearrange("p a (b n) -> p (a b) n", b=8)
        for j in range(16):
            h_, p_ = j // 2, j % 2
            P.dma(lambda e, h_=h_, p_=p_: e.dma_start(out=bwf[:], in_=dr["peer_keys"][l, h_, p_, :, :]), w=["bwf"])
            P.dve(lambda e: e.tensor_copy(bwt[0][:], bwf[:]), r=["bwf"], w=["bwt0"])
            bb = bankb[j % 2]
            bbk = "bankb%d" % (j % 2)
            P.pe(lambda e, bb=bb: e.transpose(bb[:, 0:128], bwt[0][:], ident[:]), r=["bwt0", "ident"], w=[bbk])
            P.act(lambda e, bb=bb, j=j: e.copy(keysT[:, j, :], bb[:, 0:128]), r=[bbk], w=["keysT"])
        sc = [ar[2], ar[3]]
        s1b, cand, candb, eq = ar[4], ar[5], ar[6], accp
        for t in range(NT):
            norm_tile(t)
            transpose_tile_to(lambda half: hfT[:, half * 8:(half + 1) * 8, :], "mixS")
            for jb in range(8):
                w_t, wk = load_cols("peer_w_q", l, jb * 256, 256, D)
                for jj in range(2):
                    j = jb * 2 + jj
                    bk = bank[2 + jj]
                    bkk = "bank%d" % (2 + jj)
                    for k in range(16):
                        P.pe(lambda e, k=k, jj=jj, bk=bk, w_t=w_t: e.matmul(bk[:, 0:128], w_t[:, k, jj * 128:(jj + 1) * 128], hfT[:, k, :], start=(k == 0), stop=(k == 15)),
                             r=[wk, "mixS"], w=[bkk])
                    P.act(lambda e, jj=jj, bk=bk: e.copy(qTj[:, jj, :], bk[:, 0:128]), r=[bkk], w=["qTj%d" % jj])
                    bs_ = bank[4 + jj]
                    bsk = "bank%d" % (4 + jj)
                    P.pe(lambda e, jj=jj, j=j, bs_=bs_: e.matmul(bs_[:, 0:128], qTj[:, jj, :], keysT[:, j, :], start=True, stop=True), r=["qTj%d" % jj, "keysT"], w=[bsk])
                    P.dve(lambda e, j=j, bs_=bs_: e.tensor_copy(sc[j // 8][:, (j % 8) * 128:(j % 8 + 1) * 128], bs_[:, 0:128]), r=[bsk], w=[ark[2 + j // 8]])
            for h in range(8):
                for p_ in range(2):
                    j = h * 2 + p_
                    s_ap = sc[j // 8][:, (j % 8) * 128:(j % 8 + 1) * 128]
                    sk = ark[2 + j // 8]
                    m = tk[:, h, p_ * 16:(p_ + 1) * 16]
                    ii = tki[:, h, p_ * 16:(p_ + 1) * 16]
                    P.dve(lambda e, m=m, s_ap=s_ap: e.max(out=m[:, 0:8], in_=s_ap), r=[sk], w=["tk"])
                    P.dve(lambda e, m=m, ii=ii, s_ap=s_ap: e.max_index(out=ii[:, 0:8], in_max=m[:, 0:8], in_values=s_ap), r=[sk, "tk"], w=["tki"])
                    P.dve(lambda e, m=m, s_ap=s_ap: e.match_replace(out=s1b[:, 0:128], in_to_replace=m[:, 0:8], in_values=s_ap, imm_value=-1e30), r=[sk, "tk"], w=["ar4"])
                    P.dve(lambda e, m=m: e.max(out=m[:, 8:16], in_=s1b[:, 0:128]), r=["ar4"], w=["tk"])
                    P.dve(lambda e, m=m, ii=ii: e.max_index(out=ii[:, 8:16], in_max=m[:, 8:16], in_values=s1b[:, 0:128]), r=["ar4", "tk"], w=["tki"])
                c3 = cand[:, 0:256].rearrange("p (a b) -> p a b", a=16)
                P.dve(lambda e, h=h, c3=c3: e.tensor_tensor(c3, tk[:, h, 0:16].unsqueeze(2).to_broadcast([128, 16, 16]),
                                                          tk[:, h, 16:32].unsqueeze(1).to_broadcast([128, 16, 16]), ALU.add), r=["tk"], w=["ar5"])
                m = tk[:, h, 32:48]
                ii = tki[:, h, 32:48]
                P.dve(lambda e, m=m: e.max(out=m[:, 0:8], in_=cand[:, 0:256]), r=["ar5"], w=["tk"])
                P.dve(lambda e, m=m, ii=ii: e.max_index(out=ii[:, 0:8], in_max=m[:, 0:8], in_values=cand[:, 0:256]), r=["ar5", "tk"], w=["tki"])
                P.dve(lambda e, m=m: e.match_replace(out=candb[:, 0:256], in_to_replace=m[:, 0:8], in_values=cand[:, 0:256], imm_value=-1e30), r=["ar5", "tk"], w=["ar6"])
                P.dve(lambda e, m=m: e.max(out=m[:, 8:16], in_=candb[:, 0:256]), r=["ar6"], w=["tk"])
                P.dve(lambda e, m=m, ii=ii: e.max_index(out=ii[:, 8:16], in_max=m[:, 8:16], in_values=candb[:, 0:256]), r=["ar6", "tk"], w=["tki"])
            posu = tki[:, :, 32:48]
            k1f, k2f, idxf, gate, actv, wgt = (pk[:, i, :] for i in range(6))
            pu3 = pki[:, :].bitcast(U32).rearrange("p (h r) -> p h r", h=8)
            a3 = actv.rearrange("p (h r) -> p h r", h=8)
            w3 = wgt.rearrange("p (h r) -> p h r", h=8)
            eq4 = eq.rearrange("p (h r q) -> p h r q", h=8, r=16)
            for which, (sop, samt, src_i, dstf) in enumerate(((ALU.logical_shift_right, 4, 0, k1f), (ALU.bitwise_and, 15, 16, k2f))):
                P.dve(lambda e, sop=sop, samt=samt: e.tensor_single_scalar(pu3, posu, samt, op=sop), r=["tki"], w=["pki"])
                P.dve(lambda e: e.tensor_copy(a3, pu3), r=["pki"], w=["pk4"])
                P.dve(lambda e, src_i=src_i: e.tensor_copy(w3, tki[:, :, src_i:src_i + 16]), r=["tki"], w=["pk5"])
                P.dve(lambda e: e.tensor_tensor(eq4, a3.unsqueeze(3).to_broadcast([128, 8, 16, 16]),
                                                iot[:, :].unsqueeze(1).unsqueeze(1).to_broadcast([128, 8, 16, 16]), ALU.is_equal), r=["pk4", "iot"], w=["accp"])
                P.dve(lambda e: e.tensor_tensor(eq4, eq4, w3.unsqueeze(2).to_broadcast([128, 8, 16, 16]), ALU.mult), r=["accp", "pk5"], w=["accp"])
                P.dve(lambda e, dstf=dstf: e.tensor_reduce(out=dstf.rearrange("p (h r) -> p h r", h=8), in_=eq4, axis=AX.X, op=ALU.add), r=["accp"], w=["pk%d" % which])
            P.dve(lambda e: e.scalar_tensor_tensor(idxf, k1f, 128.0, k2f, ALU.mult, ALU.add), r=["pk0", "pk1"], w=["pk2"])
            P.dve(lambda e: e.tensor_scalar(idxf, idxf, float(l * 16384), None, op0=ALU.add), r=["pk2"], w=["pk2"])
            P.dve(lambda e: e.tensor_copy(pki[:, :], idxf), r=["pk2"], w=["pki"])
            c16 = tk[:, :, 32:48]
            g3 = gate.rearrange("p (h r) -> p h r", h=8)
            P.dve(lambda e: e.tensor_tensor(g3, c16, tk[:, :, 32:33].to_broadcast([128, 8, 16]), ALU.subtract), r=["tk"], w=["pk3"])
            P.act(lambda e: e.activation(out=gate, in_=gate, func=AF.Exp), r=["pk3"], w=["pk3"])
            P.dve(lambda e: e.tensor_reduce(out=sm[:, 0:8], in_=g3, axis=AX.X, op=ALU.add), r=["pk3"], w=["smg"])
            P.dve(lambda e: e.reciprocal(sm[:, 0:8], sm[:, 0:8]), r=["smg"], w=["smg"])
            P.dve(lambda e: e.tensor_tensor(g3, g3, sm[:, 0:8].unsqueeze(2).to_broadcast([128, 8, 16]), ALU.mult), r=["pk3", "smg"], w=["pk3"])
            for j in range(128):
                g_ = ugb[j % 6]
                gk = "ugb%d" % (j % 6)
                P.dma(lambda e, j=j, g_=g_: e.indirect_dma_start(out=g_, out_offset=None, in_=dr["peer_u"],
                                                                 in_offset=bass.IndirectOffsetOnAxis(ap=pki[:, j:j + 1], axis=0)),
                      r=["pki"], w=[gk], eng="pool")
                P.dve(lambda e, j=j, g_=g_: e.scalar_tensor_tensor(junkb, g_, 1.0, tmB[:], ALU.mult, ALU.mult, accum_out=actv[:, j:j + 1]),
                      r=[gk, "tmB"], w=["junkb", "pk4"])
            P.dve(lambda e: e.tensor_tensor(wgt, actv, actv, ALU.mult), r=["pk4"], w=["pk5"])
            P.dve(lambda e: e.tensor_scalar(wgt, wgt, 0.044715, 1.0, op0=ALU.mult, op1=ALU.add), r=["pk5"], w=["pk5"])
            P.dve(lambda e: e.tensor_tensor(wgt, wgt, actv, ALU.mult), r=["pk5", "pk4"], w=["pk5"])
            P.act(lambda e: e.activation(out=wgt, in_=wgt, func=AF.Sigmoid, scale=GELU_C), r=["pk5"], w=["pk5"])
            P.dve(lambda e: e.tensor_tensor(wgt, wgt, actv, ALU.mult), r=["pk5", "pk4"], w=["pk5"])
            P.dve(lambda e: e.tensor_tensor(wgt, wgt, gate, ALU.mult), r=["pk5", "pk3"], w=["pk5"])
            for j in range(128):
                g_ = vgb[j % 6]
                gk = "ugb%d" % (j % 6)
                dgi = j % 4
                P.dma(lambda e, j=j, g_=g_: e.indirect_dma_start(out=g_, out_offset=None, in_=dr["peer_v"],
                                                                 in_offset=bass.IndirectOffsetOnAxis(ap=pki[:, j:j + 1], axis=0)),
                      r=["pki"], w=[gk], eng="pool")
                P.act(lambda e, j=j, dgi=dgi: e.activation(out=dgr[dgi][:], in_=identf[:], func=AF.Copy, scale=wgt[:, j:j + 1]), r=["identf", "pk5"], w=["dgr%d" % dgi])
                for cb in range(4):
                    P.pe(lambda e, j=j, cb=cb, g_=g_, dgi=dgi: e.matmul(bank[2 + cb][:], dgr[dgi][:], g_[:, cb * 512:(cb + 1) * 512], start=(j == 0), stop=(j == 127)),
                         r=["dgr%d" % dgi, gk], w=["bank%d" % (2 + cb)])
            for hf in range(2):
                hs = slice(hf * 1024, (hf + 1) * 1024)
                P.dma(lambda e, hf=hf, hs=hs, t=t: e.dma_start(out=ar[hf][:, 0:1024], in_=xd[t][:, hs]), r=["xd%d" % t], w=[ark[hf]])
                for cbh in range(2):
                    cb = hf * 2 + cbh
                    cs = slice(cb * 512, (cb + 1) * 512)
                    P.dve(lambda e, cb=cb, cs=cs: e.tensor_tensor(tmA[:, cs], bank[2 + cb][:], modG[:, cs], ALU.mult), r=["bank%d" % (2 + cb), "modG"], w=["tmA"])
                    P.dve(lambda e, hf=hf, cbh=cbh, cs=cs: e.tensor_tensor(ar[hf][:, cbh * 512:(cbh + 1) * 512], ar[hf][:, cbh * 512:(cbh + 1) * 512], tmA[:, cs], ALU.add),
                          r=[ark[hf], "tmA"], w=[ark[hf]])
                P.dma(lambda e, hf=hf, hs=hs, t=t: e.dma_start(out=xd[t][:, hs], in_=ar[hf][:, 0:1024]), r=[ark[hf]], w=["xd%d" % t])

    def do_barrier():
        P.barrier({"act": lambda e: e.copy(bsc[:, 0:1], bsc[:, 1:2]), "dve": lambda e: e.memset(bsc[:, 2:3], 0.0),
                   "pool": lambda e: e.memset(bsc[:, 3:4], 0.0), "sp": lambda e: e.nop()})

    for t_ in range(NT):
        P.alias("xd%d" % t_, ["xd%d_%d" % (t_, cb_) for cb_ in range(4)])

    def peer_dense(l):
        do_barrier()
        ada_mod(l, 1)
        norm_to_hT()
        for j in range(16):
            h_, p_ = j // 2, j % 2
            P.dma(lambda e, h_=h_, p_=p_: e.dma_start(out=bwf[:], in_=dr["peer_keys"][l, h_, p_, :, :]), w=["bwf"])
            P.dve(lambda e: e.tensor_copy(bwt[0][:], bwf[:]), r=["bwf"], w=["bwt0"])
            bb = bankb[j % 2]
            bbk = "bankb%d" % (j % 2)
            P.pe(lambda e, bb=bb: e.transpose(bb[:, 0:128], bwt[0][:], ident[:]), r=["bwt0", "ident"], w=[bbk])
            P.act(lambda e, bb=bb, j=j: e.copy(keysT[:, j, :], bb[:, 0:128]), r=[bbk], w=["keysT"])
        qT1 = mixS
        qT2 = modAB[:, :].bitcast(BF16).rearrange("p (c n) -> p c n", c=8)
        P.alias("qT2", ["modA", "modB", "kst"])
        for jb in range(8):
            w_t, wk = load_cols("peer_w_q", l, jb * 256, 256, D)

            def to_q1(blk, bk, bkk, jb=jb):
                P.act(lambda e: e.copy(qT1[:, jb, blk * 512:(blk + 1) * 512], bk[:]), r=[bkk], w=["mixS"])

            def to_q2(blk, bk, bkk, jb=jb):
                P.dve(lambda e: e.tensor_copy(qT2[:, jb, blk * 512:(blk + 1) * 512], bk[:]), r=[bkk], w=["qT2"])
            proj_fm(w_t, wk, 0, 128, hT, "hT", 16, to_q1)
            proj_fm(w_t, wk, 128, 128, hT, "hT", 16, to_q2)
        sc = [ar[2], ar[3]]
        s1b, cand, candb = ar[4], ar[5], ar[6]
        for t in range(NT):
            ts_ = slice(t * 128, (t + 1) * 128)
            for j in range(16):
                h_, p_ = j // 2, j % 2
                qsrc, qk = (qT1, "mixS") if p_ == 0 else (qT2, "qT2")
                bi = 2 + j // 4
                P.pe(lambda e, j=j, h_=h_, qsrc=qsrc, bi=bi, ts_=ts_: e.matmul(bank[bi][:, (j % 4) * 128:(j % 4 + 1) * 128], qsrc[:, h_, ts_], keysT[:, j, :],
                                                                          start=True, stop=True), r=[qk, "keysT"], w=["bank%d" % bi])
            for q4 in range(4):
                dst = sc[q4 // 2][:, (q4 % 2) * 512:(q4 % 2 + 1) * 512]
                if q4 % 2 == 0:
                    P.act(lambda e, dst=dst, q4=q4: e.copy(dst, bank[2 + q4][:]), r=["bank%d" % (2 + q4)], w=[ark[2 + q4 // 2]])
                else:
                    P.dve(lambda e, dst=dst, q4=q4: e.tensor_copy(dst, bank[2 + q4][:]), r=["bank%d" % (2 + q4)], w=[ark[2 + q4 // 2]])
            for h in range(8):
                for p_ in range(2):
                    j = h * 2 + p_
                    s_ap = sc[j // 8][:, (j % 8) * 128:(j % 8 + 1) * 128]
                    sk = ark[2 + j // 8]
                    m = tk[:, h, p_ * 16:(p_ + 1) * 16]
                    P.dve(lambda e, m=m, s_ap=s_ap: e.max(out=m[:, 0:8], in_=s_ap), r=[sk], w=["tk"])
                    P.dve(lambda e, m=m, s_ap=s_ap: e.match_replace(out=s1b[:, 0:128], in_to_replace=m[:, 0:8], in_values=s_ap, imm_value=-1e30), r=[sk, "tk"], w=["ar4"])
                    P.dve(lambda e, m=m: e.max(out=m[:, 8:16], in_=s1b[:, 0:128]), r=["ar4"], w=["tk"])
                c3 = cand[:, 0:256].rearrange("p (a b) -> p a b", a=16)
                P.dve(lambda e, h=h, c3=c3: e.tensor_tensor(c3, tk[:, h, 0:16].unsqueeze(2).to_broadcast([128, 16, 16]),
                                                          tk[:, h, 16:32].unsqueeze(1).to_broadcast([128, 16, 16]), ALU.add), r=["tk"], w=["ar5"])
                m = tk[:, h, 32:48]
                P.dve(lambda e, m=m: e.max(out=m[:, 0:8], in_=cand[:, 0:256]), r=["ar5"], w=["tk"])
                P.dve(lambda e, m=m: e.match_replace(out=candb[:, 0:256], in_to_replace=m[:, 0:8], in_values=cand[:, 0:256], imm_value=-1e30), r=["ar5", "tk"], w=["ar6"])
                P.dve(lambda e, m=m: e.max(out=m[:, 8:16], in_=candb[:, 0:256]), r=["ar6"], w=["tk"])
            P.dve(lambda e, t=t: e.tensor_copy(thrT[:, t, :], tk[:, :, 47]), r=["tk"], w=["thrT"])
            g3 = s1b[:, 0:128].rearrange("p (h r) -> p h r", h=8)
            P.dve(lambda e: e.tensor_tensor(g3, tk[:, :, 32:48], tk[:, :, 32:33].to_broadcast([128, 8, 16]), ALU.subtract), r=["tk"], w=["ar4"])
            P.act(lambda e: e.activation(out=s1b[:, 0:128], in_=s1b[:, 0:128], func=AF.Exp), r=["ar4"], w=["ar4"])
            P.dve(lambda e: e.tensor_reduce(out=sm[:, 0:8], in_=g3, axis=AX.X, op=ALU.add), r=["ar4"], w=["smg"])
            P.act(lambda e: e.activation(out=sm[:, 0:8], in_=sm[:, 0:8], func=AF.Ln), r=["smg"], w=["smg"])
            P.dve(lambda e, t=t: e.scalar_tensor_tensor(biasT[:, t, :], sm[:, 0:8], -1.0, tk[:, :, 32], ALU.mult, ALU.subtract), r=["smg", "tk"], w=["biasT"])
        if peer_stop == 2:
            return
        do_barrier()
        ur = [wb[i // 2][:, :, :].rearrange("p k n -> p (k n)")[:, (i % 2) * 2048:(i % 2 + 1) * 2048] for i in range(4)]
        tmAb = tmA[:, :].bitcast(BF16)
        uT = [tmAb[:, i * 2048:(i + 1) * 2048].rearrange("p (k n) -> p k n", k=16) for i in range(2)]
        vs = [ar[i][:, 0:1024].bitcast(BF16) for i in range(4)]
        As = [ar[6 + i // 2][:, 0:1024].bitcast(BF16)[:, (i % 2) * 1024:(i % 2 + 1) * 1024] for i in range(4)]
        ost = [ar[6][:, 1024:1536], ar[7][:, 1024:1536]]
        tA = ar[8][:, 0:1024]
        gel = ar[8][:, 1024:1536].bitcast(BF16)
        ar4b = ar[4][:, :].bitcast(BF16)
        WT = [[qaT[:, 2, :], qaT[:, 3, :], ckvT[:, 0, 0:1024], ckvT[:, 1, 0:1024]],
              [ar4b[:, 0:1024], ar4b[:, 1024:2048], ar4b[:, 2048:3072], ar[5][:, 0:512].bitcast(BF16)]]
        s2sb = tmB[:, :].bitcast(F32).rearrange("p (h n) -> p h n", h=8)
        sums = [vh[:, :, :].rearrange("p a b -> p (a b)").bitcast(F32)[:, 0:512], qnT[:, :].bitcast(F32), ar[5][:, 512:1024], ar[5][:, 1024:1536]]
        es = [pT[0][:, :], pT[1][:, :], knT[:, 0:512], knT[:, 512:1024]]
        cTbf = cTb[:, :, :].rearrange("p k n -> p (k n)")
        Wh = [cTbf[:, i * 512:(i + 1) * 512] for i in range(4)]
        NB = 4
        NG = 32 if peer_stop < 3 else (1 if peer_stop == 3 else 2)

        def wgen_scores(g, t):
            ts_ = slice(t * 128, (t + 1) * 128)
            sb_ = t % 2
            for hh in range(2):
                for hq in range(4):
                    h = hh * 4 + hq
                    P.pe(lambda e, h=h, hq=hq, ts_=ts_: e.matmul(bank[4][:, hq * 128:(hq + 1) * 128], qT2[:, h, ts_], keysT[:, 2 * h + 1, :], start=True, stop=True),
                         r=["qT2", "keysT"], w=["bank4"])
                P.act(lambda e, hh=hh: e.copy(s2sb[:, hh * 4:(hh + 1) * 4, :], bank[4][:, :].rearrange("p (h n) -> p h n", h=4)), r=["bank4"], w=["s2sb"])
            for h in range(8):
                P.pe(lambda e, h=h, ts_=ts_, g=g: e.matmul(bank[4][:, h * 4:h * 4 + 4], qT1[:, h, ts_], keysT[:, 2 * h, 4 * g:4 * g + 4], start=True, stop=True),
                     r=["mixS", "keysT"], w=["bank4"])
            P.act(lambda e, sb_=sb_: e.copy(s1sb[:, sb_, :, :], bank[4][:, 0:32].rearrange("p (h n) -> p h n", h=8)), r=["bank4"], w=["s1sb%d" % sb_])

        def wgen_gates(g, t):
            ts_ = slice(t * 128, (t + 1) * 128)
            sb_ = t % 2
            wb_ = g % 2
            P.pe(lambda e: e.matmul(bank[2][:], zer[:], hT[:, 0, 0:512], start=True, stop=False), r=["zer", "hT"], w=["bank2"])

            def emit_add(h):
                i = (t * 8 + h) % NB
                sm3 = sums[i].rearrange("p (c k) -> p c k", c=4)
                addeng = P.pool if h in POOL_HEAD_SET else P.dve
                addeng(lambda e, h=h, sm3=sm3, sb_=sb_: e.tensor_tensor(sm3, s1sb[:, sb_, h, :].unsqueeze(2).to_broadcast([128, 4, 128]),
                                                                   s2sb[:, h, :].unsqueeze(1).to_broadcast([128, 4, 128]), ALU.add),
                       r=["s1sb%d" % sb_, "s2sb"], w=["sum%d" % i])

            def emit_rest(h):
                i = (t * 8 + h) % NB
                P.act(lambda e, h=h, i=i, t=t: e.activation(out=es[i], in_=sums[i], func=AF.Exp, bias=biasT[:, t, h:h + 1]), r=["sum%d" % i, "biasT"], w=["e%d" % i])
                P.dve(lambda e, h=h, i=i, t=t: e.scalar_tensor_tensor(Wh[i], sums[i], thrT[:, t, h:h + 1], es[i], ALU.is_ge, ALU.mult),
                      r=["sum%d" % i, "e%d" % i, "thrT"], w=["Wh%d" % i])
                for ci in range(4):
                    P.pe(lambda e, h=h, i=i, ci=ci: e.matmul(bank[2][:, ci * 128:(ci + 1) * 128], Wh[i][:, ci * 128:(ci + 1) * 128], ident[:],
                                                             start=False, stop=(h == 7 and ci == 3)), r=["Wh%d" % i, "ident"], w=["bank2"])
            AHEAD = 2
            for h in range(AHEAD):
                emit_add(h)
            for h in range(8):
                if h + AHEAD < 8:
                    emit_add(h + AHEAD)
                emit_rest(h)
            for ci in range(4):
                P.act(lambda e, ci=ci, ts_=ts_, wb_=wb_: e.copy(WT[wb_][ci][:, ts_], bank[2][:, ci * 128:(ci + 1) * 128]), r=["bank2"], w=["WT%d_%d" % (wb_, ci)])

        def chunk(g, ci):
            c = 4 * g + ci
            ui, vi, ti = c % 4, c % 4, c % 2
            wb_ = g % 2
            row0 = l * 16384 + c * 128
            P.dma(lambda e, ui=ui, row0=row0: e.dma_start(out=ur[ui], in_=dr["peer_u"][row0:row0 + 128, :]), w=["ur%d" % ui], eng="pool")
            P.dma(lambda e, vi=vi, row0=row0: e.dma_start(out=vs[vi], in_=dr["peer_v"][row0:row0 + 128, :]), w=["vs%d" % vi], eng="pool")
            P.pool(lambda e, vi=vi: e.tensor_tensor(vs[vi], vs[vi], modG[:, :], ALU.mult), r=["vs%d" % vi, "modG"], w=["vs%d" % vi])
            for half in range(2):
                bb = bankb[half]
                bbk = "bankb%d" % half
                for k in range(8):
                    kk_ = half * 8 + k
                    P.pe(lambda e, bb=bb, k=k, kk_=kk_, ui=ui: e.transpose(bb[:, k * 128:(k + 1) * 128], ur[ui][:, kk_ * 128:(kk_ + 1) * 128], ident[:]),
                         r=["ur%d" % ui, "ident"], w=[bbk])
                if half == 0:
                    P.act(lambda e, bb=bb, ti=ti: e.copy(uT[ti][:, 0:8, :], bb[:, :].rearrange("p (c n) -> p c n", c=8)), r=[bbk], w=["uT%d" % ti])
                else:
                    P.dve(lambda e, bb=bb, ti=ti: e.tensor_copy(uT[ti][:, 8:16, :], bb[:, :].rearrange("p (c n) -> p c n", c=8)), r=[bbk], w=["uT%d" % ti])
            for blk in range(2):
                bs = slice(blk * 512, (blk + 1) * 512)
                for k in range(16):
                    P.pe(lambda e, blk=blk, k=k, ti=ti, bs=bs: e.matmul(bank[blk][:], uT[ti][:, k, :], hT[:, k, bs], start=(k == 0), stop=(k == 15)),
                         r=["uT%d" % ti, "hT"], w=["bank%d" % blk])
                bkk = "bank%d" % blk
                tk_ = "tA%d" % blk
                P.act(lambda e, blk=blk, bs=bs: e.activation(out=tA[:, bs], in_=bank[blk][:], func=AF.Square), r=[bkk], w=[tk_])
                P.dve(lambda e, bs=bs: e.tensor_scalar(tA[:, bs], tA[:, bs], 0.044715, 1.0, op0=ALU.mult, op1=ALU.add), r=[tk_], w=[tk_])
                P.dve(lambda e, blk=blk, bs=bs: e.tensor_tensor(tA[:, bs], tA[:, bs], bank[blk][:], ALU.mult), r=[tk_, bkk], w=[tk_])
                P.act(lambda e, bs=bs: e.activation(out=tA[:, bs], in_=tA[:, bs], func=AF.Sigmoid, scale=GELU_C), r=[tk_], w=[tk_])
                P.dve(lambda e, blk=blk, bs=bs: e.tensor_tensor(gel[:, bs], tA[:, bs], bank[blk][:], ALU.mult), r=[tk_, bkk], w=["gel%d" % blk])
                P.dve(lambda e, ci=ci, bs=bs, wb_=wb_: e.tensor_tensor(As[ci][:, bs], gel[:, bs], WT[wb_][ci][:, bs], ALU.mult),
                      r=["gel%d" % blk, "WT%d_%d" % (wb_, ci)], w=["As%d" % ci])

        def outs_tile(g, t):
            ts_ = slice(t * 128, (t + 1) * 128)
            for cb in range(4):
                ob = 3 if cb % 2 == 0 else 5
                for ci in range(4):
                    vi = (4 * g + ci) % 4
                    P.pe(lambda e, ci=ci, vi=vi, ts_=ts_, cb=cb, ob=ob: e.matmul(bank[ob][:], As[ci][:, ts_], vs[vi][:, cb * 512:(cb + 1) * 512], start=(ci == 0), stop=(ci == 3)),
                         r=["As%d" % ci, "vs%d" % vi], w=["bank%d" % ob])
                oi = (t * 4 + cb) % 2
                P.act(lambda e, oi=oi, ob=ob: e.copy(ost[oi], bank[ob][:]), r=["bank%d" % ob], w=["ost%d" % oi])
                P.dma(lambda e, oi=oi, t=t, cb=cb: e.dma_start(out=xd[t][:, cb * 512:(cb + 1) * 512], in_=ost[oi], accum_op=ALU.add),
                      r=["ost%d" % oi], w=["xd%d_%d" % (t, cb)], eng="pool")

        for t in range(NT):
            wgen_scores(0, t)
            wgen_gates(0, t)
        for g in range(NG):
            nxt = g + 1 < NG
            if nxt:
                wgen_scores(g + 1, 0)
            for s_ in range(8):
                if s_ < 4:
                    chunk(g, s_)
                else:
                    outs_tile(g, 2 * (s_ - 4))
                    outs_tile(g, 2 * (s_ - 4) + 1)
                if nxt:
                    wgen_gates(g + 1, s_)
                    if s_ + 1 < 8:
                        wgen_scores(g + 1, s_ + 1)
        do_barrier()

    for l in range(n_layers):
        if do_mixer:
            ada_mod(l, 0)
            if o_dbg is not None and l == 0:
                for i, (src, k_) in enumerate(((modG[:, :], "modG"), (modA, "modA"), (modB, "modB"))):
                    for hf in range(2):
                        P.dma(lambda e, i=i, hf=hf, src=src: e.dma_start(out=o_dbg[16 + 2 * i + hf, :, 0:1024], in_=src[:, hf * 1024:(hf + 1) * 1024]), r=[k_])
            norm_to_hT()
            lru_branch(l)
            conv_branch(l)
            attention(l)
            out_proj(l)
        if do_peer:
            if peer_mode == 'dense':
                peer_dense(l)
            else:
                ada_mod(l, 1)
                peer(l)
    P.build(st)
    return nc, st


def _rope_tables(latent):
    cc = np.ones((64, NK), np.float32)
    ss = np.zeros((64, NK), np.float32)
    if latent:
        n = 1024
        row = np.repeat(np.arange(n // 64), 64).astype(np.float32)
        col = np.tile(np.arange(64), n // 64).astype(np.float32)
        inv = (1.0 / (np.float32(10000.0) ** (np.arange(16, dtype=np.float32) / np.float32(16)))).astype(np.float32)
        ang = np.concatenate([row[:, None] * inv, col[:, None] * inv], -1).astype(np.float32)
        c, s = np.cos(ang).T.astype(np.float32), np.sin(ang).T.astype(np.float32)
        cc[0:32, 512:] = c
        cc[32:64, 512:] = c
        ss[0:32, 512:] = -s
        ss[32:64, 512:] = s
    return cc, ss


def _mask_bias(latent):
    m = np.zeros((128, NKT * 4), np.float32)
    if not latent:
        for kt in range(NKT):
            for qb in range(4):
                ok = kt >= 4 and (kt - 4) // 2 == qb
                m[:, kt * 4 + qb] = 0.0 if ok else NEG
    return m


def make_in_maps(inp, cores, with_uv=True):
    in_maps = []
    for core in cores:
        latent = core >= 4
        m = {}
        if latent:
            b = core - 4
            m["x"] = inp["x_sample"][b]
            m["cvec"] = np.ascontiguousarray(inp["c"][b].reshape(16, 128).T)
            m["cache_ckv"] = inp["cache_ckv"][b]
            m["cache_krope"] = inp["cache_krope"][b]
            h0 = inp["state_lru"][b]
        else:
            m["x"] = np.ascontiguousarray(inp["x_prompt"][core * 4:(core + 1) * 4].reshape(T, D))
            m["cvec"] = np.ascontiguousarray(inp["c_ctx"].reshape(16, 128).T)
            m["cache_ckv"] = np.zeros((L, 512, 256), np.float32)
            m["cache_krope"] = np.zeros((L, 512, 64), np.float32)
            h0 = np.zeros((L, 2, 512), np.float32)
        m["h0"] = np.ascontiguousarray(h0.reshape(L, 2, 4, 128).transpose(3, 0, 1, 2).reshape(128, L * 2 * 4))
        m["CC"], m["SS"] = _rope_tables(latent)
        m["maskb"] = _mask_bias(latent)
        m["flag"] = np.full((128, 1), 1.0 if latent else 0.0, np.float32)
        m["iota16"] = np.ascontiguousarray(np.broadcast_to(np.arange(16, dtype=np.float32), (128, 16)))
        for wn in W_NAMES:
            if wn in ("peer_u", "peer_v"):
                if with_uv:
                    m[wn] = inp[wn].reshape(L * 16384, D)
                continue
            m[wn] = inp[wn]
        in_maps.append(m)
    return in_maps


def kernel(**inputs):
    inp = {k: np.ascontiguousarray(np.asarray(v)) for k, v in inputs.items()}
    in_maps = make_in_maps(inp, list(range(8)))
    shapes = {k: (v.shape, F32) for k, v in in_maps[0].items()}
    nc, st = build_program(shapes)
    with st:
        res = run_bass_kernel_spmd(nc, in_maps, core_ids=list(range(8)))
    r = res.results
    y_prompt = np.concatenate([r[c]["o_y"].reshape(4, 256, D) for c in range(4)], 0)
    y_sample = np.stack([r[c]["o_y"] for c in range(4, 8)], 0)
    new_ckv = np.concatenate([r[c]["o_ckv"].reshape(L, 4, 256, 256).transpose(1, 0, 2, 3) for c in range(4)], 0)
    new_kr = np.concatenate([r[c]["o_kr"].reshape(L, 4, 256, 64).transpose(1, 0, 2, 3) for c in range(4)], 0)
    new_lru = np.concatenate([r[c]["o_lru"].transpose(1, 0, 2, 3) for c in range(4)], 0)
    return (np.ascontiguousarray(y_prompt, np.float32), np.ascontiguousarray(y_sample, np.float32),
            np.ascontiguousarray(new_ckv, np.float32), np.ascontiguousarray(new_kr, np.float32),
            np.ascontiguousarray(new_lru, np.float32))
```

```python
import numpy as np
from contextlib import ExitStack
import concourse.bass as bass
import concourse.mybir as mybir
from concourse.bass_utils import run_bass_kernel_spmd

F32 = mybir.dt.float32
BF16 = mybir.dt.bfloat16
I32 = mybir.dt.int32
U32 = mybir.dt.uint32
ALU = mybir.AluOpType
AF = mybir.ActivationFunctionType
AX = mybir.AxisListType

D = 2048
L = 2
NT = 8
T = 1024
NK = 1536
NKT = 12
IN_W = 2880
EPS = 1e-6
NEG = -30000.0

ENGS = ("pe", "act", "dve", "pool", "sp")
N_DMA_SEM = {"sp": 12, "pool": 6, "act": 4}


class Op:
    __slots__ = ("eng", "fn", "reads", "writes", "dma", "deps", "sig", "idx", "dsem", "dval", "prev_same", "barrier")

    def __init__(self, eng, fn, reads, writes, dma):
        self.eng, self.fn, self.reads, self.writes, self.dma = eng, fn, reads, writes, dma
        self.deps = set()
        self.sig = None
        self.dsem = None
        self.dval = None
        self.prev_same = None
        self.barrier = False


class Prog:
    def __init__(self, nc):
        self.nc = nc
        self.ops = []
        self.overlaps = {}

    def alias(self, a, others):
        for b in others:
            self.overlaps.setdefault(a, set()).add(b)
            self.overlaps.setdefault(b, set()).add(a)

    def add(self, eng, fn, reads=(), writes=(), dma=False):
        o = Op(eng, fn, tuple(reads), tuple(writes), dma)
        o.idx = len(self.ops)
        self.ops.append(o)
        return o

    def pe(self, fn, r=(), w=()):
        return self.add("pe", fn, r, w)

    def act(self, fn, r=(), w=()):
        return self.add("act", fn, r, w)

    def dve(self, fn, r=(), w=()):
        return self.add("dve", fn, r, w)

    def pool(self, fn, r=(), w=()):
        return self.add("pool", fn, r, w)

    def dma(self, fn, r=(), w=(), eng="sp"):
        return self.add(eng, fn, r, w, dma=True)

    def barrier(self, fns):
        for eng, fn in fns.items():
            o = self.add(eng, fn, (), ())
            o.barrier = True

    def build(self, stack):
        nc = self.nc
        ops = self.ops
        last_w = {}
        readers = {}
        ov = self.overlaps
        for o in ops:
            deps = set()
            for k0 in o.reads:
                for k in (k0, *ov.get(k0, ())):
                    if k in last_w:
                        deps.add(last_w[k])
            for k0 in o.writes:
                for k in (k0, *ov.get(k0, ())):
                    if k in last_w:
                        deps.add(last_w[k])
                    for rd in readers.get(k, ()):
                        deps.add(rd)
            if o.barrier:
                seen_e = {}
                seen_q = {}
                for p in ops[:o.idx]:
                    if p.dma:
                        seen_q.setdefault(p.eng, []).append(p.idx)
                    elif p.eng != "sp":
                        seen_e[p.eng] = p.idx
                deps |= set(seen_e.values())
                for q, lst in seen_q.items():
                    deps |= set(lst[-N_DMA_SEM[q]:])
            deps.discard(o.idx)
            o.deps = deps
            for k in o.writes:
                last_w[k] = o.idx
                readers[k] = []
            for k in o.reads:
                readers.setdefault(k, []).append(o.idx)
        need = [False] * len(ops)
        for o in ops:
            latest = {}
            for d in o.deps:
                p = ops[d]
                if p.dma:
                    continue
                if p.eng == o.eng and p.eng == "pe" and not o.dma:
                    continue
                if d > latest.get(p.eng, -1):
                    latest[p.eng] = d
            for d in latest.values():
                need[d] = True
        esem = {e: stack.enter_context(nc.semaphore("s_" + e)) for e in ENGS}
        dsems = {e: [stack.enter_context(nc.semaphore("d_%s%d" % (e, i))) for i in range(n)]
                 for e, n in N_DMA_SEM.items()}
        cnt = {e: 0 for e in ENGS}
        dcnt = {e: 0 for e in N_DMA_SEM}
        prev_on_sem = {}
        for o in ops:
            if o.dma:
                k = dcnt[o.eng]
                n = N_DMA_SEM[o.eng]
                o.dsem = dsems[o.eng][k % n]
                o.dval = 16 * (k // n + 1)
                o.prev_same = prev_on_sem.get((o.eng, k % n))
                prev_on_sem[(o.eng, k % n)] = o.idx
                dcnt[o.eng] += 1
            elif need[o.idx]:
                cnt[o.eng] += 1
                o.sig = cnt[o.eng]
        global SEM_COUNTS
        SEM_COUNTS = (dict(cnt), dict(dcnt), len(ops))
        print('SEM_COUNTS', SEM_COUNTS, flush=True)
        by_eng = {e: [o for o in ops if o.eng == e] for e in ENGS}
        block = stack.enter_context(nc.Block())

        def emit(ename):
            def body(eng):
                waited = {}

                def wait(sem, val):
                    key = id(sem)
                    if waited.get(key, 0) >= val:
                        return
                    waited[key] = val
                    eng.wait_ge(sem, val)

                issued = []
                for o in by_eng[ename]:
                    if o.dma and o.prev_same is not None:
                        p = ops[o.prev_same]
                        wait(p.dsem, p.dval)
                    for d in sorted(o.deps):
                        p = ops[d]
                        if p.dma:
                            wait(p.dsem, p.dval)
                        else:
                            if p.sig is None:
                                continue
                            if p.eng == ename and ename == "pe" and not o.dma:
                                continue
                            wait(esem[p.eng], p.sig)
                    ins = o.fn(eng)
                    if o.dma:
                        ins.then_inc(o.dsem, 16)
                        issued.append(o)
                    elif o.sig is not None:
                        ins.then_inc(esem[ename], 1)
                for o in issued[-N_DMA_SEM.get(ename, 0):]:
                    wait(o.dsem, o.dval)
            return body

        block.tensor(emit("pe"))
        block.scalar(emit("act"))
        block.vector(emit("dve"))
        block.gpsimd(emit("pool"))
        block.sync(emit("sp"))


W_NAMES = ['ada_w', 'ada_b', 'norm_mix_g', 'w_in', 'mla_qa_g', 'mla_w_qb', 'mla_q_g', 'mla_kva_g', 'mla_w_kvb',
           'mla_k_g', 'lru_conv_w', 'lru_conv_b', 'lru_w_a', 'lru_b_a', 'lru_w_i', 'lru_b_i', 'lru_lam',
           'cm_dw_w', 'cm_dw_b', 'cm_ln_g', 'cm_ln_b', 'grp_g', 'w_out', 'norm_ffn_g', 'peer_w_q', 'peer_keys',
           'peer_u', 'peer_v']
GELU_C = 1.5957691216057308
POOL_HEADS = 0
POOL_HEAD_SET = (1, 3, 5, 7)


def build_program(shapes, n_layers=L, do_mixer=True, do_peer=True, dbg=None, peer_mode='gather', peer_stop=0):
    nc = bass.Bass("TRN2", target_bir_lowering=False)
    st = ExitStack()
    P = Prog(nc)
    dr = {}
    for name, (shape, dt) in shapes.items():
        dr[name] = nc.dram_tensor(name, list(shape), dt, kind="ExternalInput").ap()
    o_y = nc.dram_tensor("o_y", [T, D], F32, kind="ExternalOutput").ap()
    o_ckv = nc.dram_tensor("o_ckv", [L, T, 256], F32, kind="ExternalOutput").ap()
    o_kr = nc.dram_tensor("o_kr", [L, T, 64], F32, kind="ExternalOutput").ap()
    o_lru = nc.dram_tensor("o_lru", [L, 4, 2, 512], F32, kind="ExternalOutput").ap()
    o_dbg = nc.dram_tensor("o_dbg", list(dbg), F32, kind="ExternalOutput").ap() if dbg else None
    xd = o_y.rearrange("(t p) d -> t p d", p=128)

    def sb(name, shape, dt=F32):
        return st.enter_context(nc.sbuf_tensor("sb_" + name, shape, dt))

    def ps(name, shape, dt=F32):
        return st.enter_context(nc.psum_tensor("ps_" + name, shape, dt))

    big = sb("big", [128, 8192])
    hT = big[:, :].bitcast(BF16).rearrange("p (c n) -> p c n", c=16)
    ug = [big[:, i * D:(i + 1) * D] for i in range(3)]
    accp = big[:, 3 * D:4 * D]
    bigb = big[:, :].bitcast(BF16)
    ugb = [bigb[:, i * D:(i + 1) * D] for i in range(6)]
    P.alias("hT", ["ug0", "ug1", "ug2", "accp", "ugb0", "ugb1", "ugb2", "ugb3", "ugb4", "ugb5", "junkb"])
    mixS = sb("mixS", [128, 8, T], BF16)
    vgb = ugb
    junkb = bigb[:, 7 * D:8 * D]
    P.alias("accp", ["junkb"])
    modAB = sb("modAB", [128, 2 * D])
    modA, modB = modAB[:, 0:D], modAB[:, D:2 * D]
    kst = modAB[:, 0:8 * 320].rearrange("p (t d) -> p t d", t=8)
    P.alias("kst", ["modA", "modB"])
    modG = sb("modG", [128, D])
    wb = [sb("wb%d" % i, [128, 16, 256], BF16) for i in range(2)]
    tmA = sb("tmA", [128, D])
    tmB = sb("tmB", [128, D], BF16)
    ar = [sb("ar%d" % i, [128, NK]) for i in range(9)]
    ark = ["ar%d" % i for i in range(9)]
    CC = sb("CC", [64, NK])
    SS = sb("SS", [64, NK])
    qaT = sb("qaT", [128, 4, T], BF16)
    ckvT = sb("ckvT", [128, 2, NK], BF16)
    knT = sb("knT", [128, NK], BF16)
    krT = sb("krT", [64, NK], BF16)
    qnT = sb("qnT", [128, T], BF16)
    qrT = sb("qrT", [64, T], BF16)
    vh = sb("vh", [128, NKT, 128], BF16)
    pT = [sb("pT%d" % i, [128, 512], BF16) for i in range(2)]
    keysT = qaT[:, 0:2, :].rearrange("p a (b n) -> p (a b) n", b=8)
    P.alias("qaT", ["keysT"])
    qTj = sb("qTj", [128, 2, 128], BF16)
    ident = sb("ident", [128, 128], BF16)
    identf = sb("identf", [128, 128])
    ones = sb("ones", [128, 128], BF16)
    onesf = sb("onesf", [128, 128])
    cT = sb("cT", [128, 16])
    cTb = sb("cTb", [128, 16, 128], BF16)
    sm = sb("sm", [128, 16])
    vec = sb("vec", [128, 96])
    maskb = sb("maskb", [128, 48])
    flag = sb("flag", [128, 1])
    h0t = sb("h0t", [128, L * 2 * 4])
    taps = sb("taps", [128, 4, 31])
    bwt = [sb("bwt%d" % i, [128, 128], BF16) for i in range(2)]
    bwf = sb("bwf", [128, 128])
    iot = sb("iot", [128, 16])
    pk = ckvT[:, 0, :].bitcast(F32).rearrange("p (s n) -> p s n", s=6)
    tk = ckvT[:, 1, 0:1024].bitcast(F32).rearrange("p (h n) -> p h n", h=8)
    tki = knT[:, 0:768].bitcast(U32).rearrange("p (h n) -> p h n", h=8)
    pki = knT[:, 768:1024].bitcast(I32)
    P.alias("ckvT", ["pk0", "pk1", "pk2", "pk3", "pk4", "pk5", "tk"])
    P.alias("knT", ["tki", "pki"])
    thrT = sb("thrT", [128, 8, 8])
    biasT = sb("biasT", [128, 8, 8])
    s1sb = sb("s1sb", [128, 2, 8, 4])
    bsc = sb("bsc", [128, 4])
    dgr = [sb("dgr%d" % i, [128, 128], BF16) for i in range(4)]
    zer = sb("zer", [128, 128], BF16)
    bank = [ps("bank%d" % i, [128, 512]) for i in range(6)]
    bankb = [ps("bankb%d" % i, [128, 1024], BF16) for i in range(2)]

    wb_i = [0]

    def next_wb():
        i = wb_i[0] % 2
        wb_i[0] += 1
        return wb[i], "wb%d" % i

    P.pool(lambda e: e.memset(identf[:], 0.0), w=["identf"])
    P.pool(lambda e: e.affine_select(out=identf[:], in_=identf[:], pattern=[[-1, 128]], compare_op=ALU.not_equal,
                                     fill=1.0, base=0, channel_multiplier=1), r=["identf"], w=["identf"])
    P.dve(lambda e: e.tensor_copy(ident[:], identf[:]), r=["identf"], w=["ident"])
    P.dve(lambda e: e.memset(onesf[:], 1.0), w=["onesf"])
    P.dve(lambda e: e.memset(bsc[:], 0.0), w=["bsc"])
    P.dve(lambda e: e.memset(zer[:], 0.0), w=["zer"])
    P.dve(lambda e: e.memset(ones[:], 1.0), w=["ones"])
    for t in range(NT):
        P.dma(lambda e, t=t: e.dma_start(out=xd[t], in_=dr["x"][t * 128:(t + 1) * 128, :]), w=["xd%d" % t])
    P.dma(lambda e: e.dma_start(out=maskb[:], in_=dr["maskb"]), w=["maskb"])
    P.dma(lambda e: e.dma_start(out=flag[:], in_=dr["flag"]), w=["flag"])
    P.dma(lambda e: e.dma_start(out=CC[:], in_=dr["CC"]), w=["CC"])
    P.dma(lambda e: e.dma_start(out=SS[:], in_=dr["SS"]), w=["SS"])
    P.dma(lambda e: e.dma_start(out=h0t[:], in_=dr["h0"]), w=["h0t"])
    P.dma(lambda e: e.dma_start(out=iot[:], in_=dr["iota16"]), w=["iot"])
    P.dma(lambda e: e.dma_start(out=cT[:], in_=dr["cvec"]), w=["cT"])
    P.act(lambda e: e.activation(out=cT[:], in_=cT[:], func=AF.Silu), r=["cT"], w=["cT"])
    P.dve(lambda e: e.tensor_copy(cTb[:], cT[:, :].unsqueeze(2).to_broadcast([128, 16, 128])), r=["cT"], w=["cTb"])

    dbg_n = [0]

    def dump(src_ap, key, ncols):
        if o_dbg is None:
            return
        o = dbg_n[0]
        dbg_n[0] += 1
        P.dma(lambda e: e.dma_start(out=o_dbg[o, :, 0:ncols], in_=src_ap), r=[key])

    def vcol(i, n=128):
        return vec[0:n, i:i + 1]

    def load_vec(i, ap_1d, n=128):
        P.dma(lambda e: e.dma_start(out=vec[0:n, i:i + 1], in_=ap_1d.rearrange("(p o) -> p o", o=1)), w=["vec%d" % i])

    def load_cols(wname, l, c0, ncols, kdim):
        w_t, wk = next_wb()
        kc = kdim // 128
        P.dma(lambda e: e.dma_start(out=w_t[:, 0:kc, 0:ncols],
                                    in_=dr[wname][l, :, c0:c0 + ncols].rearrange("(k p) n -> p k n", p=128)),
              w=[wk], eng="pool")
        return w_t, wk

    def proj_fm(w_t, wk, col, m, src, srck, kc, dst_fn, nblk=2, bi0=2):
        for blk in range(nblk):
            bi = bi0 + blk % 2
            bk = bank[bi]
            bkk = "bank%d" % bi
            for k in range(kc):
                P.pe(lambda e, k=k, bk=bk, blk=blk: e.matmul(bk[0:m, :], w_t[:, k, col:col + m], src[:, k, blk * 512:(blk + 1) * 512],
                                                            start=(k == 0), stop=(k == kc - 1)), r=[wk, srck], w=[bkk])
            dst_fn(blk, bk, bkk)

    def rstd_rows(srcs, dst, dstk, nfeat, eps, nblk=2):
        n = len(srcs)
        for blk in range(nblk):
            bi = 4 + blk % 2
            bk = bank[bi]
            bkk = "bank%d" % bi
            for i, (fn, key, m) in enumerate(srcs):
                sq = tmA[0:m, (i % 2) * 512:(i % 2) * 512 + 512]
                sqk = "tmA%d" % (i % 2)
                P.act(lambda e, fn=fn, sq=sq, blk=blk: e.activation(out=sq, in_=fn(blk), func=AF.Square), r=[key], w=[sqk])
                P.pe(lambda e, sq=sq, m=m, i=i, bk=bk: e.matmul(bk[:], onesf[0:m, :], sq, start=(i == 0), stop=(i == n - 1)),
                     r=[sqk, "onesf"], w=[bkk])
            P.act(lambda e, bk=bk, blk=blk: e.activation(out=dst[:, blk * 512:(blk + 1) * 512], in_=bk[:], func=AF.Sqrt,
                                                        bias=eps, scale=1.0 / nfeat), r=[bkk], w=[dstk])
        P.dve(lambda e: e.reciprocal(dst[:, 0:nblk * 512], dst[:, 0:nblk * 512]), r=[dstk], w=[dstk])
    P.alias("tmA", ["tmA0", "tmA1"])
    P.alias("smg", ["sm0", "sm1", "sm2", "sm3"])

    def ada_mod(l, part):
        gname = "norm_mix_g" if part == 0 else "norm_ffn_g"
        P.dve(lambda e: e.tensor_copy(cTb[:], cT[:, :].unsqueeze(2).to_broadcast([128, 16, 128])), r=["cT"], w=["cTb"])
        for j, (dst, dname) in enumerate(((modB, "modB"), (modA, "modA"), (modG[:, :], "modG"))):
            c0 = (part * 3 + j) * D
            for cb in range(8):
                w_t, wk = load_cols("ada_w", l, c0 + cb * 256, 256, D)
                bi = cb % 2
                bk = bank[bi]
                bkk = "bank%d" % bi
                for k in range(16):
                    P.pe(lambda e, w_t=w_t, k=k, bk=bk: e.matmul(bk[:, 0:256], cTb[:, k, :], w_t[:, k, :], start=(k == 0), stop=(k == 15)),
                         r=["cTb", wk], w=[bkk])
                P.act(lambda e, dst=dst, cb=cb, bk=bk: e.copy(dst[:, cb * 256:(cb + 1) * 256], bk[:, 0:256]), r=[bkk], w=[dname])
            P.dma(lambda e, c0=c0: e.dma_start(out=tmA[:], in_=dr["ada_b"][l:l + 1, c0:c0 + D].partition_broadcast(128)), w=["tmA"])
            P.dve(lambda e, dst=dst: e.tensor_tensor(dst, dst, tmA[:], ALU.add), r=[dname, "tmA"], w=[dname])
        P.dma(lambda e: e.dma_start(out=tmA[:], in_=dr[gname][l:l + 1, :].partition_broadcast(128)), w=["tmA"])
        P.dve(lambda e: e.scalar_tensor_tensor(modA, modA, 1.0, tmA[:], ALU.add, ALU.mult), r=["modA", "tmA"], w=["modA"])

    def norm_tile(t):
        for hf in range(2):
            P.dma(lambda e, hf=hf: e.dma_start(out=ar[hf][:, 0:1024], in_=xd[t][:, hf * 1024:(hf + 1) * 1024]), r=["xd%d" % t], w=[ark[hf]])
            P.act(lambda e, hf=hf: e.activation(out=tmB[:, hf * 1024:(hf + 1) * 1024], in_=ar[hf][:, 0:1024], func=AF.Square, accum_out=sm[:, hf:hf + 1]),
                  r=[ark[hf]], w=["tmB", "sm%d" % hf])
        P.dve(lambda e: e.tensor_tensor(sm[:, 2:3], sm[:, 0:1], sm[:, 1:2], ALU.add), r=["sm0", "sm1"], w=["sm2"])
        P.act(lambda e: e.activation(out=sm[:, 3:4], in_=sm[:, 2:3], func=AF.Sqrt, bias=EPS, scale=1.0 / D), r=["sm2"], w=["sm3"])
        P.dve(lambda e: e.reciprocal(sm[:, 3:4], sm[:, 3:4]), r=["sm3"], w=["sm3"])
        for hf in range(2):
            hs = slice(hf * 1024, (hf + 1) * 1024)
            P.dve(lambda e, hf=hf, hs=hs: e.scalar_tensor_tensor(tmA[:, hs], ar[hf][:, 0:1024], sm[:, 3:4], modA[:, hs], ALU.mult, ALU.mult),
                  r=[ark[hf], "sm3", "modA"], w=["tmA"])
        P.dve(lambda e: e.tensor_tensor(tmA[:], tmA[:], modB, ALU.add), r=["tmA", "modB"], w=["tmA"])
        P.act(lambda e: e.copy(tmB[:], tmA[:]), r=["tmA"], w=["tmB"])

    def transpose_tile_to(dst_fn, dstk):
        for half in range(2):
            bb = bankb[half]
            bbk = "bankb%d" % half
            for c in range(8):
                cc = half * 8 + c
                P.pe(lambda e, bb=bb, c=c, cc=cc: e.transpose(bb[:, c * 128:(c + 1) * 128], tmB[:, cc * 128:(cc + 1) * 128], ident[:]),
                     r=["tmB", "ident"], w=[bbk])
            if half == 0:
                P.act(lambda e, bb=bb, half=half: e.copy(dst_fn(half), bb[:, :].rearrange("p (c n) -> p c n", c=8)), r=[bbk], w=[dstk])
            else:
                P.dve(lambda e, bb=bb, half=half: e.tensor_copy(dst_fn(half), bb[:, :].rearrange("p (c n) -> p c n", c=8)), r=[bbk], w=[dstk])

    def norm_to_hT():
        for t in range(NT):
            norm_tile(t)
            transpose_tile_to(lambda half, t=t: hT[:, half * 8:(half + 1) * 8, t * 128:(t + 1) * 128], "hT")

    def gelu_fm(src, srck, tmp, tmpk, dst, dstk, n=T):
        P.dve(lambda e: e.tensor_tensor(tmp[:, 0:n], src[:, 0:n], src[:, 0:n], ALU.mult), r=[srck], w=[tmpk])
        P.dve(lambda e: e.tensor_scalar(tmp[:, 0:n], tmp[:, 0:n], 0.044715, 1.0, op0=ALU.mult, op1=ALU.add), r=[tmpk], w=[tmpk])
        P.dve(lambda e: e.tensor_tensor(tmp[:, 0:n], tmp[:, 0:n], src[:, 0:n], ALU.mult), r=[tmpk, srck], w=[tmpk])
        P.act(lambda e: e.activation(out=tmp[:, 0:n], in_=tmp[:, 0:n], func=AF.Sigmoid, scale=GELU_C), r=[tmpk], w=[tmpk])
        P.dve(lambda e: e.tensor_tensor(dst[:, 0:n], tmp[:, 0:n], src[:, 0:n], ALU.mult), r=[tmpk, srck], w=[dstk])

    def group_norm_inplace(mix, mixk, c0, nch, gcol0, nfeat, rs, rsk):
        rstd_rows([(lambda blk, i=i: mix[:, c0 + i, blk * 512:(blk + 1) * 512], mixk, 128) for i in range(nch)], rs, rsk, nfeat, EPS)
        for i in range(nch):
            P.dve(lambda e, i=i: e.scalar_tensor_tensor(mix[:, c0 + i, :], mix[:, c0 + i, :], vcol(gcol0 + i), rs[:, 0:T], ALU.mult, ALU.mult),
                  r=[mixk, rsk, "vec%d" % (gcol0 + i)], w=[mixk])

    def lru_branch(l):
        for c in range(4):
            for j in range(4):
                load_vec(j * 4 + c, dr["lru_conv_w"][l, j, c * 128:(c + 1) * 128])
            load_vec(16 + c, dr["lru_conv_b"][l, c * 128:(c + 1) * 128])
            for d in range(2):
                load_vec(20 + d * 4 + c, dr["lru_b_a"][l, d, c * 128:(c + 1) * 128])
                load_vec(28 + d * 4 + c, dr["lru_b_i"][l, d, c * 128:(c + 1) * 128])
                load_vec(36 + d * 4 + c, dr["lru_lam"][l, d, c * 128:(c + 1) * 128])
            load_vec(44 + c, dr["grp_g"][l, 1024 + c * 128:1024 + (c + 1) * 128])
        vk = ["vec%d" % i for i in range(48)]
        P.act(lambda e: e.activation(out=vec[:, 36:44], in_=vec[:, 36:44], func=AF.Exp, scale=-1.0), r=vk[36:44], w=vk[36:44])
        P.act(lambda e: e.activation(out=vec[:, 36:44], in_=vec[:, 36:44], func=AF.Ln, bias=1.0), r=vk[36:44], w=vk[36:44])
        P.dve(lambda e: e.tensor_scalar(vec[:, 36:44], vec[:, 36:44], -8.0, None, op0=ALU.mult), r=vk[36:44], w=vk[36:44])
        XP = 259
        xpad, xc, hb, r_t, i_t, a_t, u_t, hfw = ar[0], ar[1], ar[2], ar[3], ar[4], ar[5], ar[6], ar[7]
        xp = xpad[:, 0:4 * XP].rearrange("p (s n) -> p s n", s=4)
        xc3 = xc[:, 0:T].rearrange("p (s n) -> p s n", s=4)
        for c in range(4):
            for cbh in range(1):
                w_x, wxk = load_cols("w_in", l, 832 + c * 128, 128, D)

            def to_pad(blk, bk, bkk):
                P.act(lambda e: e.copy(xp[:, 2 * blk:2 * blk + 2, 2:258], bk[:, :].rearrange("p (s n) -> p s n", s=2)), r=[bkk], w=["ar0"])
            proj_fm(w_x, wxk, 0, 128, hT, "hT", 16, to_pad)
            P.dve(lambda e: e.memset(xp[:, 0:1, 0:2], 0.0), r=["ar0"], w=["ar0"])
            P.dve(lambda e: e.memset(xp[:, 3:4, 258:259], 0.0), r=["ar0"], w=["ar0"])
            P.dve(lambda e: e.tensor_scalar(xp[:, 1:4, 0:2], xp[:, 0:3, 256:258], flag[:, 0:1], None, op0=ALU.mult), r=["ar0", "flag"], w=["ar0"])
            P.dve(lambda e: e.tensor_scalar(xp[:, 0:3, 258:259], xp[:, 1:4, 2:3], flag[:, 0:1], None, op0=ALU.mult), r=["ar0", "flag"], w=["ar0"])
            P.dve(lambda e, c=c: e.tensor_scalar(xc3, xp[:, :, 0:256], vcol(c), vcol(16 + c), op0=ALU.mult, op1=ALU.add),
                  r=["ar0", "vec%d" % c, "vec%d" % (16 + c)], w=["ar1"])
            for j in range(1, 4):
                P.dve(lambda e, c=c, j=j: e.scalar_tensor_tensor(xc3, xp[:, :, j:j + 256], vcol(j * 4 + c), xc3, ALU.mult, ALU.add),
                      r=["ar0", "ar1", "vec%d" % (j * 4 + c)], w=["ar1"])
            xcb = tmB[:, 0:T]
            P.act(lambda e: e.copy(xcb, xc[:, 0:T]), r=["ar1"], w=["tmB"])
            for d in range(2):
                for gi, (gname, bcol, gdst, gk) in enumerate((("lru_w_a", 20, r_t, "ar3"), ("lru_w_i", 28, i_t, "ar4"))):
                    P.pool(lambda e: e.memset(bwf[:], 0.0), w=["bwf"])
                    for hh in range(2):
                        P.dma(lambda e, gname=gname, hh=hh, d=d, c=c: e.dma_start(out=bwf[hh * 64:(hh + 1) * 64, hh * 64:(hh + 1) * 64],
                                                                                 in_=dr[gname][l, d, 2 * c + hh, :, :]), r=["bwf"], w=["bwf"])
                    P.dve(lambda e, gi=gi: e.tensor_copy(bwt[gi][:], bwf[:]), r=["bwf"], w=["bwt%d" % gi])
                    for blk in range(2):
                        bi = 2 + blk
                        bk = bank[bi]
                        bkk = "bank%d" % bi
                        P.pe(lambda e, gi=gi, bk=bk, blk=blk: e.matmul(bk[:], bwt[gi][:], tmB[:, blk * 512:(blk + 1) * 512], start=True, stop=True),
                             r=["bwt%d" % gi, "tmB"], w=[bkk])
                        P.act(lambda e, gdst=gdst, bk=bk, blk=blk, bcol=bcol, d=d, c=c: e.activation(
                            out=gdst[:, blk * 512:(blk + 1) * 512], in_=bk[:], func=AF.Sigmoid, bias=vcol(bcol + d * 4 + c)),
                            r=[bkk, "vec%d" % (bcol + d * 4 + c)], w=[gk])
                P.act(lambda e, d=d, c=c: e.activation(out=a_t[:, 0:T], in_=r_t[:, 0:T], func=AF.Exp, scale=vcol(36 + d * 4 + c)),
                      r=["ar3", "vec%d" % (36 + d * 4 + c)], w=["ar5"])
                P.dve(lambda e: e.tensor_tensor(u_t[:, 0:T], a_t[:, 0:T], a_t[:, 0:T], ALU.mult), r=["ar5"], w=["ar6"])
                P.act(lambda e: e.activation(out=u_t[:, 0:T], in_=u_t[:, 0:T], func=AF.Sqrt, scale=-1.0, bias=1.0), r=["ar6"], w=["ar6"])
                P.dve(lambda e: e.tensor_tensor(u_t[:, 0:T], u_t[:, 0:T], i_t[:, 0:T], ALU.mult), r=["ar6", "ar4"], w=["ar6"])
                P.dve(lambda e: e.tensor_tensor(u_t[:, 0:T], u_t[:, 0:T], xc[:, 0:T], ALU.mult), r=["ar6", "ar1"], w=["ar6"])
                hcol = (l * 2 + d) * 4 + c
                hdst, hk = (hfw, "ar7") if d == 0 else (hb, "ar2")
                order = range(4) if d == 0 else range(3, -1, -1)
                for si, s in enumerate(order):
                    ss_ = slice(s * 256, (s + 1) * 256)
                    if si == 0:
                        init = h0t[:, hcol:hcol + 1]
                        ik = "h0t"
                    else:
                        prev = (s * 256 - 1) if d == 0 else ((s + 1) * 256)
                        P.dve(lambda e, prev=prev, hdst=hdst: e.tensor_scalar(sm[:, 8:9], hdst[:, prev:prev + 1], flag[:, 0:1], None, op0=ALU.mult),
                              r=[hk, "flag"], w=["sm8"])
                        init = sm[:, 8:9]
                        ik = "sm8"
                    if d == 0:
                        P.dve(lambda e, ss_=ss_, init=init, hdst=hdst: e.tensor_tensor_scan(hdst[:, ss_], a_t[:, ss_], u_t[:, ss_], init, ALU.mult, ALU.add),
                              r=["ar5", "ar6", ik, hk], w=[hk])
                    else:
                        rs_ = slice((s + 1) * 256 - 1, s * 256 - 1 if s > 0 else None, -1)
                        P.dve(lambda e, rs_=rs_, init=init, hdst=hdst: e.tensor_tensor_scan(hdst[:, rs_], a_t[:, rs_], u_t[:, rs_], init, ALU.mult, ALU.add),
                              r=["ar5", "ar6", ik, hk], w=[hk])
                for s in range(4):
                    col = s * 256 + 255 if d == 0 else s * 256
                    P.dma(lambda e, c=c, s=s, d=d, col=col, hdst=hdst: e.dma_start(
                        out=o_lru[l, s, d, c * 128:(c + 1) * 128].rearrange("(p o) -> p o", o=1), in_=hdst[:, col:col + 1]), r=[hk])
            P.dve(lambda e: e.tensor_tensor(hfw[:, 0:T], hfw[:, 0:T], hb[:, 0:T], ALU.add), r=["ar7", "ar2"], w=["ar7"])
            w_g, wgk = load_cols("w_in", l, 1344 + c * 128, 128, D)
            zg = r_t

            def to_zg(blk, bk, bkk):
                P.act(lambda e: e.copy(zg[:, blk * 512:(blk + 1) * 512], bk[:]), r=[bkk], w=["ar3"])
            proj_fm(w_g, wgk, 0, 128, hT, "hT", 16, to_zg)
            gelu_fm(zg, "ar3", i_t, "ar4", a_t, "ar5")
            P.dve(lambda e, c=c: e.tensor_tensor(mixS[:, c, :], hfw[:, 0:T], a_t[:, 0:T], ALU.mult), r=["ar7", "ar5"], w=["mixS"])
        group_norm_inplace(mixS, "mixS", 0, 4, 44, 512, ar[8], "ar8")

    def conv_branch(l):
        for c in range(4):
            load_vec(48 + c, dr["cm_dw_b"][l, c * 128:(c + 1) * 128])
            load_vec(52 + c, dr["cm_ln_g"][l, c * 128:(c + 1) * 128])
            load_vec(56 + c, dr["cm_ln_b"][l, c * 128:(c + 1) * 128])
            load_vec(60 + c, dr["grp_g"][l, 1536 + c * 128:1536 + (c + 1) * 128])
        for c in range(4):
            P.dma(lambda e, c=c: e.dma_start(out=taps[:, c, :], in_=dr["cm_dw_w"][l, :, c * 128:(c + 1) * 128].rearrange("j p -> p j"),
                                             allow_slow_non_contiguous=True), r=["taps"], w=["taps"])
        HP = 286
        hp = ar[0]
        hp3 = hp[:, 0:4 * HP].rearrange("p (s n) -> p s n", s=4)
        sgm = ar[1]
        keep = [ar[4 + c] for c in range(4)]
        for c in range(4):
            w_b, wbk = load_cols("w_in", l, 2368 + c * 128, 128, D)

            def to_sig(blk, bk, bkk):
                P.act(lambda e: e.activation(out=sgm[:, blk * 512:(blk + 1) * 512], in_=bk[:], func=AF.Sigmoid), r=[bkk], w=["ar1"])
            proj_fm(w_b, wbk, 0, 128, hT, "hT", 16, to_sig)
            w_a, wak = load_cols("w_in", l, 1856 + c * 128, 128, D)

            def to_glu(blk, bk, bkk):
                P.dve(lambda e: e.tensor_tensor(hp3[:, 2 * blk:2 * blk + 2, 15:271], bk[:, :].rearrange("p (s n) -> p s n", s=2),
                                                sgm[:, blk * 512:(blk + 1) * 512].rearrange("p (s n) -> p s n", s=2), ALU.mult),
                      r=[bkk, "ar1"], w=["ar0"])
            proj_fm(w_a, wak, 0, 128, hT, "hT", 16, to_glu)
            P.dve(lambda e: e.memset(hp3[:, 0:1, 0:15], 0.0), r=["ar0"], w=["ar0"])
            P.dve(lambda e: e.memset(hp3[:, 3:4, 271:286], 0.0), r=["ar0"], w=["ar0"])
            P.dve(lambda e: e.tensor_scalar(hp3[:, 1:4, 0:15], hp3[:, 0:3, 256:271], flag[:, 0:1], None, op0=ALU.mult), r=["ar0", "flag"], w=["ar0"])
            P.dve(lambda e: e.tensor_scalar(hp3[:, 0:3, 271:286], hp3[:, 1:4, 15:30], flag[:, 0:1], None, op0=ALU.mult), r=["ar0", "flag"], w=["ar0"])
            acc3 = keep[c][:, 0:T].rearrange("p (s n) -> p s n", s=4)
            ck = ark[4 + c]
            P.dve(lambda e, c=c, acc3=acc3: e.tensor_scalar(acc3, hp3[:, :, 0:256], taps[:, c, 0:1], vcol(48 + c), op0=ALU.mult, op1=ALU.add),
                  r=["ar0", "taps", "vec%d" % (48 + c)], w=[ck])
            for j in range(1, 31):
                P.dve(lambda e, c=c, j=j, acc3=acc3: e.scalar_tensor_tensor(acc3, hp3[:, :, j:j + 256], taps[:, c, j:j + 1], acc3, ALU.mult, ALU.add),
                      r=["ar0", "taps", ck], w=[ck])
        mu, rs = ar[2], ar[3]
        for blk in range(2):
            bs = slice(blk * 512, (blk + 1) * 512)
            b1, b2 = bank[4], bank[5]
            for c in range(4):
                P.pe(lambda e, c=c, bs=bs: e.matmul(b1[:], onesf[:], keep[c][:, bs], start=(c == 0), stop=(c == 3)), r=[ark[4 + c], "onesf"], w=["bank4"])
            for c in range(4):
                sq = tmA[:, (c % 2) * 512:(c % 2) * 512 + 512]
                sqk = "tmA%d" % (c % 2)
                P.act(lambda e, c=c, bs=bs, sq=sq: e.activation(out=sq, in_=keep[c][:, bs], func=AF.Square), r=[ark[4 + c]], w=[sqk])
                P.pe(lambda e, c=c, sq=sq: e.matmul(b2[:], onesf[:], sq, start=(c == 0), stop=(c == 3)), r=[sqk, "onesf"], w=["bank5"])
            P.act(lambda e, bs=bs: e.activation(out=mu[:, bs], in_=b1[:], func=AF.Copy, scale=1.0 / 512), r=["bank4"], w=["ar2"])
            P.dve(lambda e, bs=bs: e.tensor_tensor(rs[:, bs], mu[:, bs], mu[:, bs], ALU.mult), r=["ar2"], w=["ar3"])
            P.dve(lambda e, bs=bs: e.scalar_tensor_tensor(rs[:, bs], b2[:], 1.0 / 512, rs[:, bs], ALU.mult, ALU.subtract), r=["bank5", "ar3"], w=["ar3"])
        P.act(lambda e: e.activation(out=rs[:, 0:T], in_=rs[:, 0:T], func=AF.Sqrt, bias=1e-5, scale=1.0), r=["ar3"], w=["ar3"])
        P.dve(lambda e: e.reciprocal(rs[:, 0:T], rs[:, 0:T]), r=["ar3"], w=["ar3"])
        for c in range(4):
            ck = ark[4 + c]
            acc = keep[c][:, 0:T]
            P.dve(lambda e, acc=acc: e.tensor_tensor(acc, acc, mu[:, 0:T], ALU.subtract), r=[ck, "ar2"], w=[ck])
            P.dve(lambda e, acc=acc: e.tensor_tensor(acc, acc, rs[:, 0:T], ALU.mult), r=[ck, "ar3"], w=[ck])
            P.act(lambda e, acc=acc, c=c: e.activation(out=mixS[:, 4 + c, :], in_=acc, func=AF.Silu, scale=vcol(52 + c), bias=vcol(56 + c)),
                  r=[ck, "vec%d" % (52 + c), "vec%d" % (56 + c)], w=["mixS"])
        group_norm_inplace(mixS, "mixS", 4, 4, 60, 512, ar[8], "ar8")

    def attention(l):
        for c in range(4):
            load_vec(64 + c, dr["mla_qa_g"][l, c * 128:(c + 1) * 128])
        for c in range(2):
            load_vec(68 + c, dr["mla_kva_g"][l, c * 128:(c + 1) * 128])
        load_vec(70, dr["mla_q_g"][l, 0:128])
        load_vec(71, dr["mla_q_g"][l, 128:192], 64)
        load_vec(73, dr["mla_k_g"][l, 0:128])
        load_vec(74, dr["mla_k_g"][l, 128:192], 64)
        for c in range(8):
            load_vec(76 + c, dr["grp_g"][l, c * 128:(c + 1) * 128])
        zq = [ar[i] for i in range(4)]
        for c in range(4):
            w_q, wqk = load_cols("w_in", l, c * 128, 128, D)

            def to_zq(blk, bk, bkk, c=c):
                P.act(lambda e: e.copy(zq[c][:, blk * 512:(blk + 1) * 512], bk[:]), r=[bkk], w=[ark[c]])
            proj_fm(w_q, wqk, 0, 128, hT, "hT", 16, to_zq)
        rs = ar[8]
        rstd_rows([(lambda blk, c=c: zq[c][:, blk * 512:(blk + 1) * 512], ark[c], 128) for c in range(4)], rs, "ar8", 512, EPS)
        for c in range(4):
            P.dve(lambda e, c=c: e.scalar_tensor_tensor(qaT[:, c, :], zq[c][:, 0:T], vcol(64 + c), rs[:, 0:T], ALU.mult, ALU.mult),
                  r=[ark[c], "ar8", "vec%d" % (64 + c)], w=["qaT"])
        zkv = [ar[4], ar[5]]
        for c in range(2):
            w_kv, wkvk = load_cols("w_in", l, 512 + c * 128, 128, D)

            def to_zkv(blk, bk, bkk, c=c):
                P.act(lambda e: e.copy(zkv[c][:, blk * 512:(blk + 1) * 512], bk[:]), r=[bkk], w=[ark[4 + c]])
            proj_fm(w_kv, wkvk, 0, 128, hT, "hT", 16, to_zkv)
        rstd_rows([(lambda blk, c=c: zkv[c][:, blk * 512:(blk + 1) * 512], ark[4 + c], 128) for c in range(2)], rs, "ar8", 256, EPS)
        for c in range(2):
            P.dve(lambda e, c=c: e.scalar_tensor_tensor(zkv[c][:, 0:T], zkv[c][:, 0:T], vcol(68 + c), rs[:, 0:T], ALU.mult, ALU.mult),
                  r=[ark[4 + c], "ar8", "vec%d" % (68 + c)], w=[ark[4 + c]])
            P.act(lambda e, c=c: e.copy(ckvT[:, c, 512:NK], zkv[c][:, 0:T]), r=[ark[4 + c]], w=["ckvT"])
        for t in range(NT):
            for c in range(2):
                bk = bank[2 + c]
                P.pe(lambda e, bk=bk, c=c, t=t: e.transpose(bk[:, 0:128], zkv[c][:, t * 128:(t + 1) * 128], identf[:]), r=[ark[4 + c], "identf"], w=["bank%d" % (2 + c)])
                P.act(lambda e, bk=bk, c=c, t=t: e.copy(kst[:, t, c * 128:(c + 1) * 128], bk[:, 0:128]), r=["bank%d" % (2 + c)], w=["kst"])
        kr = ar[7]
        w_kr, wkrk = load_cols("w_in", l, 768, 64, D)

        def to_kr(blk, bk, bkk):
            P.act(lambda e: e.copy(kr[0:64, 512 + blk * 512:512 + (blk + 1) * 512], bk[0:64, :]), r=[bkk], w=["ar7"])
        proj_fm(w_kr, wkrk, 0, 64, hT, "hT", 16, to_kr)
        for t in range(NT):
            bk = bank[2 + t % 2]
            bkk = "bank%d" % (2 + t % 2)
            P.pe(lambda e, bk=bk, t=t: e.transpose(bk[:, 0:64], kr[0:64, 512 + t * 128:512 + (t + 1) * 128], identf[0:64, 0:64]),
                 r=["ar7", "identf"], w=[bkk])
            P.act(lambda e, bk=bk, t=t: e.copy(kst[:, t, 256:320], bk[:, 0:64]), r=[bkk], w=["kst"])
        P.dma(lambda e: e.dma_start(out=o_ckv[l].rearrange("(t p) d -> p t d", p=128), in_=kst[:, :, 0:256]), r=["kst"])
        P.dma(lambda e: e.dma_start(out=o_kr[l].rearrange("(t p) d -> p t d", p=128), in_=kst[:, :, 256:320]), r=["kst"])
        P.dma(lambda e: e.dma_start(out=kst[:, 0:4, 0:256], in_=dr["cache_ckv"][l].rearrange("(t p) d -> p t d", p=128)), r=["kst"], w=["kst"])
        P.dma(lambda e: e.dma_start(out=kst[:, 0:4, 256:320], in_=dr["cache_krope"][l].rearrange("(t p) d -> p t d", p=128)), r=["kst"], w=["kst"])
        for t in range(4):
            for c in range(2):
                bk = bank[2 + c]
                P.pe(lambda e, bk=bk, c=c, t=t: e.transpose(bk[:, 0:128], kst[:, t, c * 128:(c + 1) * 128], identf[:]), r=["kst", "identf"], w=["bank%d" % (2 + c)])
                P.act(lambda e, bk=bk, c=c, t=t: e.copy(ckvT[:, c, t * 128:(t + 1) * 128], bk[:, 0:128]), r=["bank%d" % (2 + c)], w=["ckvT"])
            bk = bank[4]
            P.pe(lambda e, t=t: e.transpose(bk[0:64, 0:128], kst[:, t, 256:320], identf[:]), r=["kst", "identf"], w=["bank4"])
            P.act(lambda e, t=t: e.copy(kr[0:64, t * 128:(t + 1) * 128], bk[0:64, 0:128]), r=["bank4"], w=["ar7"])
        sskr = ar[6]
        for blk in range(3):
            bs = slice(blk * 512, (blk + 1) * 512)
            P.act(lambda e, bs=bs: e.activation(out=tmA[0:64, 0:512], in_=kr[0:64, bs], func=AF.Square), r=["ar7"], w=["tmA0"])
            P.pe(lambda e: e.matmul(bank[4][:], onesf[0:64, :], tmA[0:64, 0:512], start=True, stop=True), r=["tmA0", "onesf"], w=["bank4"])
            P.act(lambda e, bs=bs: e.copy(sskr[:, bs], bank[4][:]), r=["bank4"], w=["ar6"])
        kk, kk2, rk, qr, Rs, t2 = ar[0], ar[1], ar[2], ar[3], ar[4], ar[5]
        for h in range(8):
            w_qb, wqbk = load_cols("mla_w_qb", l, h * 192, 192, 512)
            w_kvb, wkvbk = load_cols("mla_w_kvb", l, h * 256, 256, 256)
            for blk in range(3):
                bs = slice(blk * 512, (blk + 1) * 512)
                for k in range(2):
                    P.pe(lambda e, k=k, bs=bs, w_kvb=w_kvb: e.matmul(bank[2][:], w_kvb[:, k, 0:128], ckvT[:, k, bs], start=(k == 0), stop=(k == 1)), r=[wkvbk, "ckvT"], w=["bank2"])
                P.act(lambda e, bs=bs: e.copy(kk[:, bs], bank[2][:]), r=["bank2"], w=["ar0"])
                P.act(lambda e, bs=bs: e.activation(out=kk2[:, bs], in_=bank[2][:], func=AF.Square), r=["bank2"], w=["ar1"])
                P.pe(lambda e, bs=bs: e.matmul(bank[3][:], onesf[:], kk2[:, bs], start=True, stop=True), r=["ar1", "onesf"], w=["bank3"])
                P.dve(lambda e, bs=bs: e.tensor_tensor(rk[:, bs], bank[3][:], sskr[:, bs], ALU.add), r=["bank3", "ar6"], w=["ar2"])
            P.act(lambda e: e.activation(out=rk[:], in_=rk[:], func=AF.Sqrt, bias=EPS, scale=1.0 / 192), r=["ar2"], w=["ar2"])
            P.dve(lambda e: e.reciprocal(rk[:], rk[:]), r=["ar2"], w=["ar2"])
            P.dve(lambda e: e.scalar_tensor_tensor(knT[:], kk[:], vcol(73), rk[:], ALU.mult, ALU.mult), r=["ar0", "ar2", "vec73"], w=["knT"])
            P.dve(lambda e: e.scalar_tensor_tensor(kk2[0:64, :], kr[0:64, :], vcol(74, 64), rk[0:64, :], ALU.mult, ALU.mult), r=["ar7", "ar2", "vec74"], w=["ar1"])
            rope64(kk2, "ar1", Rs, "ar4", t2, "ar5", CC, SS, 0, NK, krT, "krT")
            for kt in range(NKT):
                bi = kt % 2 + 2
                bk = bank[bi]
                for k in range(2):
                    P.pe(lambda e, k=k, kt=kt, bk=bk, w_kvb=w_kvb: e.matmul(bk[:, 0:128], ckvT[:, k, kt * 128:(kt + 1) * 128], w_kvb[:, k, 128:256],
                                                              start=(k == 0), stop=(k == 1)), r=["ckvT", wkvbk], w=["bank%d" % bi])
                P.act(lambda e, kt=kt, bk=bk: e.copy(vh[:, kt, :], bk[:, 0:128]), r=["bank%d" % bi], w=["vh"])
            qq, qq2, rq = kk, kk2, rk
            for blk in range(2):
                bs = slice(blk * 512, (blk + 1) * 512)
                for k in range(4):
                    P.pe(lambda e, k=k, bs=bs, w_qb=w_qb: e.matmul(bank[2][:], w_qb[:, k, 0:128], qaT[:, k, bs], start=(k == 0), stop=(k == 3)), r=[wqbk, "qaT"], w=["bank2"])
                P.act(lambda e, bs=bs: e.copy(qq[:, bs], bank[2][:]), r=["bank2"], w=["ar0"])
                P.act(lambda e, bs=bs: e.activation(out=qq2[:, bs], in_=bank[2][:], func=AF.Square), r=["bank2"], w=["ar1"])
                for k in range(4):
                    P.pe(lambda e, k=k, bs=bs, w_qb=w_qb: e.matmul(bank[3][0:64, :], w_qb[:, k, 128:192], qaT[:, k, bs], start=(k == 0), stop=(k == 3)), r=[wqbk, "qaT"], w=["bank3"])
                P.act(lambda e, bs=bs: e.copy(qr[0:64, bs], bank[3][0:64, :]), r=["bank3"], w=["ar3"])
                P.act(lambda e, bs=bs: e.activation(out=t2[0:64, bs], in_=bank[3][0:64, :], func=AF.Square), r=["bank3"], w=["ar5"])
                P.pe(lambda e, bs=bs: e.matmul(bank[4][:], onesf[:], qq2[:, bs], start=True, stop=False), r=["ar1", "onesf"], w=["bank4"])
                P.pe(lambda e, bs=bs: e.matmul(bank[4][:], onesf[0:64, :], t2[0:64, bs], start=False, stop=True), r=["ar5", "onesf"], w=["bank4"])
                P.act(lambda e, bs=bs: e.activation(out=rq[:, bs], in_=bank[4][:], func=AF.Sqrt, bias=EPS, scale=1.0 / 192), r=["bank4"], w=["ar2"])
            P.dve(lambda e: e.reciprocal(rq[:, 0:T], rq[:, 0:T]), r=["ar2"], w=["ar2"])
            P.dve(lambda e: e.scalar_tensor_tensor(qnT[:], qq[:, 0:T], vcol(70), rq[:, 0:T], ALU.mult, ALU.mult), r=["ar0", "ar2", "vec70"], w=["qnT"])
            P.dve(lambda e: e.scalar_tensor_tensor(qr[0:64, 0:T], qr[0:64, 0:T], vcol(71, 64), rq[0:64, 0:T], ALU.mult, ALU.mult), r=["ar3", "ar2", "vec71"], w=["ar3"])
            rope64(qr, "ar3", Rs, "ar4", t2, "ar5", CC, SS, 512, T, qrT, "qrT")
            for blk in range(2):
                bs = slice(blk * 512, (blk + 1) * 512)
                for kt in range(NKT):
                    ks = slice(kt * 128, (kt + 1) * 128)
                    bi = kt % 2
                    bk = bank[bi]
                    bkk = "bank%d" % bi
                    P.pe(lambda e, ks=ks, bs=bs, bk=bk: e.matmul(bk[:], knT[:, ks], qnT[:, bs], start=True, stop=False), r=["knT", "qnT"], w=[bkk])
                    P.pe(lambda e, ks=ks, bs=bs, bk=bk: e.matmul(bk[:], krT[:, ks], qrT[:, bs], start=False, stop=True), r=["krT", "qrT"], w=[bkk])
                    pt = pT[bi]
                    ptk = "pT%d" % bi
                    for qb in range(2):
                        mcol = kt * 4 + blk * 2 + qb
                        P.act(lambda e, bk=bk, pt=pt, qb=qb, mcol=mcol: e.activation(out=pt[:, qb * 256:(qb + 1) * 256], in_=bk[:, qb * 256:(qb + 1) * 256],
                                                                                     func=AF.Exp, scale=192.0 ** -0.5, bias=maskb[:, mcol:mcol + 1]),
                              r=[bkk, "maskb"], w=[ptk])
                    P.pe(lambda e, kt=kt, pt=pt: e.matmul(bank[2][:], vh[:, kt, :], pt[:], start=(kt == 0), stop=(kt == NKT - 1)), r=["vh", ptk], w=["bank2"])
                    P.pe(lambda e, kt=kt, pt=pt: e.matmul(bank[3][:], ones[:], pt[:], start=(kt == 0), stop=(kt == NKT - 1)), r=["ones", ptk], w=["bank3"])
                P.dve(lambda e: e.reciprocal(tmA[:, 0:512], bank[3][:]), r=["bank3"], w=["tmA0"])
                P.dve(lambda e, bs=bs, h=h: e.tensor_tensor(hT[:, h, bs], bank[2][:], tmA[:, 0:512], ALU.mult), r=["bank2", "tmA0"], w=["hT"])
        group_norm_inplace(hT, "hT", 0, 8, 76, 1024, ar[8], "ar8")

    def rope64(R, Rk, Rs, Rsk, t2, t2k, CCt, SSt, c0, n, out, outk):
        P.act(lambda e: e.copy(Rs[0:32, 0:n], R[32:64, 0:n]), r=[Rk], w=[Rsk])
        P.act(lambda e: e.copy(Rs[32:64, 0:n], R[0:32, 0:n]), r=[Rk, Rsk], w=[Rsk])
        P.dve(lambda e: e.tensor_tensor(Rs[0:64, 0:n], Rs[0:64, 0:n], SSt[:, c0:c0 + n], ALU.mult), r=[Rsk, "SS"], w=[Rsk])
        P.dve(lambda e: e.tensor_tensor(t2[0:64, 0:n], R[0:64, 0:n], CCt[:, c0:c0 + n], ALU.mult), r=[Rk, "CC"], w=[t2k])
        P.dve(lambda e: e.tensor_tensor(out[:, 0:n], t2[0:64, 0:n], Rs[0:64, 0:n], ALU.add), r=[t2k, Rsk], w=[outk])

    def out_proj(l):
        if o_dbg is not None and l == 0:
            for k in range(16):
                src = hT[:, k, :] if k < 8 else mixS[:, k - 8, :]
                P.dma(lambda e, k=k, src=src: e.dma_start(out=o_dbg[k, :, 0:T], in_=src), r=["hT", "mixS"], eng="pool")
        def mixchunk(k, t):
            if k < 8:
                return hT[:, k, t * 128:(t + 1) * 128]
            return mixS[:, k - 8, t * 128:(t + 1) * 128]
        for cb in range(8):
            w_t, wk = load_cols("w_out", l, cb * 256, 256, D)
            cs = slice(cb * 256, (cb + 1) * 256)
            for t in range(NT):
                bi = t % 2
                bk = bank[bi]
                bkk = "bank%d" % bi
                xs = ar[t % 2]
                xsk = ark[t % 2]
                P.dma(lambda e, t=t, cs=cs, xs=xs: e.dma_start(out=xs[:, 0:256], in_=xd[t][:, cs]), r=["xd%d" % t], w=[xsk])
                for k in range(16):
                    P.pe(lambda e, k=k, t=t, bk=bk, w_t=w_t: e.matmul(bk[:, 0:256], mixchunk(k, t), w_t[:, k, :], start=(k == 0), stop=(k == 15)),
                         r=["hT", "mixS", wk], w=[bkk])
                P.dve(lambda e, bk=bk, cs=cs, xs=xs: e.tensor_tensor(xs[:, 256:512], bk[:, 0:256], modG[:, cs], ALU.mult), r=[bkk, "modG", xsk], w=[xsk])
                P.dve(lambda e, xs=xs: e.tensor_tensor(xs[:, 0:256], xs[:, 0:256], xs[:, 256:512], ALU.add), r=[xsk], w=[xsk])
                P.dma(lambda e, t=t, cs=cs, xs=xs: e.dma_start(out=xd[t][:, cs], in_=xs[:, 0:256]), r=[xsk], w=["xd%d" % t])

    def peer(l):
        hfT = mixS[:, 0:2, :].rearrange("p a (b n) -> p (a b) n", b=8)
        for j in range(16):
            h_, p_ = j // 2, j % 2
            P.dma(lambda e, h_=h_, p_=p_: e.dma_start(out=bwf[:], in_=dr["peer_keys"][l, h_, p_, :, :]), w=["bwf"])
            P.dve(lambda e: e.tensor_copy(bwt[0][:], bwf[:]), r=["bwf"], w=["bwt0"])
            bb = bankb[j % 2]
            bbk = "bankb%d" % (j % 2)
            P.pe(lambda e, bb=bb: e.transpose(bb[:, 0:128], bwt[0][:], ident[:]), r=["bwt0", "ident"], w=[bbk])
            P.act(lambda e, bb=bb, j=j: e.copy(keysT[:, j, :], bb[:, 0:128]), r=[bbk], w=["keysT"])
        sc = [ar[2], ar[3]]
        s1b, cand, candb, eq = ar[4], ar[5], ar[6], accp
        for t in range(NT):
            norm_tile(t)
            transpose_tile_to(lambda half: hfT[:, half * 8:(half + 1) * 8, :], "mixS")
            for jb in range(8):
                w_t, wk = load_cols("peer_w_q", l, jb * 256, 256, D)
                for jj in range(2):
                    j = jb * 2 + jj
                    bk = bank[2 + jj]
                    bkk = "bank%d" % (2 + jj)
                    for k in range(16):
                        P.pe(lambda e, k=k, jj=jj, bk=bk, w_t=w_t: e.matmul(bk[:, 0:128], w_t[:, k, jj * 128:(jj + 1) * 128], hfT[:, k, :], start=(k == 0), stop=(k == 15)),
                             r=[wk, "mixS"], w=[bkk])
                    P.act(lambda e, jj=jj, bk=bk: e.copy(qTj[:, jj, :], bk[:, 0:128]), r=[bkk], w=["qTj%d" % jj])
                    bs_ = bank[4 + jj]
                    bsk = "bank%d" % (4 + jj)
                    P.pe(lambda e, jj=jj, j=j, bs_=bs_: e.matmul(bs_[:, 0:128], qTj[:, jj, :], keysT[:, j, :], start=True, stop=True), r=["qTj%d" % jj, "keysT"], w=[bsk])
                    P.dve(lambda e, j=j, bs_=bs_: e.tensor_copy(sc[j // 8][:, (j % 8) * 128:(j % 8 + 1) * 128], bs_[:, 0:128]), r=[bsk], w=[ark[2 + j // 8]])
            for h in range(8):
                for p_ in range(2):
                    j = h * 2 + p_
                    s_ap = sc[j // 8][:, (j % 8) * 128:(j % 8 + 1) * 128]
                    sk = ark[2 + j // 8]
                    m = tk[:, h, p_ * 16:(p_ + 1) * 16]
                    ii = tki[:, h, p_ * 16:(p_ + 1) * 16]
                    P.dve(lambda e, m=m, s_ap=s_ap: e.max(out=m[:, 0:8], in_=s_ap), r=[sk], w=["tk"])
                    P.dve(lambda e, m=m, ii=ii, s_ap=s_ap: e.max_index(out=ii[:, 0:8], in_max=m[:, 0:8], in_values=s_ap), r=[sk, "tk"], w=["tki"])
                    P.dve(lambda e, m=m, s_ap=s_ap: e.match_replace(out=s1b[:, 0:128], in_to_replace=m[:, 0:8], in_values=s_ap, imm_value=-1e30), r=[sk, "tk"], w=["ar4"])
                    P.dve(lambda e, m=m: e.max(out=m[:, 8:16], in_=s1b[:, 0:128]), r=["ar4"], w=["tk"])
                    P.dve(lambda e, m=m, ii=ii: e.max_index(out=ii[:, 8:16], in_max=m[:, 8:16], in_values=s1b[:, 0:128]), r=["ar4", "tk"], w=["tki"])
                c3 = cand[:, 0:256].rearrange("p (a b) -> p a b", a=16)
                P.dve(lambda e, h=h, c3=c3: e.tensor_tensor(c3, tk[:, h, 0:16].unsqueeze(2).to_broadcast([128, 16, 16]),
                                                          tk[:, h, 16:32].unsqueeze(1).to_broadcast([128, 16, 16]), ALU.add), r=["tk"], w=["ar5"])
                m = tk[:, h, 32:48]
                ii = tki[:, h, 32:48]
                P.dve(lambda e, m=m: e.max(out=m[:, 0:8], in_=cand[:, 0:256]), r=["ar5"], w=["tk"])
                P.dve(lambda e, m=m, ii=ii: e.max_index(out=ii[:, 0:8], in_max=m[:, 0:8], in_values=cand[:, 0:256]), r=["ar5", "tk"], w=["tki"])
                P.dve(lambda e, m=m: e.match_replace(out=candb[:, 0:256], in_to_replace=m[:, 0:8], in_values=cand[:, 0:256], imm_value=-1e30), r=["ar5", "tk"], w=["ar6"])
                P.dve(lambda e, m=m: e.max(out=m[:, 8:16], in_=candb[:, 0:256]), r=["ar6"], w=["tk"])
                P.dve(lambda e, m=m, ii=ii: e.max_index(out=ii[:, 8:16], in_max=m[:, 8:16], in_values=candb[:, 0:256]), r=["ar6", "tk"], w=["tki"])
            posu = tki[:, :, 32:48]
            k1f, k2f, idxf, gate, actv, wgt = (pk[:, i, :] for i in range(6))
            pu3 = pki[:, :].bitcast(U32).rearrange("p (h r) -> p h r", h=8)
            a3 = actv.rearrange("p (h r) -> p h r", h=8)
            w3 = wgt.rearrange("p (h r) -> p h r", h=8)
            eq4 = eq.rearrange("p (h r q) -> p h r q", h=8, r=16)
            for which, (sop, samt, src_i, dstf) in enumerate(((ALU.logical_shift_right, 4, 0, k1f), (ALU.bitwise_and, 15, 16, k2f))):
                P.dve(lambda e, sop=sop, samt=samt: e.tensor_single_scalar(pu3, posu, samt, op=sop), r=["tki"], w=["pki"])
                P.dve(lambda e: e.tensor_copy(a3, pu3), r=["pki"], w=["pk4"])
                P.dve(lambda e, src_i=src_i: e.tensor_copy(w3, tki[:, :, src_i:src_i + 16]), r=["tki"], w=["pk5"])
                P.dve(lambda e: e.tensor_tensor(eq4, a3.unsqueeze(3).to_broadcast([128, 8, 16, 16]),
                                                iot[:, :].unsqueeze(1).unsqueeze(1).to_broadcast([128, 8, 16, 16]), ALU.is_equal), r=["pk4", "iot"], w=["accp"])
                P.dve(lambda e: e.tensor_tensor(eq4, eq4, w3.unsqueeze(2).to_broadcast([128, 8, 16, 16]), ALU.mult), r=["accp", "pk5"], w=["accp"])
                P.dve(lambda e, dstf=dstf: e.tensor_reduce(out=dstf.rearrange("p (h r) -> p h r", h=8), in_=eq4, axis=AX.X, op=ALU.add), r=["accp"], w=["pk%d" % which])
            P.dve(lambda e: e.scalar_tensor_tensor(idxf, k1f, 128.0, k2f, ALU.mult, ALU.add), r=["pk0", "pk1"], w=["pk2"])
            P.dve(lambda e: e.tensor_scalar(idxf, idxf, float(l * 16384), None, op0=ALU.add), r=["pk2"], w=["pk2"])
            P.dve(lambda e: e.tensor_copy(pki[:, :], idxf), r=["pk2"], w=["pki"])
            c16 = tk[:, :, 32:48]
            g3 = gate.rearrange("p (h r) -> p h r", h=8)
            P.dve(lambda e: e.tensor_tensor(g3, c16, tk[:, :, 32:33].to_broadcast([128, 8, 16]), ALU.subtract), r=["tk"], w=["pk3"])
            P.act(lambda e: e.activation(out=gate, in_=gate, func=AF.Exp), r=["pk3"], w=["pk3"])
            P.dve(lambda e: e.tensor_reduce(out=sm[:, 0:8], in_=g3, axis=AX.X, op=ALU.add), r=["pk3"], w=["smg"])
            P.dve(lambda e: e.reciprocal(sm[:, 0:8], sm[:, 0:8]), r=["smg"], w=["smg"])
            P.dve(lambda e: e.tensor_tensor(g3, g3, sm[:, 0:8].unsqueeze(2).to_broadcast([128, 8, 16]), ALU.mult), r=["pk3", "smg"], w=["pk3"])
            for j in range(128):
                g_ = ugb[j % 6]
                gk = "ugb%d" % (j % 6)
                P.dma(lambda e, j=j, g_=g_: e.indirect_dma_start(out=g_, out_offset=None, in_=dr["peer_u"],
                                                                 in_offset=bass.IndirectOffsetOnAxis(ap=pki[:, j:j + 1], axis=0)),
                      r=["pki"], w=[gk], eng="pool")
                P.dve(lambda e, j=j, g_=g_: e.scalar_tensor_tensor(junkb, g_, 1.0, tmB[:], ALU.mult, ALU.mult, accum_out=actv[:, j:j + 1]),
                      r=[gk, "tmB"], w=["junkb", "pk4"])
            P.dve(lambda e: e.tensor_tensor(wgt, actv, actv, ALU.mult), r=["pk4"], w=["pk5"])
            P.dve(lambda e: e.tensor_scalar(wgt, wgt, 0.044715, 1.0, op0=ALU.mult, op1=ALU.add), r=["pk5"], w=["pk5"])
            P.dve(lambda e: e.tensor_tensor(wgt, wgt, actv, ALU.mult), r=["pk5", "pk4"], w=["pk5"])
            P.act(lambda e: e.activation(out=wgt, in_=wgt, func=AF.Sigmoid, scale=GELU_C), r=["pk5"], w=["pk5"])
            P.dve(lambda e: e.tensor_tensor(wgt, wgt, actv, ALU.mult), r=["pk5", "pk4"], w=["pk5"])
            P.dve(lambda e: e.tensor_tensor(wgt, wgt, gate, ALU.mult), r=["pk5", "pk3"], w=["pk5"])
            for j in range(128):
                g_ = vgb[j % 6]
                gk = "ugb%d" % (j % 6)
                dgi = j % 4
                P.dma(lambda e, j=j, g_=g_: e.indirect_dma_start(out=g_, out_offset=None, in_=dr["peer_v"],
                                                                 in_offset=bass.IndirectOffsetOnAxis(ap=pki[:, j:j + 1], axis=0)),
                      r=["pki"], w=[gk], eng="pool")
                P.dve(lambda e, j=j, dgi=dgi: e.tensor_scalar(dgr[dgi][:], identf[:], wgt[:, j:j + 1], None, op0=ALU.mult), r=["identf", "pk5"], w=["dgr%d" % dgi])
                for cb in range(4):
                    P.pe(lambda e, j=j, cb=cb, g_=g_, dgi=dgi: e.matmul(bank[2 + cb][:], dgr[dgi][:], g_[:, cb * 512:(cb + 1) * 512], start=(j == 0), stop=(j == 127)),
                         r=["dgr%d" % dgi, gk], w=["bank%d" % (2 + cb)])
            for hf in range(2):
                hs = slice(hf * 1024, (hf + 1) * 1024)
                for cbh in range(2):
                    cb = hf * 2 + cbh
                    cs = slice(cb * 512, (cb + 1) * 512)
                    P.dve(lambda e, cb=cb, cs=cs: e.tensor_tensor(tmA[:, cs], bank[2 + cb][:], modG[:, cs], ALU.mult), r=["bank%d" % (2 + cb), "modG"], w=["tmA"])
                    P.dve(lambda e, hf=hf, cbh=cbh, cs=cs: e.tensor_tensor(ar[hf][:, cbh * 512:(cbh + 1) * 512], ar[hf][:, cbh * 512:(cbh + 1) * 512], tmA[:, cs], ALU.add),
                          r=[ark[hf], "tmA"], w=[ark[hf]])
                P.dma(lambda e, hf=hf, hs=hs, t=t: e.dma_start(out=xd[t][:, hs], in_=ar[hf][:, 0:1024]), r=[ark[hf]], w=["xd%d" % t])

    def do_barrier():
        P.barrier({"act": lambda e: e.copy(bsc[:, 0:1], bsc[:, 1:2]), "dve": lambda e: e.memset(bsc[:, 2:3], 0.0),
                   "pool": lambda e: e.memset(bsc[:, 3:4], 0.0), "sp": lambda e: e.nop()})

    for t_ in range(NT):
        P.alias("xd%d" % t_, ["xd%d_%d" % (t_, cb_) for cb_ in range(4)])

    def peer_dense(l):
        do_barrier()
        ada_mod(l, 1)
        norm_to_hT()
        for j in range(16):
            h_, p_ = j // 2, j % 2
            P.dma(lambda e, h_=h_, p_=p_: e.dma_start(out=bwf[:], in_=dr["peer_keys"][l, h_, p_, :, :]), w=["bwf"])
            P.dve(lambda e: e.tensor_copy(bwt[0][:], bwf[:]), r=["bwf"], w=["bwt0"])
            bb = bankb[j % 2]
            bbk = "bankb%d" % (j % 2)
            P.pe(lambda e, bb=bb: e.transpose(bb[:, 0:128], bwt[0][:], ident[:]), r=["bwt0", "ident"], w=[bbk])
            P.act(lambda e, bb=bb, j=j: e.copy(keysT[:, j, :], bb[:, 0:128]), r=[bbk], w=["keysT"])
        qT1 = mixS
        qT2 = modAB[:, :].bitcast(BF16).rearrange("p (c n) -> p c n", c=8)
        P.alias("qT2", ["modA", "modB", "kst"])
        for jb in range(8):
            w_t, wk = load_cols("peer_w_q", l, jb * 256, 256, D)

            def to_q1(blk, bk, bkk, jb=jb):
                P.act(lambda e: e.copy(qT1[:, jb, blk * 512:(blk + 1) * 512], bk[:]), r=[bkk], w=["mixS"])

            def to_q2(blk, bk, bkk, jb=jb):
                P.dve(lambda e: e.tensor_copy(qT2[:, jb, blk * 512:(blk + 1) * 512], bk[:]), r=[bkk], w=["qT2"])
            proj_fm(w_t, wk, 0, 128, hT, "hT", 16, to_q1)
            proj_fm(w_t, wk, 128, 128, hT, "hT", 16, to_q2)
        sc = [ar[2], ar[3]]
        s1b, cand, candb = ar[4], ar[5], ar[6]
        for t in range(NT):
            ts_ = slice(t * 128, (t + 1) * 128)
            for j in range(16):
                h_, p_ = j // 2, j % 2
                qsrc, qk = (qT1, "mixS") if p_ == 0 else (qT2, "qT2")
                bi = 2 + j // 4
                P.pe(lambda e, j=j, h_=h_, qsrc=qsrc, bi=bi, ts_=ts_: e.matmul(bank[bi][:, (j % 4) * 128:(j % 4 + 1) * 128], qsrc[:, h_, ts_], keysT[:, j, :],
                                                                          start=True, stop=True), r=[qk, "keysT"], w=["bank%d" % bi])
            for q4 in range(4):
                dst = sc[q4 // 2][:, (q4 % 2) * 512:(q4 % 2 + 1) * 512]
                if q4 % 2 == 0:
                    P.act(lambda e, dst=dst, q4=q4: e.copy(dst, bank[2 + q4][:]), r=["bank%d" % (2 + q4)], w=[ark[2 + q4 // 2]])
                else:
                    P.dve(lambda e, dst=dst, q4=q4: e.tensor_copy(dst, bank[2 + q4][:]), r=["bank%d" % (2 + q4)], w=[ark[2 + q4 // 2]])
            for h in range(8):
                for p_ in range(2):
                    j = h * 2 + p_
                    s_ap = sc[j // 8][:, (j % 8) * 128:(j % 8 + 1) * 128]
                    sk = ark[2 + j // 8]
                    m = tk[:, h, p_ * 16:(p_ + 1) * 16]
                    P.dve(lambda e, m=m, s_ap=s_ap: e.max(out=m[:, 0:8], in_=s_ap), r=[sk], w=["tk"])
                    P.dve(lambda e, m=m, s_ap=s_ap: e.match_replace(out=s1b[:, 0:128], in_to_replace=m[:, 0:8], in_values=s_ap, imm_value=-1e30), r=[sk, "tk"], w=["ar4"])
                    P.dve(lambda e, m=m: e.max(out=m[:, 8:16], in_=s1b[:, 0:128]), r=["ar4"], w=["tk"])
                c3 = cand[:, 0:256].rearrange("p (a b) -> p a b", a=16)
                P.dve(lambda e, h=h, c3=c3: e.tensor_tensor(c3, tk[:, h, 0:16].unsqueeze(2).to_broadcast([128, 16, 16]),
                                                          tk[:, h, 16:32].unsqueeze(1).to_broadcast([128, 16, 16]), ALU.add), r=["tk"], w=["ar5"])
                m = tk[:, h, 32:48]
                P.dve(lambda e, m=m: e.max(out=m[:, 0:8], in_=cand[:, 0:256]), r=["ar5"], w=["tk"])
                P.dve(lambda e, m=m: e.match_replace(out=candb[:, 0:256], in_to_replace=m[:, 0:8], in_values=cand[:, 0:256], imm_value=-1e30), r=["ar5", "tk"], w=["ar6"])
                P.dve(lambda e, m=m: e.max(out=m[:, 8:16], in_=candb[:, 0:256]), r=["ar6"], w=["tk"])
            P.dve(lambda e, t=t: e.tensor_copy(thrT[:, t, :], tk[:, :, 47]), r=["tk"], w=["thrT"])
            g3 = s1b[:, 0:128].rearrange("p (h r) -> p h r", h=8)
            P.dve(lambda e: e.tensor_tensor(g3, tk[:, :, 32:48], tk[:, :, 32:33].to_broadcast([128, 8, 16]), ALU.subtract), r=["tk"], w=["ar4"])
            P.act(lambda e: e.activation(out=s1b[:, 0:128], in_=s1b[:, 0:128], func=AF.Exp), r=["ar4"], w=["ar4"])
            P.dve(lambda e: e.tensor_reduce(out=sm[:, 0:8], in_=g3, axis=AX.X, op=ALU.add), r=["ar4"], w=["smg"])
            P.act(lambda e: e.activation(out=sm[:, 0:8], in_=sm[:, 0:8], func=AF.Ln), r=["smg"], w=["smg"])
            P.dve(lambda e, t=t: e.scalar_tensor_tensor(biasT[:, t, :], sm[:, 0:8], -1.0, tk[:, :, 32], ALU.mult, ALU.subtract), r=["smg", "tk"], w=["biasT"])
        if peer_stop == 2:
            return
        do_barrier()
        ur = [wb[i // 2][:, :, :].rearrange("p k n -> p (k n)")[:, (i % 2) * 2048:(i % 2 + 1) * 2048] for i in range(4)]
        tmAb = tmA[:, :].bitcast(BF16)
        uT = [tmAb[:, i * 2048:(i + 1) * 2048].rearrange("p (k n) -> p k n", k=16) for i in range(2)]
        vs = [ar[i][:, 0:1024].bitcast(BF16) for i in range(4)]
        As = [ar[6 + i // 2][:, 0:1024].bitcast(BF16)[:, (i % 2) * 1024:(i % 2 + 1) * 1024] for i in range(4)]
        ost = [ar[6][:, 1024:1536], ar[7][:, 1024:1536]]
        tA = ar[8][:, 0:1024]
        gel = ar[8][:, 1024:1536].bitcast(BF16)
        ar4b = ar[4][:, :].bitcast(BF16)
        WT = [[qaT[:, 2, :], qaT[:, 3, :], ckvT[:, 0, 0:1024], ckvT[:, 1, 0:1024]],
              [ar4b[:, 0:1024], ar4b[:, 1024:2048], ar4b[:, 2048:3072], ar[5][:, 0:512].bitcast(BF16)]]
        s2sb = tmB[:, :].bitcast(F32).rearrange("p (h n) -> p h n", h=8)
        sums = [vh[:, :, :].rearrange("p a b -> p (a b)").bitcast(F32)[:, 0:512], qnT[:, :].bitcast(F32), ar[5][:, 512:1024], ar[5][:, 1024:1536]]
        es = [pT[0][:, :], pT[1][:, :], knT[:, 0:512], knT[:, 512:1024]]
        cTbf = cTb[:, :, :].rearrange("p k n -> p (k n)")
        Wh = [cTbf[:, i * 512:(i + 1) * 512] for i in range(4)]
        NB = 4
        NG = 32 if peer_stop < 3 else (1 if peer_stop == 3 else 2)

        def wgen_scores(g, t):
            ts_ = slice(t * 128, (t + 1) * 128)
            sb_ = t % 2
            for hh in range(2):
                for hq in range(4):
                    h = hh * 4 + hq
                    P.pe(lambda e, h=h, hq=hq, ts_=ts_: e.matmul(bank[4][:, hq * 128:(hq + 1) * 128], qT2[:, h, ts_], keysT[:, 2 * h + 1, :], start=True, stop=True),
                         r=["qT2", "keysT"], w=["bank4"])
                P.act(lambda e, hh=hh: e.copy(s2sb[:, hh * 4:(hh + 1) * 4, :], bank[4][:, :].rearrange("p (h n) -> p h n", h=4)), r=["bank4"], w=["s2sb"])
            for h in range(8):
                P.pe(lambda e, h=h, ts_=ts_, g=g: e.matmul(bank[4][:, h * 4:h * 4 + 4], qT1[:, h, ts_], keysT[:, 2 * h, 4 * g:4 * g + 4], start=True, stop=True),
                     r=["mixS", "keysT"], w=["bank4"])
            P.act(lambda e, sb_=sb_: e.copy(s1sb[:, sb_, :, :], bank[4][:, 0:32].rearrange("p (h n) -> p h n", h=8)), r=["bank4"], w=["s1sb%d" % sb_])

        def wgen_gates(g, t):
            ts_ = slice(t * 128, (t + 1) * 128)
            sb_ = t % 2
            wb_ = g % 2
            P.pe(lambda e: e.matmul(bank[2][:], zer[:], hT[:, 0, 0:512], start=True, stop=False), r=["zer", "hT"], w=["bank2"])

            def emit_add(h):
                i = (t * 8 + h) % NB
                sm3 = sums[i].rearrange("p (c k) -> p c k", c=4)
                addeng = P.pool if h in POOL_HEAD_SET else P.dve
                addeng(lambda e, h=h, sm3=sm3, sb_=sb_: e.tensor_tensor(sm3, s1sb[:, sb_, h, :].unsqueeze(2).to_broadcast([128, 4, 128]),
                                                                   s2sb[:, h, :].unsqueeze(1).to_broadcast([128, 4, 128]), ALU.add),
                       r=["s1sb%d" % sb_, "s2sb"], w=["sum%d" % i])

            def emit_rest(h):
                i = (t * 8 + h) % NB
                P.act(lambda e, h=h, i=i, t=t: e.activation(out=es[i], in_=sums[i], func=AF.Exp, bias=biasT[:, t, h:h + 1]), r=["sum%d" % i, "biasT"], w=["e%d" % i])
                P.dve(lambda e, h=h, i=i, t=t: e.scalar_tensor_tensor(Wh[i], sums[i], thrT[:, t, h:h + 1], es[i], ALU.is_ge, ALU.mult),
                      r=["sum%d" % i, "e%d" % i, "thrT"], w=["Wh%d" % i])
                for ci in range(4):
                    P.pe(lambda e, h=h, i=i, ci=ci: e.matmul(bank[2][:, ci * 128:(ci + 1) * 128], Wh[i][:, ci * 128:(ci + 1) * 128], ident[:],
                                                             start=False, stop=(h == 7 and ci == 3)), r=["Wh%d" % i, "ident"], w=["bank2"])
            AHEAD = 2
            for h in range(AHEAD):
                emit_add(h)
            for h in range(8):
                if h + AHEAD < 8:
                    emit_add(h + AHEAD)
                emit_rest(h)
            for ci in range(4):
                P.act(lambda e, ci=ci, ts_=ts_, wb_=wb_: e.copy(WT[wb_][ci][:, ts_], bank[2][:, ci * 128:(ci + 1) * 128]), r=["bank2"], w=["WT%d_%d" % (wb_, ci)])

        def chunk(g, ci):
            c = 4 * g + ci
            ui, vi, ti = c % 4, c % 4, c % 2
            wb_ = g % 2
            row0 = l * 16384 + c * 128
            P.dma(lambda e, ui=ui, row0=row0: e.dma_start(out=ur[ui], in_=dr["peer_u"][row0:row0 + 128, :]), w=["ur%d" % ui], eng="pool")
            P.dma(lambda e, vi=vi, row0=row0: e.dma_start(out=vs[vi], in_=dr["peer_v"][row0:row0 + 128, :]), w=["vs%d" % vi], eng="pool")
            P.pool(lambda e, vi=vi: e.tensor_tensor(vs[vi], vs[vi], modG[:, :], ALU.mult), r=["vs%d" % vi, "modG"], w=["vs%d" % vi])
            for half in range(2):
                bb = bankb[half]
                bbk = "bankb%d" % half
                for k in range(8):
                    kk_ = half * 8 + k
                    P.pe(lambda e, bb=bb, k=k, kk_=kk_, ui=ui: e.transpose(bb[:, k * 128:(k + 1) * 128], ur[ui][:, kk_ * 128:(kk_ + 1) * 128], ident[:]),
                         r=["ur%d" % ui, "ident"], w=[bbk])
                if half == 0:
                    P.act(lambda e, bb=bb, ti=ti: e.copy(uT[ti][:, 0:8, :], bb[:, :].rearrange("p (c n) -> p c n", c=8)), r=[bbk], w=["uT%d" % ti])
                else:
                    P.dve(lambda e, bb=bb, ti=ti: e.tensor_copy(uT[ti][:, 8:16, :], bb[:, :].rearrange("p (c n) -> p c n", c=8)), r=[bbk], w=["uT%d" % ti])
            for blk in range(2):
                bs = slice(blk * 512, (blk + 1) * 512)
                for k in range(16):
                    P.pe(lambda e, blk=blk, k=k, ti=ti, bs=bs: e.matmul(bank[blk][:], uT[ti][:, k, :], hT[:, k, bs], start=(k == 0), stop=(k == 15)),
                         r=["uT%d" % ti, "hT"], w=["bank%d" % blk])
                bkk = "bank%d" % blk
                tk_ = "tA%d" % blk
                P.act(lambda e, blk=blk, bs=bs: e.activation(out=tA[:, bs], in_=bank[blk][:], func=AF.Square), r=[bkk], w=[tk_])
                P.dve(lambda e, bs=bs: e.tensor_scalar(tA[:, bs], tA[:, bs], 0.044715, 1.0, op0=ALU.mult, op1=ALU.add), r=[tk_], w=[tk_])
                P.dve(lambda e, blk=blk, bs=bs: e.tensor_tensor(tA[:, bs], tA[:, bs], bank[blk][:], ALU.mult), r=[tk_, bkk], w=[tk_])
                P.act(lambda e, bs=bs: e.activation(out=tA[:, bs], in_=tA[:, bs], func=AF.Sigmoid, scale=GELU_C), r=[tk_], w=[tk_])
                P.dve(lambda e, blk=blk, bs=bs: e.tensor_tensor(gel[:, bs], tA[:, bs], bank[blk][:], ALU.mult), r=[tk_, bkk], w=["gel%d" % blk])
                P.dve(lambda e, ci=ci, bs=bs, wb_=wb_: e.tensor_tensor(As[ci][:, bs], gel[:, bs], WT[wb_][ci][:, bs], ALU.mult),
                      r=["gel%d" % blk, "WT%d_%d" % (wb_, ci)], w=["As%d" % ci])

        def outs_tile(g, t):
            ts_ = slice(t * 128, (t + 1) * 128)
            for cb in range(4):
                ob = 3 if cb % 2 == 0 else 5
                for ci in range(4):
                    vi = (4 * g + ci) % 4
                    P.pe(lambda e, ci=ci, vi=vi, ts_=ts_, cb=cb, ob=ob: e.matmul(bank[ob][:], As[ci][:, ts_], vs[vi][:, cb * 512:(cb + 1) * 512], start=(ci == 0), stop=(ci == 3)),
                         r=["As%d" % ci, "vs%d" % vi], w=["bank%d" % ob])
                oi = (t * 4 + cb) % 2
                P.act(lambda e, oi=oi, ob=ob: e.copy(ost[oi], bank[ob][:]), r=["bank%d" % ob], w=["ost%d" % oi])
                P.dma(lambda e, oi=oi, t=t, cb=cb: e.dma_start(out=xd[t][:, cb * 512:(cb + 1) * 512], in_=ost[oi], accum_op=ALU.add),
                      r=["ost%d" % oi], w=["xd%d_%d" % (t, cb)], eng="pool")

        for t in range(NT):
            wgen_scores(0, t)
            wgen_gates(0, t)
        for g in range(NG):
            nxt = g + 1 < NG
            if nxt:
                wgen_scores(g + 1, 0)
            for s_ in range(8):
                if s_ < 4:
                    chunk(g, s_)
                else:
                    outs_tile(g, 2 * (s_ - 4))
                    outs_tile(g, 2 * (s_ - 4) + 1)
                if nxt:
                    wgen_gates(g + 1, s_)
                    if s_ + 1 < 8:
                        wgen_scores(g + 1, s_ + 1)
        do_barrier()

    for l in range(n_layers):
        if do_mixer:
            ada_mod(l, 0)
            if o_dbg is not None and l == 0:
                for i, (src, k_) in enumerate(((modG[:, :], "modG"), (modA, "modA"), (modB, "modB"))):
                    for hf in range(2):
                        P.dma(lambda e, i=i, hf=hf, src=src: e.dma_start(out=o_dbg[16 + 2 * i + hf, :, 0:1024], in_=src[:, hf * 1024:(hf + 1) * 1024]), r=[k_])
            norm_to_hT()
            lru_branch(l)
            conv_branch(l)
            attention(l)
            out_proj(l)
        if do_peer:
            if peer_mode == 'dense':
                peer_dense(l)
            else:
                ada_mod(l, 1)
                peer(l)
    P.build(st)
    return nc, st


def _rope_tables(latent):
    cc = np.ones((64, NK), np.float32)
    ss = np.zeros((64, NK), np.float32)
    if latent:
        n = 1024
        row = np.repeat(np.arange(n // 64), 64).astype(np.float32)
        col = np.tile(np.arange(64), n // 64).astype(np.float32)
        inv = (1.0 / (np.float32(10000.0) ** (np.arange(16, dtype=np.float32) / np.float32(16)))).astype(np.float32)
        ang = np.concatenate([row[:, None] * inv, col[:, None] * inv], -1).astype(np.float32)
        c, s = np.cos(ang).T.astype(np.float32), np.sin(ang).T.astype(np.float32)
        cc[0:32, 512:] = c
        cc[32:64, 512:] = c
        ss[0:32, 512:] = -s
        ss[32:64, 512:] = s
    return cc, ss


def _mask_bias(latent):
    m = np.zeros((128, NKT * 4), np.float32)
    if not latent:
        for kt in range(NKT):
            for qb in range(4):
                ok = kt >= 4 and (kt - 4) // 2 == qb
                m[:, kt * 4 + qb] = 0.0 if ok else NEG
    return m


def make_in_maps(inp, cores, with_uv=True):
    in_maps = []
    for core in cores:
        latent = core >= 4
        m = {}
        if latent:
            b = core - 4
            m["x"] = inp["x_sample"][b]
            m["cvec"] = np.ascontiguousarray(inp["c"][b].reshape(16, 128).T)
            m["cache_ckv"] = inp["cache_ckv"][b]
            m["cache_krope"] = inp["cache_krope"][b]
            h0 = inp["state_lru"][b]
        else:
            m["x"] = np.ascontiguousarray(inp["x_prompt"][core * 4:(core + 1) * 4].reshape(T, D))
            m["cvec"] = np.ascontiguousarray(inp["c_ctx"].reshape(16, 128).T)
            m["cache_ckv"] = np.zeros((L, 512, 256), np.float32)
            m["cache_krope"] = np.zeros((L, 512, 64), np.float32)
            h0 = np.zeros((L, 2, 512), np.float32)
        m["h0"] = np.ascontiguousarray(h0.reshape(L, 2, 4, 128).transpose(3, 0, 1, 2).reshape(128, L * 2 * 4))
        m["CC"], m["SS"] = _rope_tables(latent)
        m["maskb"] = _mask_bias(latent)
        m["flag"] = np.full((128, 1), 1.0 if latent else 0.0, np.float32)
        m["iota16"] = np.ascontiguousarray(np.broadcast_to(np.arange(16, dtype=np.float32), (128, 16)))
        for wn in W_NAMES:
            if wn in ("peer_u", "peer_v"):
                if with_uv:
                    m[wn] = inp[wn].reshape(L * 16384, D)
                continue
            m[wn] = inp[wn]
        in_maps.append(m)
    return in_maps


def kernel(**inputs):
    inp = {k: np.ascontiguousarray(np.asarray(v)) for k, v in inputs.items()}
    in_maps = make_in_maps(inp, list(range(8)))
    shapes = {k: (v.shape, F32) for k, v in in_maps[0].items()}
    nc, st = build_program(shapes)
    with st:
        res = run_bass_kernel_spmd(nc, in_maps, core_ids=list(range(8)))
    r = res.results
    y_prompt = np.concatenate([r[c]["o_y"].reshape(4, 256, D) for c in range(4)], 0)
    y_sample = np.stack([r[c]["o_y"] for c in range(4, 8)], 0)
    new_ckv = np.concatenate([r[c]["o_ckv"].reshape(L, 4, 256, 256).transpose(1, 0, 2, 3) for c in range(4)], 0)
    new_kr = np.concatenate([r[c]["o_kr"].reshape(L, 4, 256, 64).transpose(1, 0, 2, 3) for c in range(4)], 0)
    new_lru = np.concatenate([r[c]["o_lru"].transpose(1, 0, 2, 3) for c in range(4)], 0)
    return (np.ascontiguousarray(y_prompt, np.float32), np.ascontiguousarray(y_sample, np.float32),
            np.ascontiguousarray(new_ckv, np.float32), np.ascontiguousarray(new_kr, np.float32),
            np.ascontiguousarray(new_lru, np.float32))
```

```python
import numpy as np
from contextlib import ExitStack
import concourse.bass as bass
import concourse.mybir as mybir
from concourse.bass_utils import run_bass_kernel_spmd

F32 = mybir.dt.float32
BF16 = mybir.dt.bfloat16
I32 = mybir.dt.int32
U32 = mybir.dt.uint32
ALU = mybir.AluOpType
AF = mybir.ActivationFunctionType
AX = mybir.AxisListType

D = 2048
L = 2
NT = 8
T = 1024
NK = 1536
NKT = 12
IN_W = 2880
EPS = 1e-6
NEG = -30000.0

ENGS = ("pe", "act", "dve", "pool", "sp")
N_DMA_SEM = {"sp": 12, "pool": 6, "act": 4}


class Op:
    __slots__ = ("eng", "fn", "reads", "writes", "dma", "deps", "sig", "idx", "dsem", "dval", "prev_same", "barrier")

    def __init__(self, eng, fn, reads, writes, dma):
        self.eng, self.fn, self.reads, self.writes, self.dma = eng, fn, reads, writes, dma
        self.deps = set()
        self.sig = None
        self.dsem = None
        self.dval = None
        self.prev_same = None
        self.barrier = False


class Prog:
    def __init__(self, nc):
        self.nc = nc
        self.ops = []
        self.overlaps = {}

    def alias(self, a, others):
        for b in others:
            self.overlaps.setdefault(a, set()).add(b)
            self.overlaps.setdefault(b, set()).add(a)

    def add(self, eng, fn, reads=(), writes=(), dma=False):
        o = Op(eng, fn, tuple(reads), tuple(writes), dma)
        o.idx = len(self.ops)
        self.ops.append(o)
        return o

    def pe(self, fn, r=(), w=()):
        return self.add("pe", fn, r, w)

    def act(self, fn, r=(), w=()):
        return self.add("act", fn, r, w)

    def dve(self, fn, r=(), w=()):
        return self.add("dve", fn, r, w)

    def pool(self, fn, r=(), w=()):
        return self.add("pool", fn, r, w)

    def dma(self, fn, r=(), w=(), eng="sp"):
        return self.add(eng, fn, r, w, dma=True)

    def barrier(self, fns):
        for eng, fn in fns.items():
            o = self.add(eng, fn, (), ())
            o.barrier = True

    def build(self, stack):
        nc = self.nc
        ops = self.ops
        last_w = {}
        readers = {}
        ov = self.overlaps
        for o in ops:
            deps = set()
            for k0 in o.reads:
                for k in (k0, *ov.get(k0, ())):
                    if k in last_w:
                        deps.add(last_w[k])
            for k0 in o.writes:
                for k in (k0, *ov.get(k0, ())):
                    if k in last_w:
                        deps.add(last_w[k])
                    for rd in readers.get(k, ()):
                        deps.add(rd)
            if o.barrier:
                seen_e = {}
                seen_q = {}
                for p in ops[:o.idx]:
                    if p.dma:
                        seen_q.setdefault(p.eng, []).append(p.idx)
                    elif p.eng != "sp":
                        seen_e[p.eng] = p.idx
                deps |= set(seen_e.values())
                for q, lst in seen_q.items():
                    deps |= set(lst[-N_DMA_SEM[q]:])
            deps.discard(o.idx)
            o.deps = deps
            for k in o.writes:
                last_w[k] = o.idx
                readers[k] = []
            for k in o.reads:
                readers.setdefault(k, []).append(o.idx)
        need = [False] * len(ops)
        for o in ops:
            latest = {}
            for d in o.deps:
                p = ops[d]
                if p.dma:
                    continue
                if p.eng == o.eng and p.eng == "pe" and not o.dma:
                    continue
                if d > latest.get(p.eng, -1):
                    latest[p.eng] = d
            for d in latest.values():
                need[d] = True
        esem = {e: stack.enter_context(nc.semaphore("s_" + e)) for e in ENGS}
        dsems = {e: [stack.enter_context(nc.semaphore("d_%s%d" % (e, i))) for i in range(n)]
                 for e, n in N_DMA_SEM.items()}
        cnt = {e: 0 for e in ENGS}
        dcnt = {e: 0 for e in N_DMA_SEM}
        prev_on_sem = {}
        for o in ops:
            if o.dma:
                k = dcnt[o.eng]
                n = N_DMA_SEM[o.eng]
                o.dsem = dsems[o.eng][k % n]
                o.dval = 16 * (k // n + 1)
                o.prev_same = prev_on_sem.get((o.eng, k % n))
                prev_on_sem[(o.eng, k % n)] = o.idx
                dcnt[o.eng] += 1
            elif need[o.idx]:
                cnt[o.eng] += 1
                o.sig = cnt[o.eng]
        global SEM_COUNTS
        SEM_COUNTS = (dict(cnt), dict(dcnt), len(ops))
        print('SEM_COUNTS', SEM_COUNTS, flush=True)
        by_eng = {e: [o for o in ops if o.eng == e] for e in ENGS}
        block = stack.enter_context(nc.Block())

        def emit(ename):
            def body(eng):
                waited = {}

                def wait(sem, val):
                    key = id(sem)
                    if waited.get(key, 0) >= val:
                        return
                    waited[key] = val
                    eng.wait_ge(sem, val)

                issued = []
                for o in by_eng[ename]:
                    if o.dma and o.prev_same is not None:
                        p = ops[o.prev_same]
                        wait(p.dsem, p.dval)
                    for d in sorted(o.deps):
                        p = ops[d]
                        if p.dma:
                            wait(p.dsem, p.dval)
                        else:
                            if p.sig is None:
                                continue
                            if p.eng == ename and ename == "pe" and not o.dma:
                                continue
                            wait(esem[p.eng], p.sig)
                    ins = o.fn(eng)
                    if o.dma:
                        ins.then_inc(o.dsem, 16)
                        issued.append(o)
                    elif o.sig is not None:
                        ins.then_inc(esem[ename], 1)
                for o in issued[-N_DMA_SEM.get(ename, 0):]:
                    wait(o.dsem, o.dval)
            return body

        block.tensor(emit("pe"))
        block.scalar(emit("act"))
        block.vector(emit("dve"))
        block.gpsimd(emit("pool"))
        block.sync(emit("sp"))


W_NAMES = ['ada_w', 'ada_b', 'norm_mix_g', 'w_in', 'mla_qa_g', 'mla_w_qb', 'mla_q_g', 'mla_kva_g', 'mla_w_kvb',
           'mla_k_g', 'lru_conv_w', 'lru_conv_b', 'lru_w_a', 'lru_b_a', 'lru_w_i', 'lru_b_i', 'lru_lam',
           'cm_dw_w', 'cm_dw_b', 'cm_ln_g', 'cm_ln_b', 'grp_g', 'w_out', 'norm_ffn_g', 'peer_w_q', 'peer_keys',
           'peer_u', 'peer_v']
GELU_C = 1.5957691216057308
POOL_HEADS = 0
POOL_HEAD_SET = (1, 3, 5, 7)


def build_program(shapes, n_layers=L, do_mixer=True, do_peer=True, dbg=None, peer_mode='gather', peer_stop=0):
    nc = bass.Bass("TRN2", target_bir_lowering=False)
    st = ExitStack()
    P = Prog(nc)
    dr = {}
    for name, (shape, dt) in shapes.items():
        dr[name] = nc.dram_tensor(name, list(shape), dt, kind="ExternalInput").ap()
    o_y = nc.dram_tensor("o_y", [T, D], F32, kind="ExternalOutput").ap()
    o_ckv = nc.dram_tensor("o_ckv", [L, T, 256], F32, kind="ExternalOutput").ap()
    o_kr = nc.dram_tensor("o_kr", [L, T, 64], F32, kind="ExternalOutput").ap()
    o_lru = nc.dram_tensor("o_lru", [L, 4, 2, 512], F32, kind="ExternalOutput").ap()
    o_dbg = nc.dram_tensor("o_dbg", list(dbg), F32, kind="ExternalOutput").ap() if dbg else None
    xd = o_y.rearrange("(t p) d -> t p d", p=128)

    def sb(name, shape, dt=F32):
        return st.enter_context(nc.sbuf_tensor("sb_" + name, shape, dt))

    def ps(name, shape, dt=F32):
        return st.enter_context(nc.psum_tensor("ps_" + name, shape, dt))

    big = sb("big", [128, 8192])
    hT = big[:, :].bitcast(BF16).rearrange("p (c n) -> p c n", c=16)
    ug = [big[:, i * D:(i + 1) * D] for i in range(3)]
    accp = big[:, 3 * D:4 * D]
    bigb = big[:, :].bitcast(BF16)
    ugb = [bigb[:, i * D:(i + 1) * D] for i in range(6)]
    P.alias("hT", ["ug0", "ug1", "ug2", "accp", "ugb0", "ugb1", "ugb2", "ugb3", "ugb4", "ugb5", "junkb"])
    mixS = sb("mixS", [128, 8, T], BF16)
    vgb = ugb
    junkb = bigb[:, 7 * D:8 * D]
    P.alias("accp", ["junkb"])
    modAB = sb("modAB", [128, 2 * D])
    modA, modB = modAB[:, 0:D], modAB[:, D:2 * D]
    kst = modAB[:, 0:8 * 320].rearrange("p (t d) -> p t d", t=8)
    P.alias("kst", ["modA", "modB"])
    modG = sb("modG", [128, D])
    wb = [sb("wb%d" % i, [128, 16, 256], BF16) for i in range(2)]
    tmA = sb("tmA", [128, D])
    tmB = sb("tmB", [128, D], BF16)
    ar = [sb("ar%d" % i, [128, NK]) for i in range(9)]
    ark = ["ar%d" % i for i in range(9)]
    CC = sb("CC", [64, NK])
    SS = sb("SS", [64, NK])
    qaT = sb("qaT", [128, 4, T], BF16)
    ckvT = sb("ckvT", [128, 2, NK], BF16)
    knT = sb("knT", [128, NK], BF16)
    krT = sb("krT", [64, NK], BF16)
    qnT = sb("qnT", [128, T], BF16)
    qrT = sb("qrT", [64, T], BF16)
    vh = sb("vh", [128, NKT, 128], BF16)
    pT = [sb("pT%d" % i, [128, 512], BF16) for i in range(2)]
    keysT = qaT[:, 0:2, :].rearrange("p a (b n) -> p (a b) n", b=8)
    P.alias("qaT", ["keysT"])
    qTj = sb("qTj", [128, 2, 128], BF16)
    ident = sb("ident", [128, 128], BF16)
    identf = sb("identf", [128, 128])
    ones = sb("ones", [128, 128], BF16)
    onesf = sb("onesf", [128, 128])
    cT = sb("cT", [128, 16])
    cTb = sb("cTb", [128, 16, 128], BF16)
    sm = sb("sm", [128, 16])
    vec = sb("vec", [128, 96])
    maskb = sb("maskb", [128, 48])
    flag = sb("flag", [128, 1])
    h0t = sb("h0t", [128, L * 2 * 4])
    taps = sb("taps", [128, 4, 31])
    bwt = [sb("bwt%d" % i, [128, 128], BF16) for i in range(2)]
    bwf = sb("bwf", [128, 128])
    iot = sb("iot", [128, 16])
    pk = ckvT[:, 0, :].bitcast(F32).rearrange("p (s n) -> p s n", s=6)
    tk = ckvT[:, 1, 0:1024].bitcast(F32).rearrange("p (h n) -> p h n", h=8)
    tki = knT[:, 0:768].bitcast(U32).rearrange("p (h n) -> p h n", h=8)
    pki = knT[:, 768:1024].bitcast(I32)
    P.alias("ckvT", ["pk0", "pk1", "pk2", "pk3", "pk4", "pk5", "tk"])
    P.alias("knT", ["tki", "pki"])
    thrT = sb("thrT", [128, 8, 8])
    biasT = sb("biasT", [128, 8, 8])
    s1sb = sb("s1sb", [128, 2, 8, 4])
    bsc = sb("bsc", [128, 4])
    dgr = [sb("dgr%d" % i, [128, 128], BF16) for i in range(4)]
    zer = sb("zer", [128, 128], BF16)
    bank = [ps("bank%d" % i, [128, 512]) for i in range(6)]
    bankb = [ps("bankb%d" % i, [128, 1024], BF16) for i in range(2)]

    wb_i = [0]

    def next_wb():
        i = wb_i[0] % 2
        wb_i[0] += 1
        return wb[i], "wb%d" % i

    P.pool(lambda e: e.memset(identf[:], 0.0), w=["identf"])
    P.pool(lambda e: e.affine_select(out=identf[:], in_=identf[:], pattern=[[-1, 128]], compare_op=ALU.not_equal,
                                     fill=1.0, base=0, channel_multiplier=1), r=["identf"], w=["identf"])
    P.dve(lambda e: e.tensor_copy(ident[:], identf[:]), r=["identf"], w=["ident"])
    P.dve(lambda e: e.memset(onesf[:], 1.0), w=["onesf"])
    P.dve(lambda e: e.memset(bsc[:], 0.0), w=["bsc"])
    P.dve(lambda e: e.memset(zer[:], 0.0), w=["zer"])
    P.dve(lambda e: e.memset(ones[:], 1.0), w=["ones"])
    for t in range(NT):
        P.dma(lambda e, t=t: e.dma_start(out=xd[t], in_=dr["x"][t * 128:(t + 1) * 128, :]), w=["xd%d" % t])
    P.dma(lambda e: e.dma_start(out=maskb[:], in_=dr["maskb"]), w=["maskb"])
    P.dma(lambda e: e.dma_start(out=flag[:], in_=dr["flag"]), w=["flag"])
    P.dma(lambda e: e.dma_start(out=CC[:], in_=dr["CC"]), w=["CC"])
    P.dma(lambda e: e.dma_start(out=SS[:], in_=dr["SS"]), w=["SS"])
    P.dma(lambda e: e.dma_start(out=h0t[:], in_=dr["h0"]), w=["h0t"])
    P.dma(lambda e: e.dma_start(out=iot[:], in_=dr["iota16"]), w=["iot"])
    P.dma(lambda e: e.dma_start(out=cT[:], in_=dr["cvec"]), w=["cT"])
    P.act(lambda e: e.activation(out=cT[:], in_=cT[:], func=AF.Silu), r=["cT"], w=["cT"])
    P.dve(lambda e: e.tensor_copy(cTb[:], cT[:, :].unsqueeze(2).to_broadcast([128, 16, 128])), r=["cT"], w=["cTb"])

    dbg_n = [0]

    def dump(src_ap, key, ncols):
        if o_dbg is None:
            return
        o = dbg_n[0]
        dbg_n[0] += 1
        P.dma(lambda e: e.dma_start(out=o_dbg[o, :, 0:ncols], in_=src_ap), r=[key])

    def vcol(i, n=128):
        return vec[0:n, i:i + 1]

    def load_vec(i, ap_1d, n=128):
        P.dma(lambda e: e.dma_start(out=vec[0:n, i:i + 1], in_=ap_1d.rearrange("(p o) -> p o", o=1)), w=["vec%d" % i])

    def load_cols(wname, l, c0, ncols, kdim):
        w_t, wk = next_wb()
        kc = kdim // 128
        P.dma(lambda e: e.dma_start(out=w_t[:, 0:kc, 0:ncols],
                                    in_=dr[wname][l, :, c0:c0 + ncols].rearrange("(k p) n -> p k n", p=128)),
              w=[wk], eng="pool")
        return w_t, wk

    def proj_fm(w_t, wk, col, m, src, srck, kc, dst_fn, nblk=2, bi0=2):
        for blk in range(nblk):
            bi = bi0 + blk % 2
            bk = bank[bi]
            bkk = "bank%d" % bi
            for k in range(kc):
                P.pe(lambda e, k=k, bk=bk, blk=blk: e.matmul(bk[0:m, :], w_t[:, k, col:col + m], src[:, k, blk * 512:(blk + 1) * 512],
                                                            start=(k == 0), stop=(k == kc - 1)), r=[wk, srck], w=[bkk])
            dst_fn(blk, bk, bkk)

    def rstd_rows(srcs, dst, dstk, nfeat, eps, nblk=2):
        n = len(srcs)
        for blk in range(nblk):
            bi = 4 + blk % 2
            bk = bank[bi]
            bkk = "bank%d" % bi
            for i, (fn, key, m) in enumerate(srcs):
                sq = tmA[0:m, (i % 2) * 512:(i % 2) * 512 + 512]
                sqk = "tmA%d" % (i % 2)
                P.act(lambda e, fn=fn, sq=sq, blk=blk: e.activation(out=sq, in_=fn(blk), func=AF.Square), r=[key], w=[sqk])
                P.pe(lambda e, sq=sq, m=m, i=i, bk=bk: e.matmul(bk[:], onesf[0:m, :], sq, start=(i == 0), stop=(i == n - 1)),
                     r=[sqk, "onesf"], w=[bkk])
            P.act(lambda e, bk=bk, blk=blk: e.activation(out=dst[:, blk * 512:(blk + 1) * 512], in_=bk[:], func=AF.Sqrt,
                                                        bias=eps, scale=1.0 / nfeat), r=[bkk], w=[dstk])
        P.dve(lambda e: e.reciprocal(dst[:, 0:nblk * 512], dst[:, 0:nblk * 512]), r=[dstk], w=[dstk])
    P.alias("tmA", ["tmA0", "tmA1"])
    P.alias("smg", ["sm0", "sm1", "sm2", "sm3"])

    def ada_mod(l, part):
        gname = "norm_mix_g" if part == 0 else "norm_ffn_g"
        P.dve(lambda e: e.tensor_copy(cTb[:], cT[:, :].unsqueeze(2).to_broadcast([128, 16, 128])), r=["cT"], w=["cTb"])
        for j, (dst, dname) in enumerate(((modB, "modB"), (modA, "modA"), (modG[:, :], "modG"))):
            c0 = (part * 3 + j) * D
            for cb in range(8):
                w_t, wk = load_cols("ada_w", l, c0 + cb * 256, 256, D)
                bi = cb % 2
                bk = bank[bi]
                bkk = "bank%d" % bi
                for k in range(16):
                    P.pe(lambda e, w_t=w_t, k=k, bk=bk: e.matmul(bk[:, 0:256], cTb[:, k, :], w_t[:, k, :], start=(k == 0), stop=(k == 15)),
                         r=["cTb", wk], w=[bkk])
                P.act(lambda e, dst=dst, cb=cb, bk=bk: e.copy(dst[:, cb * 256:(cb + 1) * 256], bk[:, 0:256]), r=[bkk], w=[dname])
            P.dma(lambda e, c0=c0: e.dma_start(out=tmA[:], in_=dr["ada_b"][l:l + 1, c0:c0 + D].partition_broadcast(128)), w=["tmA"])
            P.dve(lambda e, dst=dst: e.tensor_tensor(dst, dst, tmA[:], ALU.add), r=[dname, "tmA"], w=[dname])
        P.dma(lambda e: e.dma_start(out=tmA[:], in_=dr[gname][l:l + 1, :].partition_broadcast(128)), w=["tmA"])
        P.dve(lambda e: e.scalar_tensor_tensor(modA, modA, 1.0, tmA[:], ALU.add, ALU.mult), r=["modA", "tmA"], w=["modA"])

    def norm_tile(t):
        for hf in range(2):
            P.dma(lambda e, hf=hf: e.dma_start(out=ar[hf][:, 0:1024], in_=xd[t][:, hf * 1024:(hf + 1) * 1024]), r=["xd%d" % t], w=[ark[hf]])
            P.act(lambda e, hf=hf: e.activation(out=tmB[:, hf * 1024:(hf + 1) * 1024], in_=ar[hf][:, 0:1024], func=AF.Square, accum_out=sm[:, hf:hf + 1]),
                  r=[ark[hf]], w=["tmB", "sm%d" % hf])
        P.dve(lambda e: e.tensor_tensor(sm[:, 2:3], sm[:, 0:1], sm[:, 1:2], ALU.add), r=["sm0", "sm1"], w=["sm2"])
        P.act(lambda e: e.activation(out=sm[:, 3:4], in_=sm[:, 2:3], func=AF.Sqrt, bias=EPS, scale=1.0 / D), r=["sm2"], w=["sm3"])
        P.dve(lambda e: e.reciprocal(sm[:, 3:4], sm[:, 3:4]), r=["sm3"], w=["sm3"])
        for hf in range(2):
            hs = slice(hf * 1024, (hf + 1) * 1024)
            P.dve(lambda e, hf=hf, hs=hs: e.scalar_tensor_tensor(tmA[:, hs], ar[hf][:, 0:1024], sm[:, 3:4], modA[:, hs], ALU.mult, ALU.mult),
                  r=[ark[hf], "sm3", "modA"], w=["tmA"])
        P.dve(lambda e: e.tensor_tensor(tmA[:], tmA[:], modB, ALU.add), r=["tmA", "modB"], w=["tmA"])
        P.act(lambda e: e.copy(tmB[:], tmA[:]), r=["tmA"], w=["tmB"])

    def transpose_tile_to(dst_fn, dstk):
        for half in range(2):
            bb = bankb[half]
            bbk = "bankb%d" % half
            for c in range(8):
                cc = half * 8 + c
                P.pe(lambda e, bb=bb, c=c, cc=cc: e.transpose(bb[:, c * 128:(c + 1) * 128], tmB[:, cc * 128:(cc + 1) * 128], ident[:]),
                     r=["tmB", "ident"], w=[bbk])
            if half == 0:
                P.act(lambda e, bb=bb, half=half: e.copy(dst_fn(half), bb[:, :].rearrange("p (c n) -> p c n", c=8)), r=[bbk], w=[dstk])
            else:
                P.dve(lambda e, bb=bb, half=half: e.tensor_copy(dst_fn(half), bb[:, :].rearrange("p (c n) -> p c n", c=8)), r=[bbk], w=[dstk])

    def norm_to_hT():
        for t in range(NT):
            norm_tile(t)
            transpose_tile_to(lambda half, t=t: hT[:, half * 8:(half + 1) * 8, t * 128:(t + 1) * 128], "hT")

    def gelu_fm(src, srck, tmp, tmpk, dst, dstk, n=T):
        P.dve(lambda e: e.tensor_tensor(tmp[:, 0:n], src[:, 0:n], src[:, 0:n], ALU.mult), r=[srck], w=[tmpk])
        P.dve(lambda e: e.tensor_scalar(tmp[:, 0:n], tmp[:, 0:n], 0.044715, 1.0, op0=ALU.mult, op1=ALU.add), r=[tmpk], w=[tmpk])
        P.dve(lambda e: e.tensor_tensor(tmp[:, 0:n], tmp[:, 0:n], src[:, 0:n], ALU.mult), r=[tmpk, srck], w=[tmpk])
        P.act(lambda e: e.activation(out=tmp[:, 0:n], in_=tmp[:, 0:n], func=AF.Sigmoid, scale=GELU_C), r=[tmpk], w=[tmpk])
        P.dve(lambda e: e.tensor_tensor(dst[:, 0:n], tmp[:, 0:n], src[:, 0:n], ALU.mult), r=[tmpk, srck], w=[dstk])

    def group_norm_inplace(mix, mixk, c0, nch, gcol0, nfeat, rs, rsk):
        rstd_rows([(lambda blk, i=i: mix[:, c0 + i, blk * 512:(blk + 1) * 512], mixk, 128) for i in range(nch)], rs, rsk, nfeat, EPS)
        for i in range(nch):
            P.dve(lambda e, i=i: e.scalar_tensor_tensor(mix[:, c0 + i, :], mix[:, c0 + i, :], vcol(gcol0 + i), rs[:, 0:T], ALU.mult, ALU.mult),
                  r=[mixk, rsk, "vec%d" % (gcol0 + i)], w=[mixk])

    def lru_branch(l):
        for c in range(4):
            for j in range(4):
                load_vec(j * 4 + c, dr["lru_conv_w"][l, j, c * 128:(c + 1) * 128])
            load_vec(16 + c, dr["lru_conv_b"][l, c * 128:(c + 1) * 128])
            for d in range(2):
                load_vec(20 + d * 4 + c, dr["lru_b_a"][l, d, c * 128:(c + 1) * 128])
                load_vec(28 + d * 4 + c, dr["lru_b_i"][l, d, c * 128:(c + 1) * 128])
                load_vec(36 + d * 4 + c, dr["lru_lam"][l, d, c * 128:(c + 1) * 128])
            load_vec(44 + c, dr["grp_g"][l, 1024 + c * 128:1024 + (c + 1) * 128])
        vk = ["vec%d" % i for i in range(48)]
        P.act(lambda e: e.activation(out=vec[:, 36:44], in_=vec[:, 36:44], func=AF.Exp, scale=-1.0), r=vk[36:44], w=vk[36:44])
        P.act(lambda e: e.activation(out=vec[:, 36:44], in_=vec[:, 36:44], func=AF.Ln, bias=1.0), r=vk[36:44], w=vk[36:44])
        P.dve(lambda e: e.tensor_scalar(vec[:, 36:44], vec[:, 36:44], -8.0, None, op0=ALU.mult), r=vk[36:44], w=vk[36:44])
        XP = 259
        xpad, xc, hb, r_t, i_t, a_t, u_t, hfw = ar[0], ar[1], ar[2], ar[3], ar[4], ar[5], ar[6], ar[7]
        xp = xpad[:, 0:4 * XP].rearrange("p (s n) -> p s n", s=4)
        xc3 = xc[:, 0:T].rearrange("p (s n) -> p s n", s=4)
        for c in range(4):
            for cbh in range(1):
                w_x, wxk = load_cols("w_in", l, 832 + c * 128, 128, D)

            def to_pad(blk, bk, bkk):
                P.act(lambda e: e.copy(xp[:, 2 * blk:2 * blk + 2, 2:258], bk[:, :].rearrange("p (s n) -> p s n", s=2)), r=[bkk], w=["ar0"])
            proj_fm(w_x, wxk, 0, 128, hT, "hT", 16, to_pad)
            P.dve(lambda e: e.memset(xp[:, 0:1, 0:2], 0.0), r=["ar0"], w=["ar0"])
            P.dve(lambda e: e.memset(xp[:, 3:4, 258:259], 0.0), r=["ar0"], w=["ar0"])
            P.dve(lambda e: e.tensor_scalar(xp[:, 1:4, 0:2], xp[:, 0:3, 256:258], flag[:, 0:1], None, op0=ALU.mult), r=["ar0", "flag"], w=["ar0"])
            P.dve(lambda e: e.tensor_scalar(xp[:, 0:3, 258:259], xp[:, 1:4, 2:3], flag[:, 0:1], None, op0=ALU.mult), r=["ar0", "flag"], w=["ar0"])
            P.dve(lambda e, c=c: e.tensor_scalar(xc3, xp[:, :, 0:256], vcol(c), vcol(16 + c), op0=ALU.mult, op1=ALU.add),
                  r=["ar0", "vec%d" % c, "vec%d" % (16 + c)], w=["ar1"])
            for j in range(1, 4):
                P.dve(lambda e, c=c, j=j: e.scalar_tensor_tensor(xc3, xp[:, :, j:j + 256], vcol(j * 4 + c), xc3, ALU.mult, ALU.add),
                      r=["ar0", "ar1", "vec%d" % (j * 4 + c)], w=["ar1"])
            xcb = tmB[:, 0:T]
            P.act(lambda e: e.copy(xcb, xc[:, 0:T]), r=["ar1"], w=["tmB"])
            for d in range(2):
                for gi, (gname, bcol, gdst, gk) in enumerate((("lru_w_a", 20, r_t, "ar3"), ("lru_w_i", 28, i_t, "ar4"))):
                    P.pool(lambda e: e.memset(bwf[:], 0.0), w=["bwf"])
                    for hh in range(2):
                        P.dma(lambda e, gname=gname, hh=hh, d=d, c=c: e.dma_start(out=bwf[hh * 64:(hh + 1) * 64, hh * 64:(hh + 1) * 64],
                                                                                 in_=dr[gname][l, d, 2 * c + hh, :, :]), r=["bwf"], w=["bwf"])
                    P.dve(lambda e, gi=gi: e.tensor_copy(bwt[gi][:], bwf[:]), r=["bwf"], w=["bwt%d" % gi])
                    for blk in range(2):
                        bi = 2 + blk
                        bk = bank[bi]
                        bkk = "bank%d" % bi
                        P.pe(lambda e, gi=gi, bk=bk, blk=blk: e.matmul(bk[:], bwt[gi][:], tmB[:, blk * 512:(blk + 1) * 512], start=True, stop=True),
                             r=["bwt%d" % gi, "tmB"], w=[bkk])
                        P.act(lambda e, gdst=gdst, bk=bk, blk=blk, bcol=bcol, d=d, c=c: e.activation(
                            out=gdst[:, blk * 512:(blk + 1) * 512], in_=bk[:], func=AF.Sigmoid, bias=vcol(bcol + d * 4 + c)),
                            r=[bkk, "vec%d" % (bcol + d * 4 + c)], w=[gk])
                P.act(lambda e, d=d, c=c: e.activation(out=a_t[:, 0:T], in_=r_t[:, 0:T], func=AF.Exp, scale=vcol(36 + d * 4 + c)),
                      r=["ar3", "vec%d" % (36 + d * 4 + c)], w=["ar5"])
                P.dve(lambda e: e.tensor_tensor(u_t[:, 0:T], a_t[:, 0:T], a_t[:, 0:T], ALU.mult), r=["ar5"], w=["ar6"])
                P.act(lambda e: e.activation(out=u_t[:, 0:T], in_=u_t[:, 0:T], func=AF.Sqrt, scale=-1.0, bias=1.0), r=["ar6"], w=["ar6"])
                P.dve(lambda e: e.tensor_tensor(u_t[:, 0:T], u_t[:, 0:T], i_t[:, 0:T], ALU.mult), r=["ar6", "ar4"], w=["ar6"])
                P.dve(lambda e: e.tensor_tensor(u_t[:, 0:T], u_t[:, 0:T], xc[:, 0:T], ALU.mult), r=["ar6", "ar1"], w=["ar6"])
                hcol = (l * 2 + d) * 4 + c
                hdst, hk = (hfw, "ar7") if d == 0 else (hb, "ar2")
                order = range(4) if d == 0 else range(3, -1, -1)
                for si, s in enumerate(order):
                    ss_ = slice(s * 256, (s + 1) * 256)
                    if si == 0:
                        init = h0t[:, hcol:hcol + 1]
                        ik = "h0t"
                    else:
                        prev = (s * 256 - 1) if d == 0 else ((s + 1) * 256)
                        P.dve(lambda e, prev=prev, hdst=hdst: e.tensor_scalar(sm[:, 8:9], hdst[:, prev:prev + 1], flag[:, 0:1], None, op0=ALU.mult),
                              r=[hk, "flag"], w=["sm8"])
                        init = sm[:, 8:9]
                        ik = "sm8"
                    if d == 0:
                        P.dve(lambda e, ss_=ss_, init=init, hdst=hdst: e.tensor_tensor_scan(hdst[:, ss_], a_t[:, ss_], u_t[:, ss_], init, ALU.mult, ALU.add),
                              r=["ar5", "ar6", ik, hk], w=[hk])
                    else:
                        rs_ = slice((s + 1) * 256 - 1, s * 256 - 1 if s > 0 else None, -1)
                        P.dve(lambda e, rs_=rs_, init=init, hdst=hdst: e.tensor_tensor_scan(hdst[:, rs_], a_t[:, rs_], u_t[:, rs_], init, ALU.mult, ALU.add),
                              r=["ar5", "ar6", ik, hk], w=[hk])
                for s in range(4):
                    col = s * 256 + 255 if d == 0 else s * 256
                    P.dma(lambda e, c=c, s=s, d=d, col=col, hdst=hdst: e.dma_start(
                        out=o_lru[l, s, d, c * 128:(c + 1) * 128].rearrange("(p o) -> p o", o=1), in_=hdst[:, col:col + 1]), r=[hk])
            P.dve(lambda e: e.tensor_tensor(hfw[:, 0:T], hfw[:, 0:T], hb[:, 0:T], ALU.add), r=["ar7", "ar2"], w=["ar7"])
            w_g, wgk = load_cols("w_in", l, 1344 + c * 128, 128, D)
            zg = r_t

            def to_zg(blk, bk, bkk):
                P.act(lambda e: e.copy(zg[:, blk * 512:(blk + 1) * 512], bk[:]), r=[bkk], w=["ar3"])
            proj_fm(w_g, wgk, 0, 128, hT, "hT", 16, to_zg)
            gelu_fm(zg, "ar3", i_t, "ar4", a_t, "ar5")
            P.dve(lambda e, c=c: e.tensor_tensor(mixS[:, c, :], hfw[:, 0:T], a_t[:, 0:T], ALU.mult), r=["ar7", "ar5"], w=["mixS"])
        group_norm_inplace(mixS, "mixS", 0, 4, 44, 512, ar[8], "ar8")

    def conv_branch(l):
        for c in range(4):
            load_vec(48 + c, dr["cm_dw_b"][l, c * 128:(c + 1) * 128])
            load_vec(52 + c, dr["cm_ln_g"][l, c * 128:(c + 1) * 128])
            load_vec(56 + c, dr["cm_ln_b"][l, c * 128:(c + 1) * 128])
            load_vec(60 + c, dr["grp_g"][l, 1536 + c * 128:1536 + (c + 1) * 128])
        for c in range(4):
            P.dma(lambda e, c=c: e.dma_start(out=taps[:, c, :], in_=dr["cm_dw_w"][l, :, c * 128:(c + 1) * 128].rearrange("j p -> p j"),
                                             allow_slow_non_contiguous=True), r=["taps"], w=["taps"])
        HP = 286
        hp = ar[0]
        hp3 = hp[:, 0:4 * HP].rearrange("p (s n) -> p s n", s=4)
        sgm = ar[1]
        keep = [ar[4 + c] for c in range(4)]
        for c in range(4):
            w_b, wbk = load_cols("w_in", l, 2368 + c * 128, 128, D)

            def to_sig(blk, bk, bkk):
                P.act(lambda e: e.activation(out=sgm[:, blk * 512:(blk + 1) * 512], in_=bk[:], func=AF.Sigmoid), r=[bkk], w=["ar1"])
            proj_fm(w_b, wbk, 0, 128, hT, "hT", 16, to_sig)
            w_a, wak = load_cols("w_in", l, 1856 + c * 128, 128, D)

            def to_glu(blk, bk, bkk):
                P.dve(lambda e: e.tensor_tensor(hp3[:, 2 * blk:2 * blk + 2, 15:271], bk[:, :].rearrange("p (s n) -> p s n", s=2),
                                                sgm[:, blk * 512:(blk + 1) * 512].rearrange("p (s n) -> p s n", s=2), ALU.mult),
                      r=[bkk, "ar1"], w=["ar0"])
            proj_fm(w_a, wak, 0, 128, hT, "hT", 16, to_glu)
            P.dve(lambda e: e.memset(hp3[:, 0:1, 0:15], 0.0), r=["ar0"], w=["ar0"])
            P.dve(lambda e: e.memset(hp3[:, 3:4, 271:286], 0.0), r=["ar0"], w=["ar0"])
            P.dve(lambda e: e.tensor_scalar(hp3[:, 1:4, 0:15], hp3[:, 0:3, 256:271], flag[:, 0:1], None, op0=ALU.mult), r=["ar0", "flag"], w=["ar0"])
            P.dve(lambda e: e.tensor_scalar(hp3[:, 0:3, 271:286], hp3[:, 1:4, 15:30], flag[:, 0:1], None, op0=ALU.mult), r=["ar0", "flag"], w=["ar0"])
            acc3 = keep[c][:, 0:T].rearrange("p (s n) -> p s n", s=4)
            ck = ark[4 + c]
            P.dve(lambda e, c=c, acc3=acc3: e.tensor_scalar(acc3, hp3[:, :, 0:256], taps[:, c, 0:1], vcol(48 + c), op0=ALU.mult, op1=ALU.add),
                  r=["ar0", "taps", "vec%d" % (48 + c)], w=[ck])
            for j in range(1, 31):
                P.dve(lambda e, c=c, j=j, acc3=acc3: e.scalar_tensor_tensor(acc3, hp3[:, :, j:j + 256], taps[:, c, j:j + 1], acc3, ALU.mult, ALU.add),
                      r=["ar0", "taps", ck], w=[ck])
        mu, rs = ar[2], ar[3]
        for blk in range(2):
            bs = slice(blk * 512, (blk + 1) * 512)
            b1, b2 = bank[4], bank[5]
            for c in range(4):
                P.pe(lambda e, c=c, bs=bs: e.matmul(b1[:], onesf[:], keep[c][:, bs], start=(c == 0), stop=(c == 3)), r=[ark[4 + c], "onesf"], w=["bank4"])
            for c in range(4):
                sq = tmA[:, (c % 2) * 512:(c % 2) * 512 + 512]
                sqk = "tmA%d" % (c % 2)
                P.act(lambda e, c=c, bs=bs, sq=sq: e.activation(out=sq, in_=keep[c][:, bs], func=AF.Square), r=[ark[4 + c]], w=[sqk])
                P.pe(lambda e, c=c, sq=sq: e.matmul(b2[:], onesf[:], sq, start=(c == 0), stop=(c == 3)), r=[sqk, "onesf"], w=["bank5"])
            P.act(lambda e, bs=bs: e.activation(out=mu[:, bs], in_=b1[:], func=AF.Copy, scale=1.0 / 512), r=["bank4"], w=["ar2"])
            P.dve(lambda e, bs=bs: e.tensor_tensor(rs[:, bs], mu[:, bs], mu[:, bs], ALU.mult), r=["ar2"], w=["ar3"])
            P.dve(lambda e, bs=bs: e.scalar_tensor_tensor(rs[:, bs], b2[:], 1.0 / 512, rs[:, bs], ALU.mult, ALU.subtract), r=["bank5", "ar3"], w=["ar3"])
        P.act(lambda e: e.activation(out=rs[:, 0:T], in_=rs[:, 0:T], func=AF.Sqrt, bias=1e-5, scale=1.0), r=["ar3"], w=["ar3"])
        P.dve(lambda e: e.reciprocal(rs[:, 0:T], rs[:, 0:T]), r=["ar3"], w=["ar3"])
        for c in range(4):
            ck = ark[4 + c]
            acc = keep[c][:, 0:T]
            P.dve(lambda e, acc=acc: e.tensor_tensor(acc, acc, mu[:, 0:T], ALU.subtract), r=[ck, "ar2"], w=[ck])
            P.dve(lambda e, acc=acc: e.tensor_tensor(acc, acc, rs[:, 0:T], ALU.mult), r=[ck, "ar3"], w=[ck])
            P.act(lambda e, acc=acc, c=c: e.activation(out=mixS[:, 4 + c, :], in_=acc, func=AF.Silu, scale=vcol(52 + c), bias=vcol(56 + c)),
                  r=[ck, "vec%d" % (52 + c), "vec%d" % (56 + c)], w=["mixS"])
        group_norm_inplace(mixS, "mixS", 4, 4, 60, 512, ar[8], "ar8")

    def attention(l):
        for c in range(4):
            load_vec(64 + c, dr["mla_qa_g"][l, c * 128:(c + 1) * 128])
        for c in range(2):
            load_vec(68 + c, dr["mla_kva_g"][l, c * 128:(c + 1) * 128])
        load_vec(70, dr["mla_q_g"][l, 0:128])
        load_vec(71, dr["mla_q_g"][l, 128:192], 64)
        load_vec(73, dr["mla_k_g"][l, 0:128])
        load_vec(74, dr["mla_k_g"][l, 128:192], 64)
        for c in range(8):
            load_vec(76 + c, dr["grp_g"][l, c * 128:(c + 1) * 128])
        zq = [ar[i] for i in range(4)]
        for c in range(4):
            w_q, wqk = load_cols("w_in", l, c * 128, 128, D)

            def to_zq(blk, bk, bkk, c=c):
                P.act(lambda e: e.copy(zq[c][:, blk * 512:(blk + 1) * 512], bk[:]), r=[bkk], w=[ark[c]])
            proj_fm(w_q, wqk, 0, 128, hT, "hT", 16, to_zq)
        rs = ar[8]
        rstd_rows([(lambda blk, c=c: zq[c][:, blk * 512:(blk + 1) * 512], ark[c], 128) for c in range(4)], rs, "ar8", 512, EPS)
        for c in range(4):
            P.dve(lambda e, c=c: e.scalar_tensor_tensor(qaT[:, c, :], zq[c][:, 0:T], vcol(64 + c), rs[:, 0:T], ALU.mult, ALU.mult),
                  r=[ark[c], "ar8", "vec%d" % (64 + c)], w=["qaT"])
        zkv = [ar[4], ar[5]]
        for c in range(2):
            w_kv, wkvk = load_cols("w_in", l, 512 + c * 128, 128, D)

            def to_zkv(blk, bk, bkk, c=c):
                P.act(lambda e: e.copy(zkv[c][:, blk * 512:(blk + 1) * 512], bk[:]), r=[bkk], w=[ark[4 + c]])
            proj_fm(w_kv, wkvk, 0, 128, hT, "hT", 16, to_zkv)
        rstd_rows([(lambda blk, c=c: zkv[c][:, blk * 512:(blk + 1) * 512], ark[4 + c], 128) for c in range(2)], rs, "ar8", 256, EPS)
        for c in range(2):
            P.dve(lambda e, c=c: e.scalar_tensor_tensor(zkv[c][:, 0:T], zkv[c][:, 0:T], vcol(68 + c), rs[:, 0:T], ALU.mult, ALU.mult),
                  r=[ark[4 + c], "ar8", "vec%d" % (68 + c)], w=[ark[4 + c]])
            P.act(lambda e, c=c: e.copy(ckvT[:, c, 512:NK], zkv[c][:, 0:T]), r=[ark[4 + c]], w=["ckvT"])
        for t in range(NT):
            for c in range(2):
                bk = bank[2 + c]
                P.pe(lambda e, bk=bk, c=c, t=t: e.transpose(bk[:, 0:128], zkv[c][:, t * 128:(t + 1) * 128], identf[:]), r=[ark[4 + c], "identf"], w=["bank%d" % (2 + c)])
                P.act(lambda e, bk=bk, c=c, t=t: e.copy(kst[:, t, c * 128:(c + 1) * 128], bk[:, 0:128]), r=["bank%d" % (2 + c)], w=["kst"])
        kr = ar[7]
        w_kr, wkrk = load_cols("w_in", l, 768, 64, D)

        def to_kr(blk, bk, bkk):
            P.act(lambda e: e.copy(kr[0:64, 512 + blk * 512:512 + (blk + 1) * 512], bk[0:64, :]), r=[bkk], w=["ar7"])
        proj_fm(w_kr, wkrk, 0, 64, hT, "hT", 16, to_kr)
        for t in range(NT):
            bk = bank[2 + t % 2]
            bkk = "bank%d" % (2 + t % 2)
            P.pe(lambda e, bk=bk, t=t: e.transpose(bk[:, 0:64], kr[0:64, 512 + t * 128:512 + (t + 1) * 128], identf[0:64, 0:64]),
                 r=["ar7", "identf"], w=[bkk])
            P.act(lambda e, bk=bk, t=t: e.copy(kst[:, t, 256:320], bk[:, 0:64]), r=[bkk], w=["kst"])
        P.dma(lambda e: e.dma_start(out=o_ckv[l].rearrange("(t p) d -> p t d", p=128), in_=kst[:, :, 0:256]), r=["kst"])
        P.dma(lambda e: e.dma_start(out=o_kr[l].rearrange("(t p) d -> p t d", p=128), in_=kst[:, :, 256:320]), r=["kst"])
        P.dma(lambda e: e.dma_start(out=kst[:, 0:4, 0:256], in_=dr["cache_ckv"][l].rearrange("(t p) d -> p t d", p=128)), r=["kst"], w=["kst"])
        P.dma(lambda e: e.dma_start(out=kst[:, 0:4, 256:320], in_=dr["cache_krope"][l].rearrange("(t p) d -> p t d", p=128)), r=["kst"], w=["kst"])
        for t in range(4):
            for c in range(2):
                bk = bank[2 + c]
                P.pe(lambda e, bk=bk, c=c, t=t: e.transpose(bk[:, 0:128], kst[:, t, c * 128:(c + 1) * 128], identf[:]), r=["kst", "identf"], w=["bank%d" % (2 + c)])
                P.act(lambda e, bk=bk, c=c, t=t: e.copy(ckvT[:, c, t * 128:(t + 1) * 128], bk[:, 0:128]), r=["bank%d" % (2 + c)], w=["ckvT"])
            bk = bank[4]
            P.pe(lambda e, t=t: e.transpose(bk[0:64, 0:128], kst[:, t, 256:320], identf[:]), r=["kst", "identf"], w=["bank4"])
            P.act(lambda e, t=t: e.copy(kr[0:64, t * 128:(t + 1) * 128], bk[0:64, 0:128]), r=["bank4"], w=["ar7"])
        sskr = ar[6]
        for blk in range(3):
            bs = slice(blk * 512, (blk + 1) * 512)
            P.act(lambda e, bs=bs: e.activation(out=tmA[0:64, 0:512], in_=kr[0:64, bs], func=AF.Square), r=["ar7"], w=["tmA0"])
            P.pe(lambda e: e.matmul(bank[4][:], onesf[0:64, :], tmA[0:64, 0:512], start=True, stop=True), r=["tmA0", "onesf"], w=["bank4"])
            P.act(lambda e, bs=bs: e.copy(sskr[:, bs], bank[4][:]), r=["bank4"], w=["ar6"])
        kk, kk2, rk, qr, Rs, t2 = ar[0], ar[1], ar[2], ar[3], ar[4], ar[5]
        for h in range(8):
            w_qb, wqbk = load_cols("mla_w_qb", l, h * 192, 192, 512)
            w_kvb, wkvbk = load_cols("mla_w_kvb", l, h * 256, 256, 256)
            for blk in range(3):
                bs = slice(blk * 512, (blk + 1) * 512)
                for k in range(2):
                    P.pe(lambda e, k=k, bs=bs, w_kvb=w_kvb: e.matmul(bank[2][:], w_kvb[:, k, 0:128], ckvT[:, k, bs], start=(k == 0), stop=(k == 1)), r=[wkvbk, "ckvT"], w=["bank2"])
                P.act(lambda e, bs=bs: e.copy(kk[:, bs], bank[2][:]), r=["bank2"], w=["ar0"])
                P.act(lambda e, bs=bs: e.activation(out=kk2[:, bs], in_=bank[2][:], func=AF.Square), r=["bank2"], w=["ar1"])
                P.pe(lambda e, bs=bs: e.matmul(bank[3][:], onesf[:], kk2[:, bs], start=True, stop=True), r=["ar1", "onesf"], w=["bank3"])
                P.dve(lambda e, bs=bs: e.tensor_tensor(rk[:, bs], bank[3][:], sskr[:, bs], ALU.add), r=["bank3", "ar6"], w=["ar2"])
            P.act(lambda e: e.activation(out=rk[:], in_=rk[:], func=AF.Sqrt, bias=EPS, scale=1.0 / 192), r=["ar2"], w=["ar2"])
            P.dve(lambda e: e.reciprocal(rk[:], rk[:]), r=["ar2"], w=["ar2"])
            P.dve(lambda e: e.scalar_tensor_tensor(knT[:], kk[:], vcol(73), rk[:], ALU.mult, ALU.mult), r=["ar0", "ar2", "vec73"], w=["knT"])
            P.dve(lambda e: e.scalar_tensor_tensor(kk2[0:64, :], kr[0:64, :], vcol(74, 64), rk[0:64, :], ALU.mult, ALU.mult), r=["ar7", "ar2", "vec74"], w=["ar1"])
            rope64(kk2, "ar1", Rs, "ar4", t2, "ar5", CC, SS, 0, NK, krT, "krT")
            for kt in range(NKT):
                bi = kt % 2 + 2
                bk = bank[bi]
                for k in range(2):
                    P.pe(lambda e, k=k, kt=kt, bk=bk, w_kvb=w_kvb: e.matmul(bk[:, 0:128], ckvT[:, k, kt * 128:(kt + 1) * 128], w_kvb[:, k, 128:256],
                                                              start=(k == 0), stop=(k == 1)), r=["ckvT", wkvbk], w=["bank%d" % bi])
                P.act(lambda e, kt=kt, bk=bk: e.copy(vh[:, kt, :], bk[:, 0:128]), r=["bank%d" % bi], w=["vh"])
            qq, qq2, rq = kk, kk2, rk
            for blk in range(2):
                bs = slice(blk * 512, (blk + 1) * 512)
                for k in range(4):
                    P.pe(lambda e, k=k, bs=bs, w_qb=w_qb: e.matmul(bank[2][:], w_qb[:, k, 0:128], qaT[:, k, bs], start=(k == 0), stop=(k == 3)), r=[wqbk, "qaT"], w=["bank2"])
                P.act(lambda e, bs=bs: e.copy(qq[:, bs], bank[2][:]), r=["bank2"], w=["ar0"])
                P.act(lambda e, bs=bs: e.activation(out=qq2[:, bs], in_=bank[2][:], func=AF.Square), r=["bank2"], w=["ar1"])
                for k in range(4):
                    P.pe(lambda e, k=k, bs=bs, w_qb=w_qb: e.matmul(bank[3][0:64, :], w_qb[:, k, 128:192], qaT[:, k, bs], start=(k == 0), stop=(k == 3)), r=[wqbk, "qaT"], w=["bank3"])
                P.act(lambda e, bs=bs: e.copy(qr[0:64, bs], bank[3][0:64, :]), r=["bank3"], w=["ar3"])
                P.act(lambda e, bs=bs: e.activation(out=t2[0:64, bs], in_=bank[3][0:64, :], func=AF.Square), r=["bank3"], w=["ar5"])
                P.pe(lambda e, bs=bs: e.matmul(bank[4][:], onesf[:], qq2[:, bs], start=True, stop=False), r=["ar1", "onesf"], w=["bank4"])
                P.pe(lambda e, bs=bs: e.matmul(bank[4][:], onesf[0:64, :], t2[0:64, bs], start=False, stop=True), r=["ar5", "onesf"], w=["bank4"])
                P.act(lambda e, bs=bs: e.activation(out=rq[:, bs], in_=bank[4][:], func=AF.Sqrt, bias=EPS, scale=1.0 / 192), r=["bank4"], w=["ar2"])
            P.dve(lambda e: e.reciprocal(rq[:, 0:T], rq[:, 0:T]), r=["ar2"], w=["ar2"])
            P.dve(lambda e: e.scalar_tensor_tensor(qnT[:], qq[:, 0:T], vcol(70), rq[:, 0:T], ALU.mult, ALU.mult), r=["ar0", "ar2", "vec70"], w=["qnT"])
            P.dve(lambda e: e.scalar_tensor_tensor(qr[0:64, 0:T], qr[0:64, 0:T], vcol(71, 64), rq[0:64, 0:T], ALU.mult, ALU.mult), r=["ar3", "ar2", "vec71"], w=["ar3"])
            rope64(qr, "ar3", Rs, "ar4", t2, "ar5", CC, SS, 512, T, qrT, "qrT")
            for blk in range(2):
                bs = slice(blk * 512, (blk + 1) * 512)
                for kt in range(NKT):
                    ks = slice(kt * 128, (kt + 1) * 128)
                    bi = kt % 2
                    bk = bank[bi]
                    bkk = "bank%d" % bi
                    P.pe(lambda e, ks=ks, bs=bs, bk=bk: e.matmul(bk[:], knT[:, ks], qnT[:, bs], start=True, stop=False), r=["knT", "qnT"], w=[bkk])
                    P.pe(lambda e, ks=ks, bs=bs, bk=bk: e.matmul(bk[:], krT[:, ks], qrT[:, bs], start=False, stop=True), r=["krT", "qrT"], w=[bkk])
                    pt = pT[bi]
                    ptk = "pT%d" % bi
                    for qb in range(2):
                        mcol = kt * 4 + blk * 2 + qb
                        P.act(lambda e, bk=bk, pt=pt, qb=qb, mcol=mcol: e.activation(out=pt[:, qb * 256:(qb + 1) * 256], in_=bk[:, qb * 256:(qb + 1) * 256],
                                                                                     func=AF.Exp, scale=192.0 ** -0.5, bias=maskb[:, mcol:mcol + 1]),
                              r=[bkk, "maskb"], w=[ptk])
                    P.pe(lambda e, kt=kt, pt=pt: e.matmul(bank[2][:], vh[:, kt, :], pt[:], start=(kt == 0), stop=(kt == NKT - 1)), r=["vh", ptk], w=["bank2"])
                    P.pe(lambda e, kt=kt, pt=pt: e.matmul(bank[3][:], ones[:], pt[:], start=(kt == 0), stop=(kt == NKT - 1)), r=["ones", ptk], w=["bank3"])
                P.dve(lambda e: e.reciprocal(tmA[:, 0:512], bank[3][:]), r=["bank3"], w=["tmA0"])
                P.dve(lambda e, bs=bs, h=h: e.tensor_tensor(hT[:, h, bs], bank[2][:], tmA[:, 0:512], ALU.mult), r=["bank2", "tmA0"], w=["hT"])
        group_norm_inplace(hT, "hT", 0, 8, 76, 1024, ar[8], "ar8")

    def rope64(R, Rk, Rs, Rsk, t2, t2k, CCt, SSt, c0, n, out, outk):
        P.act(lambda e: e.copy(Rs[0:32, 0:n], R[32:64, 0:n]), r=[Rk], w=[Rsk])
        P.act(lambda e: e.copy(Rs[32:64, 0:n], R[0:32, 0:n]), r=[Rk, Rsk], w=[Rsk])
        P.dve(lambda e: e.tensor_tensor(Rs[0:64, 0:n], Rs[0:64, 0:n], SSt[:, c0:c0 + n], ALU.mult), r=[Rsk, "SS"], w=[Rsk])
        P.dve(lambda e: e.tensor_tensor(t2[0:64, 0:n], R[0:64, 0:n], CCt[:, c0:c0 + n], ALU.mult), r=[Rk, "CC"], w=[t2k])
        P.dve(lambda e: e.tensor_tensor(out[:, 0:n], t2[0:64, 0:n], Rs[0:64, 0:n], ALU.add), r=[t2k, Rsk], w=[outk])

    def out_proj(l):
        if o_dbg is not None and l == 0:
            for k in range(16):
                src = hT[:, k, :] if k < 8 else mixS[:, k - 8, :]
                P.dma(lambda e, k=k, src=src: e.dma_start(out=o_dbg[k, :, 0:T], in_=src), r=["hT", "mixS"], eng="pool")
        def mixchunk(k, t):
            if k < 8:
                return hT[:, k, t * 128:(t + 1) * 128]
            return mixS[:, k - 8, t * 128:(t + 1) * 128]
        for cb in range(8):
            w_t, wk = load_cols("w_out", l, cb * 256, 256, D)
            cs = slice(cb * 256, (cb + 1) * 256)
            for t in range(NT):
                bi = t % 2
                bk = bank[bi]
                bkk = "bank%d" % bi
                xs = ar[t % 2]
                xsk = ark[t % 2]
                P.dma(lambda e, t=t, cs=cs, xs=xs: e.dma_start(out=xs[:, 0:256], in_=xd[t][:, cs]), r=["xd%d" % t], w=[xsk])
                for k in range(16):
                    P.pe(lambda e, k=k, t=t, bk=bk, w_t=w_t: e.matmul(bk[:, 0:256], mixchunk(k, t), w_t[:, k, :], start=(k == 0), stop=(k == 15)),
                         r=["hT", "mixS", wk], w=[bkk])
                P.dve(lambda e, bk=bk, cs=cs, xs=xs: e.tensor_tensor(xs[:, 256:512], bk[:, 0:256], modG[:, cs], ALU.mult), r=[bkk, "modG", xsk], w=[xsk])
                P.dve(lambda e, xs=xs: e.tensor_tensor(xs[:, 0:256], xs[:, 0:256], xs[:, 256:512], ALU.add), r=[xsk], w=[xsk])
                P.dma(lambda e, t=t, cs=cs, xs=xs: e.dma_start(out=xd[t][:, cs], in_=xs[:, 0:256]), r=[xsk], w=["xd%d" % t])

    def peer(l):
        hfT = mixS[:, 0:2, :].rearrange("p a (b n) -> p (a b) n", b=8)
        for j in range(16):
            h_, p_ = j // 2, j % 2
            P.dma(lambda e, h_=h_, p_=p_: e.dma_start(out=bwf[:], in_=dr["peer_keys"][l, h_, p_, :, :]), w=["bwf"])
            P.dve(lambda e: e.tensor_copy(bwt[0][:], bwf[:]), r=["bwf"], w=["bwt0"])
            bb = bankb[j % 2]
            bbk = "bankb%d" % (j % 2)
            P.pe(lambda e, bb=bb: e.transpose(bb[:, 0:128], bwt[0][:], ident[:]), r=["bwt0", "ident"], w=[bbk])
            P.act(lambda e, bb=bb, j=j: e.copy(keysT[:, j, :], bb[:, 0:128]), r=[bbk], w=["keysT"])
        norm_to_hT()
        ar7b = ar[7][:, :].bitcast(BF16)
        ar8b = ar[8][:, :].bitcast(BF16)
        qTc = ([mixS[:, 2 + i, :] for i in range(6)] + [ar7b[:, i * 1024:(i + 1) * 1024] for i in range(3)]
               + [ar8b[:, i * 1024:(i + 1) * 1024] for i in range(3)] + [qaT[:, 2, :], qaT[:, 3, :]]
               + [vh[:, :, :].rearrange("p a b -> p (a b)")[:, 0:1024], qnT[:, :]])
        P.alias("mixS", ["qTc%d" % i for i in range(6)])
        P.alias("ar7", ["qTc6", "qTc7", "qTc8"])
        P.alias("ar8", ["qTc9", "qTc10", "qTc11"])
        P.alias("qaT", ["qTc12", "qTc13"])
        P.alias("vh", ["qTc14"])
        P.alias("qnT", ["qTc15"])
        for jb in range(8):
            w_t, wk = load_cols("peer_w_q", l, jb * 256, 256, D)
            for jj in range(2):
                j = jb * 2 + jj

                def to_q(blk, bk, bkk, j=j):
                    P.act(lambda e: e.copy(qTc[j][:, blk * 512:(blk + 1) * 512], bk[:]), r=[bkk], w=["qTc%d" % j])
                proj_fm(w_t, wk, jj * 128, 128, hT, "hT", 16, to_q)
        sc = [ar[2], ar[3]]
        s1b, cand, candb, eq = ar[4], ar[5], ar[6], accp
        for t in range(NT):
            norm_tile(t)
            for j in range(16):
                bs_ = bank[4 + j % 2]
                bsk = "bank%d" % (4 + j % 2)
                P.pe(lambda e, j=j, bs_=bs_, t=t: e.matmul(bs_[:, 0:128], qTc[j][:, t * 128:(t + 1) * 128], keysT[:, j, :], start=True, stop=True),
                     r=["qTc%d" % j, "keysT"], w=[bsk])
                P.dve(lambda e, j=j, bs_=bs_: e.tensor_copy(sc[j // 8][:, (j % 8) * 128:(j % 8 + 1) * 128], bs_[:, 0:128]), r=[bsk], w=[ark[2 + j // 8]])
            for h in range(8):
                for p_ in range(2):
                    j = h * 2 + p_
                    s_ap = sc[j // 8][:, (j % 8) * 128:(j % 8 + 1) * 128]
                    sk = ark[2 + j // 8]
                    m = tk[:, h, p_ * 16:(p_ + 1) * 16]
                    ii = tki[:, h, p_ * 16:(p_ + 1) * 16]
                    P.dve(lambda e, m=m, s_ap=s_ap: e.max(out=m[:, 0:8], in_=s_ap), r=[sk], w=["tk"])
                    P.dve(lambda e, m=m, ii=ii, s_ap=s_ap: e.max_index(out=ii[:, 0:8], in_max=m[:, 0:8], in_values=s_ap), r=[sk, "tk"], w=["tki"])
                    P.dve(lambda e, m=m, s_ap=s_ap: e.match_replace(out=s1b[:, 0:128], in_to_replace=m[:, 0:8], in_values=s_ap, imm_value=-1e30), r=[sk, "tk"], w=["ar4"])
                    P.dve(lambda e, m=m: e.max(out=m[:, 8:16], in_=s1b[:, 0:128]), r=["ar4"], w=["tk"])
                    P.dve(lambda e, m=m, ii=ii: e.max_index(out=ii[:, 8:16], in_max=m[:, 8:16], in_values=s1b[:, 0:128]), r=["ar4", "tk"], w=["tki"])
                c3 = cand[:, 0:256].rearrange("p (a b) -> p a b", a=16)
                P.dve(lambda e, h=h, c3=c3: e.tensor_tensor(c3, tk[:, h, 0:16].unsqueeze(2).to_broadcast([128, 16, 16]),
                                                          tk[:, h, 16:32].unsqueeze(1).to_broadcast([128, 16, 16]), ALU.add), r=["tk"], w=["ar5"])
                m = tk[:, h, 32:48]
                ii = tki[:, h, 32:48]
                P.dve(lambda e, m=m: e.max(out=m[:, 0:8], in_=cand[:, 0:256]), r=["ar5"], w=["tk"])
                P.dve(lambda e, m=m, ii=ii: e.max_index(out=ii[:, 0:8], in_max=m[:, 0:8], in_values=cand[:, 0:256]), r=["ar5", "tk"], w=["tki"])
                P.dve(lambda e, m=m: e.match_replace(out=candb[:, 0:256], in_to_replace=m[:, 0:8], in_values=cand[:, 0:256], imm_value=-1e30), r=["ar5", "tk"], w=["ar6"])
                P.dve(lambda e, m=m: e.max(out=m[:, 8:16], in_=candb[:, 0:256]), r=["ar6"], w=["tk"])
                P.dve(lambda e, m=m, ii=ii: e.max_index(out=ii[:, 8:16], in_max=m[:, 8:16], in_values=candb[:, 0:256]), r=["ar6", "tk"], w=["tki"])
            posu = tki[:, :, 32:48]
            k1f, k2f, idxf, gate, actv, wgt = (pk[:, i, :] for i in range(6))
            pu3 = pki[:, :].bitcast(U32).rearrange("p (h r) -> p h r", h=8)
            a3 = actv.rearrange("p (h r) -> p h r", h=8)
            w3 = wgt.rearrange("p (h r) -> p h r", h=8)
            eq4 = eq.rearrange("p (h r q) -> p h r q", h=8, r=16)
            for which, (sop, samt, src_i, dstf) in enumerate(((ALU.logical_shift_right, 4, 0, k1f), (ALU.bitwise_and, 15, 16, k2f))):
                P.dve(lambda e, sop=sop, samt=samt: e.tensor_single_scalar(pu3, posu, samt, op=sop), r=["tki"], w=["pki"])
                P.dve(lambda e: e.tensor_copy(a3, pu3), r=["pki"], w=["pk4"])
                P.dve(lambda e, src_i=src_i: e.tensor_copy(w3, tki[:, :, src_i:src_i + 16]), r=["tki"], w=["pk5"])
                P.dve(lambda e: e.tensor_tensor(eq4, a3.unsqueeze(3).to_broadcast([128, 8, 16, 16]),
                                                iot[:, :].unsqueeze(1).unsqueeze(1).to_broadcast([128, 8, 16, 16]), ALU.is_equal), r=["pk4", "iot"], w=["accp"])
                P.dve(lambda e: e.tensor_tensor(eq4, eq4, w3.unsqueeze(2).to_broadcast([128, 8, 16, 16]), ALU.mult), r=["accp", "pk5"], w=["accp"])
                P.dve(lambda e, dstf=dstf: e.tensor_reduce(out=dstf.rearrange("p (h r) -> p h r", h=8), in_=eq4, axis=AX.X, op=ALU.add), r=["accp"], w=["pk%d" % which])
            P.dve(lambda e: e.scalar_tensor_tensor(idxf, k1f, 128.0, k2f, ALU.mult, ALU.add), r=["pk0", "pk1"], w=["pk2"])
            P.dve(lambda e: e.tensor_scalar(idxf, idxf, float(l * 16384), None, op0=ALU.add), r=["pk2"], w=["pk2"])
            P.dve(lambda e: e.tensor_copy(pki[:, :], idxf), r=["pk2"], w=["pki"])
            c16 = tk[:, :, 32:48]
            g3 = gate.rearrange("p (h r) -> p h r", h=8)
            P.dve(lambda e: e.tensor_tensor(g3, c16, tk[:, :, 32:33].to_broadcast([128, 8, 16]), ALU.subtract), r=["tk"], w=["pk3"])
            P.act(lambda e: e.activation(out=gate, in_=gate, func=AF.Exp), r=["pk3"], w=["pk3"])
            P.dve(lambda e: e.tensor_reduce(out=sm[:, 0:8], in_=g3, axis=AX.X, op=ALU.add), r=["pk3"], w=["smg"])
            P.dve(lambda e: e.reciprocal(sm[:, 0:8], sm[:, 0:8]), r=["smg"], w=["smg"])
            P.dve(lambda e: e.tensor_tensor(g3, g3, sm[:, 0:8].unsqueeze(2).to_broadcast([128, 8, 16]), ALU.mult), r=["pk3", "smg"], w=["pk3"])
            for j in range(128):
                g_ = ugb[j % 6]
                gk = "ugb%d" % (j % 6)
                P.dma(lambda e, j=j, g_=g_: e.indirect_dma_start(out=g_, out_offset=None, in_=dr["peer_u"],
                                                                 in_offset=bass.IndirectOffsetOnAxis(ap=pki[:, j:j + 1], axis=0)),
                      r=["pki"], w=[gk], eng="pool")
                P.dve(lambda e, j=j, g_=g_: e.scalar_tensor_tensor(junkb, g_, 1.0, tmB[:], ALU.mult, ALU.mult, accum_out=actv[:, j:j + 1]),
                      r=[gk, "tmB"], w=["junkb", "pk4"])
            P.dve(lambda e: e.tensor_tensor(wgt, actv, actv, ALU.mult), r=["pk4"], w=["pk5"])
            P.dve(lambda e: e.tensor_scalar(wgt, wgt, 0.044715, 1.0, op0=ALU.mult, op1=ALU.add), r=["pk5"], w=["pk5"])
            P.dve(lambda e: e.tensor_tensor(wgt, wgt, actv, ALU.mult), r=["pk5", "pk4"], w=["pk5"])
            P.act(lambda e: e.activation(out=wgt, in_=wgt, func=AF.Sigmoid, scale=GELU_C), r=["pk5"], w=["pk5"])
            P.dve(lambda e: e.tensor_tensor(wgt, wgt, actv, ALU.mult), r=["pk5", "pk4"], w=["pk5"])
            P.dve(lambda e: e.tensor_tensor(wgt, wgt, gate, ALU.mult), r=["pk5", "pk3"], w=["pk5"])
            for j in range(128):
                g_ = vgb[j % 6]
                gk = "ugb%d" % (j % 6)
                dgi = j % 4
                P.dma(lambda e, j=j, g_=g_: e.indirect_dma_start(out=g_, out_offset=None, in_=dr["peer_v"],
                                                                 in_offset=bass.IndirectOffsetOnAxis(ap=pki[:, j:j + 1], axis=0)),
                      r=["pki"], w=[gk], eng="pool")
                P.dve(lambda e, j=j, dgi=dgi: e.tensor_scalar(dgr[dgi][:], identf[:], wgt[:, j:j + 1], None, op0=ALU.mult), r=["identf", "pk5"], w=["dgr%d" % dgi])
                for cb in range(4):
                    P.pe(lambda e, j=j, cb=cb, g_=g_, dgi=dgi: e.matmul(bank[2 + cb][:], dgr[dgi][:], g_[:, cb * 512:(cb + 1) * 512], start=(j == 0), stop=(j == 127)),
                         r=["dgr%d" % dgi, gk], w=["bank%d" % (2 + cb)])
            for hf in range(2):
                hs = slice(hf * 1024, (hf + 1) * 1024)
                for cbh in range(2):
                    cb = hf * 2 + cbh
                    cs = slice(cb * 512, (cb + 1) * 512)
                    P.dve(lambda e, cb=cb, cs=cs: e.tensor_tensor(tmA[:, cs], bank[2 + cb][:], modG[:, cs], ALU.mult), r=["bank%d" % (2 + cb), "modG"], w=["tmA"])
                    P.dve(lambda e, hf=hf, cbh=cbh, cs=cs: e.tensor_tensor(ar[hf][:, cbh * 512:(cbh + 1) * 512], ar[hf][:, cbh * 512:(cbh + 1) * 512], tmA[:, cs], ALU.add),
                          r=[ark[hf], "tmA"], w=[ark[hf]])
                P.dma(lambda e, hf=hf, hs=hs, t=t: e.dma_start(out=xd[t][:, hs], in_=ar[hf][:, 0:1024]), r=[ark[hf]], w=["xd%d" % t])

    def do_barrier():
        P.barrier({"act": lambda e: e.copy(bsc[:, 0:1], bsc[:, 1:2]), "dve": lambda e: e.memset(bsc[:, 2:3], 0.0),
                   "pool": lambda e: e.memset(bsc[:, 3:4], 0.0), "sp": lambda e: e.nop()})

    for t_ in range(NT):
        P.alias("xd%d" % t_, ["xd%d_%d" % (t_, cb_) for cb_ in range(4)])

    def peer_dense(l):
        do_barrier()
        ada_mod(l, 1)
        norm_to_hT()
        for j in range(16):
            h_, p_ = j // 2, j % 2
            P.dma(lambda e, h_=h_, p_=p_: e.dma_start(out=bwf[:], in_=dr["peer_keys"][l, h_, p_, :, :]), w=["bwf"])
            P.dve(lambda e: e.tensor_copy(bwt[0][:], bwf[:]), r=["bwf"], w=["bwt0"])
            bb = bankb[j % 2]
            bbk = "bankb%d" % (j % 2)
            P.pe(lambda e, bb=bb: e.transpose(bb[:, 0:128], bwt[0][:], ident[:]), r=["bwt0", "ident"], w=[bbk])
            P.act(lambda e, bb=bb, j=j: e.copy(keysT[:, j, :], bb[:, 0:128]), r=[bbk], w=["keysT"])
        qT1 = mixS
        qT2 = modAB[:, :].bitcast(BF16).rearrange("p (c n) -> p c n", c=8)
        P.alias("qT2", ["modA", "modB", "kst"])
        for jb in range(8):
            w_t, wk = load_cols("peer_w_q", l, jb * 256, 256, D)

            def to_q1(blk, bk, bkk, jb=jb):
                P.act(lambda e: e.copy(qT1[:, jb, blk * 512:(blk + 1) * 512], bk[:]), r=[bkk], w=["mixS"])

            def to_q2(blk, bk, bkk, jb=jb):
                P.dve(lambda e: e.tensor_copy(qT2[:, jb, blk * 512:(blk + 1) * 512], bk[:]), r=[bkk], w=["qT2"])
            proj_fm(w_t, wk, 0, 128, hT, "hT", 16, to_q1)
            proj_fm(w_t, wk, 128, 128, hT, "hT", 16, to_q2)
        sc = [ar[2], ar[3]]
        s1b, cand, candb = ar[4], ar[5], ar[6]
        for t in range(NT):
            ts_ = slice(t * 128, (t + 1) * 128)
            for j in range(16):
                h_, p_ = j // 2, j % 2
                qsrc, qk = (qT1, "mixS") if p_ == 0 else (qT2, "qT2")
                bi = 2 + j // 4
                P.pe(lambda e, j=j, h_=h_, qsrc=qsrc, bi=bi, ts_=ts_: e.matmul(bank[bi][:, (j % 4) * 128:(j % 4 + 1) * 128], qsrc[:, h_, ts_], keysT[:, j, :],
                                                                          start=True, stop=True), r=[qk, "keysT"], w=["bank%d" % bi])
            for q4 in range(4):
                dst = sc[q4 // 2][:, (q4 % 2) * 512:(q4 % 2 + 1) * 512]
                if q4 % 2 == 0:
                    P.act(lambda e, dst=dst, q4=q4: e.copy(dst, bank[2 + q4][:]), r=["bank%d" % (2 + q4)], w=[ark[2 + q4 // 2]])
                else:
                    P.dve(lambda e, dst=dst, q4=q4: e.tensor_copy(dst, bank[2 + q4][:]), r=["bank%d" % (2 + q4)], w=[ark[2 + q4 // 2]])
            for h in range(8):
                for p_ in range(2):
                    j = h * 2 + p_
                    s_ap = sc[j // 8][:, (j % 8) * 128:(j % 8 + 1) * 128]
                    sk = ark[2 + j // 8]
                    m = tk[:, h, p_ * 16:(p_ + 1) * 16]
                    P.dve(lambda e, m=m, s_ap=s_ap: e.max(out=m[:, 0:8], in_=s_ap), r=[sk], w=["tk"])
                    P.dve(lambda e, m=m, s_ap=s_ap: e.match_replace(out=s1b[:, 0:128], in_to_replace=m[:, 0:8], in_values=s_ap, imm_value=-1e30), r=[sk, "tk"], w=["ar4"])
                    P.dve(lambda e, m=m: e.max(out=m[:, 8:16], in_=s1b[:, 0:128]), r=["ar4"], w=["tk"])
                c3 = cand[:, 0:256].rearrange("p (a b) -> p a b", a=16)
                P.dve(lambda e, h=h, c3=c3: e.tensor_tensor(c3, tk[:, h, 0:16].unsqueeze(2).to_broadcast([128, 16, 16]),
                                                          tk[:, h, 16:32].unsqueeze(1).to_broadcast([128, 16, 16]), ALU.add), r=["tk"], w=["ar5"])
                m = tk[:, h, 32:48]
                P.dve(lambda e, m=m: e.max(out=m[:, 0:8], in_=cand[:, 0:256]), r=["ar5"], w=["tk"])
                P.dve(lambda e, m=m: e.match_replace(out=candb[:, 0:256], in_to_replace=m[:, 0:8], in_values=cand[:, 0:256], imm_value=-1e30), r=["ar5", "tk"], w=["ar6"])
                P.dve(lambda e, m=m: e.max(out=m[:, 8:16], in_=candb[:, 0:256]), r=["ar6"], w=["tk"])
            P.dve(lambda e, t=t: e.tensor_copy(thrT[:, t, :], tk[:, :, 47]), r=["tk"], w=["thrT"])
            g3 = s1b[:, 0:128].rearrange("p (h r) -> p h r", h=8)
            P.dve(lambda e: e.tensor_tensor(g3, tk[:, :, 32:48], tk[:, :, 32:33].to_broadcast([128, 8, 16]), ALU.subtract), r=["tk"], w=["ar4"])
            P.act(lambda e: e.activation(out=s1b[:, 0:128], in_=s1b[:, 0:128], func=AF.Exp), r=["ar4"], w=["ar4"])
            P.dve(lambda e: e.tensor_reduce(out=sm[:, 0:8], in_=g3, axis=AX.X, op=ALU.add), r=["ar4"], w=["smg"])
            P.act(lambda e: e.activation(out=sm[:, 0:8], in_=sm[:, 0:8], func=AF.Ln), r=["smg"], w=["smg"])
            P.dve(lambda e, t=t: e.scalar_tensor_tensor(biasT[:, t, :], sm[:, 0:8], -1.0, tk[:, :, 32], ALU.mult, ALU.subtract), r=["smg", "tk"], w=["biasT"])
        if peer_stop == 2:
            return
        do_barrier()
        ur = [wb[i // 2][:, :, :].rearrange("p k n -> p (k n)")[:, (i % 2) * 2048:(i % 2 + 1) * 2048] for i in range(4)]
        tmAb = tmA[:, :].bitcast(BF16)
        uT = [tmAb[:, i * 2048:(i + 1) * 2048].rearrange("p (k n) -> p k n", k=16) for i in range(2)]
        vs = [ar[i][:, 0:1024].bitcast(BF16) for i in range(4)]
        As = [ar[6 + i // 2][:, 0:1024].bitcast(BF16)[:, (i % 2) * 1024:(i % 2 + 1) * 1024] for i in range(4)]
        ost = [ar[6][:, 1024:1536], ar[7][:, 1024:1536]]
        tA = ar[8][:, 0:1024]
        gel = ar[8][:, 1024:1536].bitcast(BF16)
        ar4b = ar[4][:, :].bitcast(BF16)
        WT = [[qaT[:, 2, :], qaT[:, 3, :], ckvT[:, 0, 0:1024], ckvT[:, 1, 0:1024]],
              [ar4b[:, 0:1024], ar4b[:, 1024:2048], ar4b[:, 2048:3072], ar[5][:, 0:512].bitcast(BF16)]]
        s2sb = tmB[:, :].bitcast(F32).rearrange("p (h n) -> p h n", h=8)
        sums = [vh[:, :, :].rearrange("p a b -> p (a b)").bitcast(F32)[:, 0:512], qnT[:, :].bitcast(F32), ar[5][:, 512:1024], ar[5][:, 1024:1536]]
        es = [pT[0][:, :], pT[1][:, :], knT[:, 0:512], knT[:, 512:1024]]
        cTbf = cTb[:, :, :].rearrange("p k n -> p (k n)")
        Wh = [cTbf[:, i * 512:(i + 1) * 512] for i in range(4)]
        NB = 4
        NG = 32 if peer_stop < 3 else (1 if peer_stop == 3 else 2)

        def wgen_scores(g, t):
            ts_ = slice(t * 128, (t + 1) * 128)
            sb_ = t % 2
            for hh in range(2):
                for hq in range(4):
                    h = hh * 4 + hq
                    P.pe(lambda e, h=h, hq=hq, ts_=ts_: e.matmul(bank[4][:, hq * 128:(hq + 1) * 128], qT2[:, h, ts_], keysT[:, 2 * h + 1, :], start=True, stop=True),
                         r=["qT2", "keysT"], w=["bank4"])
                P.act(lambda e, hh=hh: e.copy(s2sb[:, hh * 4:(hh + 1) * 4, :], bank[4][:, :].rearrange("p (h n) -> p h n", h=4)), r=["bank4"], w=["s2sb"])
            for h in range(8):
                P.pe(lambda e, h=h, ts_=ts_, g=g: e.matmul(bank[4][:, h * 4:h * 4 + 4], qT1[:, h, ts_], keysT[:, 2 * h, 4 * g:4 * g + 4], start=True, stop=True),
                     r=["mixS", "keysT"], w=["bank4"])
            P.act(lambda e, sb_=sb_: e.copy(s1sb[:, sb_, :, :], bank[4][:, 0:32].rearrange("p (h n) -> p h n", h=8)), r=["bank4"], w=["s1sb%d" % sb_])

        def wgen_gates(g, t):
            ts_ = slice(t * 128, (t + 1) * 128)
            sb_ = t % 2
            wb_ = g % 2
            P.pe(lambda e: e.matmul(bank[2][:], zer[:], hT[:, 0, 0:512], start=True, stop=False), r=["zer", "hT"], w=["bank2"])

            def emit_add(h):
                i = (t * 8 + h) % NB
                sm3 = sums[i].rearrange("p (c k) -> p c k", c=4)
                addeng = P.pool if h in POOL_HEAD_SET else P.dve
                addeng(lambda e, h=h, sm3=sm3, sb_=sb_: e.tensor_tensor(sm3, s1sb[:, sb_, h, :].unsqueeze(2).to_broadcast([128, 4, 128]),
                                                                   s2sb[:, h, :].unsqueeze(1).to_broadcast([128, 4, 128]), ALU.add),
                       r=["s1sb%d" % sb_, "s2sb"], w=["sum%d" % i])

            def emit_rest(h):
                i = (t * 8 + h) % NB
                P.act(lambda e, h=h, i=i, t=t: e.activation(out=es[i], in_=sums[i], func=AF.Exp, bias=biasT[:, t, h:h + 1]), r=["sum%d" % i, "biasT"], w=["e%d" % i])
                P.dve(lambda e, h=h, i=i, t=t: e.scalar_tensor_tensor(Wh[i], sums[i], thrT[:, t, h:h + 1], es[i], ALU.is_ge, ALU.mult),
                      r=["sum%d" % i, "e%d" % i, "thrT"], w=["Wh%d" % i])
                for ci in range(4):
                    P.pe(lambda e, h=h, i=i, ci=ci: e.matmul(bank[2][:, ci * 128:(ci + 1) * 128], Wh[i][:, ci * 128:(ci + 1) * 128], ident[:],
                                                             start=False, stop=(h == 7 and ci == 3)), r=["Wh%d" % i, "ident"], w=["bank2"])
            AHEAD = 2
            for h in range(AHEAD):
                emit_add(h)
            for h in range(8):
                if h + AHEAD < 8:
                    emit_add(h + AHEAD)
                emit_rest(h)
            for ci in range(4):
                P.act(lambda e, ci=ci, ts_=ts_, wb_=wb_: e.copy(WT[wb_][ci][:, ts_], bank[2][:, ci * 128:(ci + 1) * 128]), r=["bank2"], w=["WT%d_%d" % (wb_, ci)])

        def chunk(g, ci):
            c = 4 * g + ci
            ui, vi, ti = c % 4, c % 4, c % 2
            wb_ = g % 2
            row0 = l * 16384 + c * 128
            P.dma(lambda e, ui=ui, row0=row0: e.dma_start(out=ur[ui], in_=dr["peer_u"][row0:row0 + 128, :]), w=["ur%d" % ui], eng="pool")
            P.dma(lambda e, vi=vi, row0=row0: e.dma_start(out=vs[vi], in_=dr["peer_v"][row0:row0 + 128, :]), w=["vs%d" % vi], eng="pool")
            P.pool(lambda e, vi=vi: e.tensor_tensor(vs[vi], vs[vi], modG[:, :], ALU.mult), r=["vs%d" % vi, "modG"], w=["vs%d" % vi])
            for half in range(2):
                bb = bankb[half]
                bbk = "bankb%d" % half
                for k in range(8):
                    kk_ = half * 8 + k
                    P.pe(lambda e, bb=bb, k=k, kk_=kk_, ui=ui: e.transpose(bb[:, k * 128:(k + 1) * 128], ur[ui][:, kk_ * 128:(kk_ + 1) * 128], ident[:]),
                         r=["ur%d" % ui, "ident"], w=[bbk])
                if half == 0:
                    P.act(lambda e, bb=bb, ti=ti: e.copy(uT[ti][:, 0:8, :], bb[:, :].rearrange("p (c n) -> p c n", c=8)), r=[bbk], w=["uT%d" % ti])
                else:
                    P.dve(lambda e, bb=bb, ti=ti: e.tensor_copy(uT[ti][:, 8:16, :], bb[:, :].rearrange("p (c n) -> p c n", c=8)), r=[bbk], w=["uT%d" % ti])
            for blk in range(2):
                bs = slice(blk * 512, (blk + 1) * 512)
                for k in range(16):
                    P.pe(lambda e, blk=blk, k=k, ti=ti, bs=bs: e.matmul(bank[blk][:], uT[ti][:, k, :], hT[:, k, bs], start=(k == 0), stop=(k == 15)),
                         r=["uT%d" % ti, "hT"], w=["bank%d" % blk])
                bkk = "bank%d" % blk
                tk_ = "tA%d" % blk
                P.act(lambda e, blk=blk, bs=bs: e.activation(out=tA[:, bs], in_=bank[blk][:], func=AF.Square), r=[bkk], w=[tk_])
                P.dve(lambda e, bs=bs: e.tensor_scalar(tA[:, bs], tA[:, bs], 0.044715, 1.0, op0=ALU.mult, op1=ALU.add), r=[tk_], w=[tk_])
                P.dve(lambda e, blk=blk, bs=bs: e.tensor_tensor(tA[:, bs], tA[:, bs], bank[blk][:], ALU.mult), r=[tk_, bkk], w=[tk_])
                P.act(lambda e, bs=bs: e.activation(out=tA[:, bs], in_=tA[:, bs], func=AF.Sigmoid, scale=GELU_C), r=[tk_], w=[tk_])
                P.dve(lambda e, blk=blk, bs=bs: e.tensor_tensor(gel[:, bs], tA[:, bs], bank[blk][:], ALU.mult), r=[tk_, bkk], w=["gel%d" % blk])
                P.dve(lambda e, ci=ci, bs=bs, wb_=wb_: e.tensor_tensor(As[ci][:, bs], gel[:, bs], WT[wb_][ci][:, bs], ALU.mult),
                      r=["gel%d" % blk, "WT%d_%d" % (wb_, ci)], w=["As%d" % ci])

        def outs_tile(g, t):
            ts_ = slice(t * 128, (t + 1) * 128)
            for cb in range(4):
                ob = 3 if cb % 2 == 0 else 5
                for ci in range(4):
                    vi = (4 * g + ci) % 4
                    P.pe(lambda e, ci=ci, vi=vi, ts_=ts_, cb=cb, ob=ob: e.matmul(bank[ob][:], As[ci][:, ts_], vs[vi][:, cb * 512:(cb + 1) * 512], start=(ci == 0), stop=(ci == 3)),
                         r=["As%d" % ci, "vs%d" % vi], w=["bank%d" % ob])
                oi = (t * 4 + cb) % 2
                P.act(lambda e, oi=oi, ob=ob: e.copy(ost[oi], bank[ob][:]), r=["bank%d" % ob], w=["ost%d" % oi])
                P.dma(lambda e, oi=oi, t=t, cb=cb: e.dma_start(out=xd[t][:, cb * 512:(cb + 1) * 512], in_=ost[oi], accum_op=ALU.add),
                      r=["ost%d" % oi], w=["xd%d_%d" % (t, cb)], eng="pool")

        for t in range(NT):
            wgen_scores(0, t)
            wgen_gates(0, t)
        for g in range(NG):
            nxt = g + 1 < NG
            if nxt:
                wgen_scores(g + 1, 0)
            for s_ in range(8):
                if s_ < 4:
                    chunk(g, s_)
                else:
                    outs_tile(g, 2 * (s_ - 4))
                    outs_tile(g, 2 * (s_ - 4) + 1)
                if nxt:
                    wgen_gates(g + 1, s_)
                    if s_ + 1 < 8:
                        wgen_scores(g + 1, s_ + 1)
        do_barrier()

    for l in range(n_layers):
        if do_mixer:
            ada_mod(l, 0)
            if o_dbg is not None and l == 0:
                for i, (src, k_) in enumerate(((modG[:, :], "modG"), (modA, "modA"), (modB, "modB"))):
                    for hf in range(2):
                        P.dma(lambda e, i=i, hf=hf, src=src: e.dma_start(out=o_dbg[16 + 2 * i + hf, :, 0:1024], in_=src[:, hf * 1024:(hf + 1) * 1024]), r=[k_])
            norm_to_hT()
            lru_branch(l)
            conv_branch(l)
            attention(l)
            out_proj(l)
        if do_peer:
            if peer_mode == 'dense':
                peer_dense(l)
            else:
                ada_mod(l, 1)
                peer(l)
    P.build(st)
    return nc, st


def _rope_tables(latent):
    cc = np.ones((64, NK), np.float32)
    ss = np.zeros((64, NK), np.float32)
    if latent:
        n = 1024
        row = np.repeat(np.arange(n // 64), 64).astype(np.float32)
        col = np.tile(np.arange(64), n // 64).astype(np.float32)
        inv = (1.0 / (np.float32(10000.0) ** (np.arange(16, dtype=np.float32) / np.float32(16)))).astype(np.float32)
        ang = np.concatenate([row[:, None] * inv, col[:, None] * inv], -1).astype(np.float32)
        c, s = np.cos(ang).T.astype(np.float32), np.sin(ang).T.astype(np.float32)
        cc[0:32, 512:] = c
        cc[32:64, 512:] = c
        ss[0:32, 512:] = -s
        ss[32:64, 512:] = s
    return cc, ss


def _mask_bias(latent):
    m = np.zeros((128, NKT * 4), np.float32)
    if not latent:
        for kt in range(NKT):
            for qb in range(4):
                ok = kt >= 4 and (kt - 4) // 2 == qb
                m[:, kt * 4 + qb] = 0.0 if ok else NEG
    return m


def make_in_maps(inp, cores, with_uv=True):
    in_maps = []
    for core in cores:
        latent = core >= 4
        m = {}
        if latent:
            b = core - 4
            m["x"] = inp["x_sample"][b]
            m["cvec"] = np.ascontiguousarray(inp["c"][b].reshape(16, 128).T)
            m["cache_ckv"] = inp["cache_ckv"][b]
            m["cache_krope"] = inp["cache_krope"][b]
            h0 = inp["state_lru"][b]
        else:
            m["x"] = np.ascontiguousarray(inp["x_prompt"][core * 4:(core + 1) * 4].reshape(T, D))
            m["cvec"] = np.ascontiguousarray(inp["c_ctx"].reshape(16, 128).T)
            m["cache_ckv"] = np.zeros((L, 512, 256), np.float32)
            m["cache_krope"] = np.zeros((L, 512, 64), np.float32)
            h0 = np.zeros((L, 2, 512), np.float32)
        m["h0"] = np.ascontiguousarray(h0.reshape(L, 2, 4, 128).transpose(3, 0, 1, 2).reshape(128, L * 2 * 4))
        m["CC"], m["SS"] = _rope_tables(latent)
        m["maskb"] = _mask_bias(latent)
        m["flag"] = np.full((128, 1), 1.0 if latent else 0.0, np.float32)
        m["iota16"] = np.ascontiguousarray(np.broadcast_to(np.arange(16, dtype=np.float32), (128, 16)))
        for wn in W_NAMES:
            if wn in ("peer_u", "peer_v"):
                if with_uv:
                    m[wn] = inp[wn].reshape(L * 16384, D)
                continue
            m[wn] = inp[wn]
        in_maps.append(m)
    return in_maps


def kernel(**inputs):
    inp = {k: np.ascontiguousarray(np.asarray(v)) for k, v in inputs.items()}
    in_maps = make_in_maps(inp, list(range(8)))
    shapes = {k: (v.shape, F32) for k, v in in_maps[0].items()}
    nc, st = build_program(shapes)
    with st:
        res = run_bass_kernel_spmd(nc, in_maps, core_ids=list(range(8)))
    r = res.results
    y_prompt = np.concatenate([r[c]["o_y"].reshape(4, 256, D) for c in range(4)], 0)
    y_sample = np.stack([r[c]["o_y"] for c in range(4, 8)], 0)
    new_ckv = np.concatenate([r[c]["o_ckv"].reshape(L, 4, 256, 256).transpose(1, 0, 2, 3) for c in range(4)], 0)
    new_kr = np.concatenate([r[c]["o_kr"].reshape(L, 4, 256, 64).transpose(1, 0, 2, 3) for c in range(4)], 0)
    new_lru = np.concatenate([r[c]["o_lru"].transpose(1, 0, 2, 3) for c in range(4)], 0)
    return (np.ascontiguousarray(y_prompt, np.float32), np.ascontiguousarray(y_sample, np.float32),
            np.ascontiguousarray(new_ckv, np.float32), np.ascontiguousarray(new_kr, np.float32),
            np.ascontiguousarray(new_lru, np.float32))
```
